# Optimizing a Trainium2 kernel written in Bass

```python
import jax, jax.numpy as jnp
from jax import lax
import numpy as np

D_MODEL = 2048
BATCH = 16
SEQ = 2048
DEPTH = 1

MIX_WIDTH = D_MODEL
HEAD_DIM = 128
DELTA_WIDTH = MIX_WIDTH // 2
N_DELTA_HEADS = DELTA_WIDTH // HEAD_DIM
DILATED_WIDTH = MIX_WIDTH - DELTA_WIDTH
N_DILATED_HEADS = DILATED_WIDTH // HEAD_DIM
DILATION_PAIRS = ((128, 1), (512, 4), (2048, 16))
N_BRANCH = 3
ATTN_BLOCK = 128
DELTA_CHUNK = 64
SHORT_CONV_WIDTH = 5
FFN_CONV_WIDTH = 3
D_FF = 5632
ROPE_THETA = 500000.0
ROT_DIM = HEAD_DIM // 4
NORM_EPS = 1e-6
NEG_INF = -1e30
PROJ_WIDTH = 4 * DELTA_WIDTH + 4 * N_DELTA_HEADS + 3 * N_BRANCH * DILATED_WIDTH

kernel_name = "hybrid_deltanet_dilated_convffn_block"


def rms_norm(x, w):
    xf = x.astype(jnp.float32)
    y = xf * lax.rsqrt(jnp.mean(xf * xf, axis=-1, keepdims=True) + NORM_EPS)
    return (y * w.astype(jnp.float32)).astype(x.dtype)


def l2_normalize(x):
    return x * lax.rsqrt(jnp.sum(x * x, axis=-1, keepdims=True) + NORM_EPS)


def depthwise_conv_centered(x, w):
    width, ch = w.shape
    return lax.conv_general_dilated(
        x, w[:, None, :].astype(x.dtype), window_strides=(1,),
        padding=[(width // 2, width // 2)],
        dimension_numbers=("NWC", "WIO", "NWC"), feature_group_count=ch)


def partial_rope(x, positions):
    inv_freq = ROPE_THETA ** (-jnp.arange(0, ROT_DIM, 2, dtype=jnp.float32) / ROT_DIM)
    ang = positions.astype(jnp.float32)[..., None] * inv_freq
    cos = jnp.cos(ang)[:, :, None, :]
    sin = jnp.sin(ang)[:, :, None, :]
    xr = x[..., :ROT_DIM].astype(jnp.float32)
    x1, x2 = xr[..., :ROT_DIM // 2], xr[..., ROT_DIM // 2:]
    rot = jnp.concatenate([x1 * cos - x2 * sin, x2 * cos + x1 * sin], axis=-1)
    return jnp.concatenate([rot.astype(x.dtype), x[..., ROT_DIM:]], axis=-1)


def gated_delta_chunked(q, k, v, g, beta):
    bsz, nh, s, dk = q.shape
    dv = v.shape[-1]
    c = DELTA_CHUNK
    n = s // c
    q = q.reshape(bsz, nh, n, c, dk)
    k = k.reshape(bsz, nh, n, c, dk)
    v = v.reshape(bsz, nh, n, c, dv)
    g = g.reshape(bsz, nh, n, c)
    beta = beta.reshape(bsz, nh, n, c)
    cum_g = jnp.cumsum(g, axis=-1)
    lower = np.tril(np.ones((c, c), dtype=bool))
    strict = np.tril(np.ones((c, c), dtype=bool), -1)
    gamma = jnp.exp(jnp.where(lower, cum_g[..., :, None] - cum_g[..., None, :], NEG_INF))
    k_beta = k * beta[..., None]
    a_mat = jnp.where(strict, jnp.einsum("bhnik,bhnjk->bhnij", k_beta, k) * gamma, 0.0)
    t_mat = a_mat + jnp.eye(c, dtype=jnp.float32)
    u = lax.linalg.triangular_solve(t_mat, v * beta[..., None], left_side=True,
                                    lower=True, unit_diagonal=True)
    w = lax.linalg.triangular_solve(t_mat, k_beta * jnp.exp(cum_g)[..., None], left_side=True,
                                    lower=True, unit_diagonal=True)
    qk = jnp.einsum("bhnik,bhnjk->bhnij", q, k) * gamma
    q_dec = q * jnp.exp(cum_g)[..., None]
    k_tail = k * jnp.exp(cum_g[..., -1:] - cum_g)[..., None]
    chunk_decay = jnp.exp(cum_g[..., -1])

    def step(state, xs):
        u_n, w_n, qd_n, qk_n, kt_n, dec_n = xs
        v_new = u_n - jnp.einsum("bhck,bhkv->bhcv", w_n, state)
        o_n = (jnp.einsum("bhck,bhkv->bhcv", qd_n, state)
               + jnp.einsum("bhcj,bhjv->bhcv", qk_n, v_new))
        state = state * dec_n[..., None, None] + jnp.einsum("bhck,bhcv->bhkv", kt_n, v_new)
        return state, o_n

    xs = tuple(jnp.moveaxis(t, 2, 0) for t in (u, w, q_dec, qk, k_tail, chunk_decay))
    state0 = jnp.zeros((bsz, nh, dk, dv), jnp.float32)
    _, o = lax.scan(step, state0, xs)
    return jnp.moveaxis(o, 0, 2).reshape(bsz, nh, s, dv)


def delta_mixer(qkv_raw, z, a_f, a_b, b_f, b_b, conv_w, a_log_f, a_log_b, dt_b_f, dt_b_b, norm_w):
    bsz, s, _ = qkv_raw.shape
    qkv = jax.nn.silu(depthwise_conv_centered(qkv_raw, conv_w))
    q, k, v = jnp.split(qkv.astype(jnp.float32), 3, axis=-1)
    to_heads = lambda t: t.reshape(bsz, s, N_DELTA_HEADS, HEAD_DIM).transpose(0, 2, 1, 3)
    q = l2_normalize(to_heads(q)) * (HEAD_DIM ** -0.5)
    k = l2_normalize(to_heads(k))
    v = to_heads(v)

    def gates(a, b, a_log, dt_bias):
        a = a.astype(jnp.float32).transpose(0, 2, 1)
        b = b.astype(jnp.float32).transpose(0, 2, 1)
        g = -jnp.exp(a_log.astype(jnp.float32))[:, None] * jax.nn.softplus(
            a + dt_bias.astype(jnp.float32)[:, None])
        return g, jax.nn.sigmoid(b)

    g_f, beta_f = gates(a_f, b_f, a_log_f, dt_b_f)
    g_b, beta_b = gates(a_b, b_b, a_log_b, dt_b_b)
    flip = lambda t: jnp.flip(t, axis=2)
    o = (gated_delta_chunked(q, k, v, g_f, beta_f)
         + flip(gated_delta_chunked(flip(q), flip(k), flip(v), flip(g_b), flip(beta_b))))
    o = o.transpose(0, 2, 1, 3)
    o = o * lax.rsqrt(jnp.mean(o * o, axis=-1, keepdims=True) + NORM_EPS) * norm_w.astype(jnp.float32)
    o = o * jax.nn.silu(z.astype(jnp.float32).reshape(bsz, s, N_DELTA_HEADS, HEAD_DIM))
    return o.reshape(bsz, s, DELTA_WIDTH)


def dilated_window_attention(q, k, v, dilation, half_span):
    bsz, nh, s, dh = q.shape
    length = s // dilation
    nb = -(-length // ATTN_BLOCK)
    lp = nb * ATTN_BLOCK
    span = ATTN_BLOCK + 2 * half_span
    by_residue = lambda t: t.reshape(bsz, nh, length, dilation, dh).transpose(0, 1, 3, 2, 4)
    pad_q = ((0, 0), (0, 0), (0, 0), (0, lp - length), (0, 0))
    pad_kv = ((0, 0), (0, 0), (0, 0), (half_span, lp - length + half_span), (0, 0))
    qb = jnp.pad(by_residue(q), pad_q).reshape(bsz, nh, dilation, nb, ATTN_BLOCK, dh)
    idx = (np.arange(nb) * ATTN_BLOCK)[:, None] + np.arange(span)[None, :]
    kb = jnp.take(jnp.pad(by_residue(k), pad_kv), idx, axis=3)
    vb = jnp.take(jnp.pad(by_residue(v), pad_kv), idx, axis=3)
    scores = jnp.einsum("bhrnqe,bhrnke->bhrnqk", qb, kb,
                        preferred_element_type=jnp.float32) * (dh ** -0.5)
    qpos = (np.arange(nb) * ATTN_BLOCK)[:, None] + np.arange(ATTN_BLOCK)[None, :]
    kpos = (np.arange(nb) * ATTN_BLOCK - half_span)[:, None] + np.arange(span)[None, :]
    rel = kpos[:, None, :] - qpos[:, :, None]
    mask = (np.abs(rel) <= half_span) & (kpos[:, None, :] >= 0) & (kpos[:, None, :] < length)
    scores = jnp.where(mask, scores, NEG_INF)
    m = jnp.max(scores, axis=-1, keepdims=True)
    p = jnp.exp(scores - m)
    denom = jnp.sum(p, axis=-1, keepdims=True)
    o = jnp.einsum("bhrnqk,bhrnke->bhrnqe", p, vb.astype(jnp.float32)) / denom
    lse = (m + jnp.log(denom))[..., 0]
    o = o.reshape(bsz, nh, dilation, lp, dh)[:, :, :, :length]
    o = o.transpose(0, 1, 3, 2, 4).reshape(bsz, nh, s, dh)
    lse = lse.reshape(bsz, nh, dilation, lp)[:, :, :, :length].transpose(0, 1, 3, 2).reshape(bsz, nh, s)
    return o, lse


def dilated_mixer(bq, bk, bv, positions):
    bsz, s, _ = bq.shape
    outs, lses = [], []
    for gi, (window, dilation) in enumerate(DILATION_PAIRS):
        sl = slice(gi * DILATED_WIDTH, (gi + 1) * DILATED_WIDTH)
        heads = lambda t: t[..., sl].reshape(bsz, s, N_DILATED_HEADS, HEAD_DIM)
        q = partial_rope(heads(bq), positions).transpose(0, 2, 1, 3)
        k = partial_rope(heads(bk), positions).transpose(0, 2, 1, 3)
        v = heads(bv).transpose(0, 2, 1, 3)
        o, lse = dilated_window_attention(q, k, v, dilation, window // (2 * dilation))
        outs.append(o)
        lses.append(lse)
    weights = jax.nn.softmax(jnp.stack(lses, axis=0), axis=0)
    o = jnp.sum(weights[..., None] * jnp.stack(outs, axis=0), axis=0)
    return o.transpose(0, 2, 1, 3).reshape(bsz, s, DILATED_WIDTH)


def token_mixer(n, positions, w_in, conv_qkv_w, a_log_f, a_log_b, dt_b_f, dt_b_b, delta_norm_w, w_out):
    proj = jnp.einsum("bsd,dp->bsp", n, w_in)
    sizes = [3 * DELTA_WIDTH, DELTA_WIDTH] + [N_DELTA_HEADS] * 4 + [N_BRANCH * DILATED_WIDTH] * 3
    splits = [int(i) for i in np.cumsum(sizes)[:-1]]
    qkv_a, z, a_f, a_b, b_f, b_b, bq, bk, bv = jnp.split(proj, splits, axis=-1)
    out_a = delta_mixer(qkv_a, z, a_f, a_b, b_f, b_b, conv_qkv_w,
                        a_log_f, a_log_b, dt_b_f, dt_b_b, delta_norm_w)
    out_b = dilated_mixer(bq, bk, bv, positions)
    mixed = jnp.concatenate([out_a, out_b], axis=-1).astype(n.dtype)
    return jnp.einsum("bsm,md->bsd", mixed, w_out)


def conv_gated_mlp(n, w_ffn_in, conv_ffn_w, w_ffn_out):
    up = depthwise_conv_centered(jnp.einsum("bsd,df->bsf", n, w_ffn_in), conv_ffn_w)
    gate, val = jnp.split(up, 2, axis=-1)
    return jnp.einsum("bsf,fd->bsd", jax.nn.silu(gate) * val, w_ffn_out)


def setup_inputs(seed: int = 0) -> dict:
    key = jax.random.key(seed)
    ks = jax.random.split(key, 16)
    f32 = jnp.float32
    nrm = lambda k, shape, scale: jax.random.normal(k, shape, f32) * scale
    gain = lambda k, shape: 1.0 + 0.02 * jax.random.normal(k, shape, f32)

    def dt_bias(k):
        dt = jnp.exp(jax.random.uniform(k, (DEPTH, N_DELTA_HEADS), f32)
                     * (np.log(0.1) - np.log(0.001)) + np.log(0.001))
        return dt + jnp.log(-jnp.expm1(-dt))

    return {
        "x": jax.random.normal(ks[0], (BATCH, SEQ, D_MODEL), f32),
        "positions": jnp.broadcast_to(jnp.arange(SEQ, dtype=jnp.int32), (BATCH, SEQ)),
        "norm_mix_w": gain(ks[1], (DEPTH, D_MODEL)),
        "w_in": nrm(ks[2], (DEPTH, D_MODEL, PROJ_WIDTH), D_MODEL ** -0.5),
        "conv_qkv_w": nrm(ks[3], (DEPTH, SHORT_CONV_WIDTH, 3 * DELTA_WIDTH), SHORT_CONV_WIDTH ** -0.5),
        "a_log_fwd": jnp.log(jax.random.uniform(ks[4], (DEPTH, N_DELTA_HEADS), f32, 1.0, 16.0)),
        "a_log_bwd": jnp.log(jax.random.uniform(ks[5], (DEPTH, N_DELTA_HEADS), f32, 1.0, 16.0)),
        "dt_bias_fwd": dt_bias(ks[6]),
        "dt_bias_bwd": dt_bias(ks[7]),
        "delta_norm_w": gain(ks[8], (DEPTH, HEAD_DIM)),
        "w_out": nrm(ks[9], (DEPTH, MIX_WIDTH, D_MODEL), MIX_WIDTH ** -0.5),
        "norm_ffn_w": gain(ks[10], (DEPTH, D_MODEL)),
        "w_ffn_in": nrm(ks[11], (DEPTH, D_MODEL, 2 * D_FF), D_MODEL ** -0.5),
        "conv_ffn_w": nrm(ks[12], (DEPTH, FFN_CONV_WIDTH, 2 * D_FF), FFN_CONV_WIDTH ** -0.5),
        "w_ffn_out": nrm(ks[13], (DEPTH, D_FF, D_MODEL), D_FF ** -0.5),
        "norm_final_w": gain(ks[14], (D_MODEL,)),
    }


def reference(x, positions, norm_mix_w, w_in, conv_qkv_w, a_log_fwd, a_log_bwd, dt_bias_fwd,
              dt_bias_bwd, delta_norm_w, w_out, norm_ffn_w, w_ffn_in, conv_ffn_w, w_ffn_out,
              norm_final_w):
    h = x
    for layer in range(DEPTH):
        n = rms_norm(h, norm_mix_w[layer])
        h = h + token_mixer(n, positions, w_in[layer], conv_qkv_w[layer], a_log_fwd[layer],
                            a_log_bwd[layer], dt_bias_fwd[layer], dt_bias_bwd[layer],
                            delta_norm_w[layer], w_out[layer]).astype(h.dtype)
        n = rms_norm(h, norm_ffn_w[layer])
        h = h + conv_gated_mlp(n, w_ffn_in[layer], conv_ffn_w[layer], w_ffn_out[layer]).astype(h.dtype)
    return rms_norm(h, norm_final_w)
```

```python
import contextlib
import numpy as np
import concourse.bass as bass
import concourse.mybir as mybir
from concourse.bass_utils import run_bass_kernel_spmd

F32 = mybir.dt.float32
BF16 = mybir.dt.bfloat16
I32 = mybir.dt.int32
AF = mybir.ActivationFunctionType
ALU = mybir.AluOpType

PE, ACT, DVE, POOL, SP = "tensor", "scalar", "vector", "gpsimd", "sync"
ENGS = (PE, ACT, DVE, POOL, SP)

S = 2048
D = 2048
NS = 2
NT = NS * S
PW = 13344
DFF = 5632
NFC = DFF // 128
EPS = 1e-6
COL_Z = 3072
COL_G = 4096
COL_BQ = 4128
COL_BK = 7200
COL_BV = 10272
NFEAT = 80
NCST = 1216


def make_cst():
    c = np.zeros((128, NCST), np.float32)
    c[:, 0:128] = np.eye(128)
    for p in range(16):
        c[p + 16, 128 + p] = -1.0
        c[p, 128 + 16 + p] = 1.0
    inv = 500000.0 ** (-np.arange(0, 32, 2, dtype=np.float32) / 32.0)
    c[0:16, 160] = inv
    c[16:32, 160] = inv
    c[:, 161] = np.pi / 2
    c[:, 162] = EPS
    k = np.arange(128)[:, None]
    q = np.arange(128)[None, :]
    c[:, 192:320] = (k >= q)
    c[:, 320:448] = (k <= q)
    c[0:64, 448:576] = (k[0:64] + 64 >= q)
    BIG = 30000.0
    c[:, 576:704] = np.where(q < k, 0.0, BIG)
    c[:, 704:832] = np.where(q >= k, 0.0, -BIG)
    c[:, 832:960] = np.where(q > k, 0.0, BIG)
    c[:, 960:1088] = np.where(q <= k, 0.0, -BIG)
    c[:, 1088:1216] = (k <= q)
    return c


class Buf:
    def __init__(self, ap, name, dram=False):
        self.ap = ap
        self.name = name
        self.dram = dram
        self.w = {}
        self.r = {}
        self.dsem = None
        self.dcnt = 0


class FW:
    def __init__(self, nc):
        self.nc = nc
        self.es = contextlib.ExitStack()
        self.sem = {}
        self.cnt = {e: 0 for e in ENGS}
        self.ops = {e: [] for e in ENGS}
        self.waited = {e: {} for e in ENGS}
        self.allsems = {}
        for e in (PE, ACT, DVE, POOL):
            self.sem[e] = self.new_sem("e_" + e)
            self.allsems[e] = [self.sem[e], 0]
        self.nbuf = 0
        self.dfree = []
        self.dbufs = []
        self.nds = 0

    def get_dsem(self):
        if self.dfree:
            return self.dfree.pop()
        self.nds += 1
        key = "ds%d" % self.nds
        ent = [self.new_sem(key), 0, key]
        self.allsems[key] = ent
        return ent

    def new_sem(self, name):
        return self.es.enter_context(self.nc.semaphore(name))

    def sbuf(self, name, shape, dtype):
        return self.es.enter_context(self.nc.sbuf_tensor(name, list(shape), dtype))

    def psum(self, name, shape, dtype):
        return self.es.enter_context(self.nc.psum_tensor(name, list(shape), dtype))

    def buf(self, ap, name=None, dram=False):
        self.nbuf += 1
        return Buf(ap, (name or "b") + "_%d" % self.nbuf, dram=dram)

    def _deps(self, eng, reads, writes, skip_key=None):
        need = {}

        def add(key, sem, val):
            if key == skip_key:
                return
            if key not in need or need[key][1] < val:
                need[key] = (sem, val)

        for b in reads:
            for key, (sem, val) in b.w.items():
                add(key, sem, val)
        for b in writes:
            for key, (sem, val) in b.w.items():
                if eng == PE and key == PE:
                    continue
                add(key, sem, val)
            for key, (sem, val) in b.r.items():
                add(key, sem, val)
        out = []
        wd = self.waited[eng]
        for key, (sem, val) in need.items():
            if wd.get(key, 0) >= val:
                continue
            wd[key] = val
            out.append((sem, val))
        return out

    def op(self, eng, fn, reads=(), writes=()):
        waits = self._deps(eng, reads, writes)
        self.cnt[eng] += 1
        c = self.cnt[eng]
        sem = self.sem[eng]
        self.allsems[eng][1] = c
        self.ops[eng].append((waits, fn, sem, 1))
        for b in reads:
            b.r[eng] = (sem, c)
        for b in writes:
            b.w = {eng: (sem, c)}
            b.r = {}

    def dma(self, q, out_ap, in_ap, reads=(), writes=(), **kw):
        owner = None
        for b in list(writes) + list(reads):
            if not b.dram:
                owner = b
                break
        if owner is None:
            owner = writes[0]
        if owner.dsem is None:
            owner.dsem = self.get_dsem()
            self.dbufs.append(owner)
        ent = owner.dsem
        key = ent[2]
        waits = self._deps(q, reads, writes, skip_key=key)
        ent[1] += 16
        sem, c = ent[0], ent[1]

        def fn(e, out_ap=out_ap, in_ap=in_ap, kw=kw):
            return e.dma_start(out=out_ap, in_=in_ap, **kw)

        self.ops[q].append((waits, fn, sem, 16))
        for b in reads:
            b.r[key] = (sem, c)
        for b in writes:
            if b.dram:
                b.w[key] = (sem, c)
            else:
                b.w = {key: (sem, c)}
                b.r = {}

    def barrier(self, engs=ENGS):
        for e in engs:
            waits = []
            wd = self.waited[e]
            for key, ent in self.allsems.items():
                sem, val = ent[0], ent[1]
                if val > 0 and wd.get(key, 0) < val:
                    wd[key] = val
                    waits.append((sem, val))
            if waits:
                self.ops[e].append((waits, None, None, 0))
        if tuple(engs) == tuple(ENGS):
            for b in self.dbufs:
                self.dfree.append(b.dsem)
                b.dsem = None
            self.dbufs = []

    def emit(self):
        print("ops per engine:", {e: len(v) for e, v in self.ops.items()}, "dsems", self.nds, flush=True)
        with self.nc.Block() as block:
            def mk(eng):
                def body(e):
                    for waits, fn, sem, inc in self.ops[eng]:
                        for (s, v) in waits:
                            e.wait_ge(s, v)
                        if fn is not None:
                            fn(e).then_inc(sem, inc)
                return body
            block.tensor(mk(PE))
            block.scalar(mk(ACT))
            block.vector(mk(DVE))
            block.gpsimd(mk(POOL))
            block.sync(mk(SP))

    def close(self):
        self.es.close()


class Arena:
    def __init__(self, f, nbytes):
        self.f = f
        self.n = nbytes // 2
        self.t = f.sbuf("arena", [128, self.n], BF16)
        self.off = 0

    def mark(self):
        return self.off

    def reset(self, m):
        self.off = m

    def alloc(self, name, free_shape, dtype, buf=True):
        nel = int(np.prod(free_shape))
        units = nel * (2 if dtype in (F32, I32) else 1)
        self.off = (self.off + 7) // 8 * 8
        assert self.off + units <= self.n, ("arena overflow", name, self.off, units, self.n)
        ap = self.t[:, self.off:self.off + units]
        self.off += units
        if dtype != BF16:
            ap = ap.bitcast(dtype)
        if len(free_shape) == 2:
            ap = ap.rearrange("p (a b) -> p a b", b=free_shape[1])
        elif len(free_shape) == 3:
            ap = ap.rearrange("p (a b c) -> p a b c", b=free_shape[1], c=free_shape[2])
        if buf:
            return ap, self.f.buf(ap, name)
        return ap


def build(debug=False, stub_mixers=False, upto=99, skip_delta=False):
    nc = bass.Bass("TRN2", target_bir_lowering=False)
    f = FW(nc)

    def din(name, shape, dt=F32):
        return nc.dram_tensor(name, list(shape), dt, kind="ExternalInput")

    def dscr(name, shape, dt):
        return nc.dram_tensor(name, list(shape), dt, kind=("ExternalOutput" if debug else "Internal"))

    x_d = din("x", [NT, D])
    pos_d = din("pos", [NS, S], I32)
    nmw_d = din("norm_mix_w", [16, 128])
    win_d = din("w_in", [D, PW])
    cqkv_d = din("conv_qkv_w", [5, 24, 128])
    alf_d = din("a_log_fwd", [1, 8]); alb_d = din("a_log_bwd", [1, 8])
    dtf_d = din("dt_bias_fwd", [1, 8]); dtb_d = din("dt_bias_bwd", [1, 8])
    dnw_d = din("delta_norm_w", [1, 128])
    wout_d = din("w_out", [D, D])
    nfw_d = din("norm_ffn_w", [16, 128])
    wfi_d = din("w_ffn_in", [D, 2 * DFF])
    cffn_d = din("conv_ffn_w", [3, 88, 128])
    wfo_d = din("w_ffn_out", [DFF, D])
    nfin_d = din("norm_final_w", [1, D])
    cst_d = din("cst", [128, NCST])
    out_d = nc.dram_tensor("out", [NT, D], F32, kind="ExternalOutput")

    winb_d = nc.dram_tensor("w_in_bf", [D, PW], BF16, kind="Internal")
    woutb_d = nc.dram_tensor("w_out_bf", [D, D], BF16, kind="Internal")
    wfib_d = nc.dram_tensor("w_ffn_in_bf", [D, 2 * DFF], BF16, kind="Internal")
    wfob_d = nc.dram_tensor("w_ffn_out_bf", [DFF, D], BF16, kind="Internal")
    projT_d = dscr("projT", [NFEAT * 128, NT], BF16)
    vtok_d = dscr("vtok", [NT, 3072], BF16)
    gates_d = dscr("gates", [128, NT // 128, 32], F32)
    mixT_d = dscr("mixT", [D, NT], BF16)
    h1_d = dscr("h1", [NT, D], F32)

    B = lambda t, n: f.buf(t.ap(), n, dram=True)
    Xd, WIN, WOUT, WFI, WFO = B(x_d, "x"), B(win_d, "win"), B(wout_d, "wout"), B(wfi_d, "wfi"), B(wfo_d, "wfo")
    WINB, WOUTB, WFIB, WFOB = B(winb_d, "winb"), B(woutb_d, "woutb"), B(wfib_d, "wfib"), B(wfob_d, "wfob")
    PROJT, VTOK, GATES, MIXT, H1, OUT = B(projT_d, "projT"), B(vtok_d, "vtok"), B(gates_d, "gates"), B(mixT_d, "mixT"), B(h1_d, "h1"), B(out_d, "out")
    SMALL = f.buf(None, "small", dram=True)

    dbg_list = []

    def dbg(name, ap, BUF, shape, dt):
        if not debug:
            return
        t = nc.dram_tensor("dbg_" + name, [128] + list(shape), dt, kind="ExternalOutput")
        DB = f.buf(t.ap(), "dbg_" + name, dram=True)
        f.dma(SP, t.ap(), ap, reads=[BUF], writes=[DB])

    ar = Arena(f, 190 * 1024)
    banks = []
    for i in range(8):
        t = f.psum("bank%d" % i, [128, 512], F32)
        banks.append((t[:], f.buf(t[:], "bank%d" % i)))
    bank_rr = [0]

    def next_bank(lo=0, hi=8):
        i = lo + bank_rr[0] % (hi - lo)
        bank_rr[0] += 1
        return banks[i]

    cst, CST = ar.alloc("cst", [NCST], F32)
    f.dma(SP, cst, cst_d.ap(), reads=[SMALL], writes=[CST])
    ident, IDENT = cst[:, 0:128], CST
    identb, IDENTB = ar.alloc("identb", [128], BF16)
    f.op(DVE, lambda e: e.tensor_copy(out=identb, in_=ident), reads=[IDENT], writes=[IDENTB])
    gates_sb, GSB = ar.alloc("gates_sb", [NT // 128, 32], F32)

    def load_featvec(dram_t, nrow, name):
        rows, ROWS = ar.alloc(name + "_r", [128], F32)
        f.dma(SP, rows[0:nrow, :], dram_t.ap(), reads=[SMALL], writes=[ROWS])
        res, RES = ar.alloc(name, [nrow], F32)
        bk, BK = next_bank()
        f.op(PE, lambda e: e.transpose(bk[:, 0:nrow], rows[0:nrow, :], ident[0:nrow, 0:nrow]), reads=[ROWS, IDENT], writes=[BK])
        f.op(DVE, lambda e: e.tensor_copy(out=res, in_=bk[:, 0:nrow]), reads=[BK], writes=[RES])
        return res, RES

    nmw, NMW = load_featvec(nmw_d, 16, "nmw")
    nfw, NFW = load_featvec(nfw_d, 16, "nfw")
    cffn = []
    for j in range(3):
        t_ = nc.dram_tensor("cffn_view%d" % j, [1], F32, kind="Internal") if False else None
        rows, ROWS = ar.alloc("cffn_r%d" % j, [128], F32)
        f.dma(SP, rows[0:88, :], cffn_d.ap()[j], reads=[SMALL], writes=[ROWS])
        res, RES = ar.alloc("cffn%d" % j, [88], F32)
        bk, BK = next_bank()
        f.op(PE, lambda e, bk=bk, rows=rows: e.transpose(bk[:, 0:88], rows[0:88, :], ident[0:88, 0:88]), reads=[ROWS, IDENT], writes=[BK])
        f.op(DVE, lambda e, bk=bk, res=res: e.tensor_copy(out=res, in_=bk[:, 0:88]), reads=[BK], writes=[RES])
        cffn.append((res, RES))
    nfin, NFIN = ar.alloc("nfin", [D], F32)
    f.dma(SP, nfin, nfin_d.ap().to_broadcast([128, D]), reads=[SMALL], writes=[NFIN])

    base_mark = ar.mark()

    def cast_w(src, dst, SRC, DST, rows, cols, b):
        for r0 in range(0, rows, 128):
            f.dma(POOL, dst.ap()[r0:r0 + 128, :].rearrange("r (a b) -> r a b", b=b),
                  src.ap()[r0:r0 + 128, :].rearrange("r (a b) -> r a b", b=b), reads=[SRC], writes=[DST])
    cast_w(win_d, winb_d, WIN, WINB, D, PW, 834)
    cast_w(wout_d, woutb_d, WOUT, WOUTB, D, D, 1024)
    cast_w(wfi_d, wfib_d, WFI, WFIB, D, 2 * DFF, 1024)
    cast_w(wfo_d, wfob_d, WFO, WFOB, DFF, D, 1024)

    if upto <= 0:
        f.barrier(); f.emit(); f.close()
        return nc
    def make_norm(tag, nslot=3):
        xs = [ar.alloc("%s_x%d" % (tag, i), [D], F32) for i in range(nslot)]
        junk, JUNK = ar.alloc(tag + "_junk", [D], BF16)
        st = [ar.alloc("%s_st%d" % (tag, i), [4], F32) for i in range(2)]
        dg = [ar.alloc("%s_dg%d" % (tag, i), [128], F32) for i in range(2)]
        ctr = [0]

        def run(blocks, wn, WN, nT, NTB):
            for blk in blocks:
                one(blk, wn, WN, nT, NTB)

        def one(blk, wn, WN, nT, NTB):
            rows, npart, col, step, SRCB = blk
            bi = ctr[0]
            ctr[0] += 1
            if True:
                xt, XT = xs[bi % nslot]
                s_, ST = st[bi % 2]
                d_, DG = dg[bi % 2]
                if npart == 128:
                    f.dma(SP, xt, rows, reads=[SRCB], writes=[XT])
                else:
                    f.op(POOL, lambda e, xt=xt: e.memset(xt[0:2, :], 0.0), writes=[XT])
                    for ri, r in enumerate(rows):
                        if r is not None:
                            f.dma(SP, xt[ri:ri + 1, :], r, reads=[SRCB], writes=[XT])
                p = npart
                f.op(ACT, lambda e, xt=xt, s_=s_, p=p: e.activation(out=junk[0:p, :], in_=xt[0:p, :], func=AF.Square, accum_out=s_[0:p, 0:1]),
                     reads=[XT], writes=[JUNK, ST])
                f.op(DVE, lambda e, s_=s_, p=p: e.tensor_scalar(out=s_[0:p, 1:2], in0=s_[0:p, 0:1], scalar1=1.0 / D, scalar2=EPS, op0=ALU.mult, op1=ALU.add),
                     reads=[ST], writes=[ST])
                f.op(ACT, lambda e, s_=s_, p=p: e.activation(out=s_[0:p, 2:3], in_=s_[0:p, 1:2], func=AF.Sqrt), reads=[ST], writes=[ST])
                f.op(DVE, lambda e, s_=s_, p=p: e.reciprocal(out=s_[0:p, 3:4], in_=s_[0:p, 2:3]), reads=[ST], writes=[ST])
                f.op(DVE, lambda e, s_=s_, d_=d_, p=p: e.tensor_scalar(out=d_[0:p, 0:p], in0=ident[0:p, 0:p], scalar1=s_[0:p, 3:4], scalar2=None, op0=ALU.mult),
                     reads=[ST, IDENT], writes=[DG])
                for g4 in range(4):
                    bk, BK = next_bank(0, 2)
                    for j in range(4):
                        kc = g4 * 4 + j
                        f.op(PE, lambda e, bk=bk, xt=xt, d_=d_, kc=kc, j=j, p=p: e.matmul(
                            bk[:, j * 128:j * 128 + p], lhsT=xt[0:p, kc * 128:(kc + 1) * 128], rhs=d_[0:p, 0:p], start=True, stop=True),
                            reads=[XT, DG], writes=[BK])
                    for j in range(4):
                        kc = g4 * 4 + j
                        if p == 128:
                            o_ap = nT[:, kc, col:col + 128]
                        else:
                            o_ap = nT[:, kc, col:col + step + 1:step]
                        i_ap = bk[:, j * 128:j * 128 + p]
                        if g4 % 2 == 0:
                            f.op(ACT, lambda e, o_ap=o_ap, i_ap=i_ap, kc=kc: e.activation(out=o_ap, in_=i_ap, func=AF.Copy, scale=wn[:, kc:kc + 1]),
                                 reads=[BK, WN], writes=[NTB[kc]])
                        else:
                            f.op(DVE, lambda e, o_ap=o_ap, i_ap=i_ap, kc=kc: e.tensor_scalar(out=o_ap, in0=i_ap, scalar1=wn[:, kc:kc + 1], scalar2=None, op0=ALU.mult),
                                 reads=[BK, WN], writes=[NTB[kc]])
        return run

    TT1 = 1024
    NH1 = TT1 // 512
    m1 = ar.mark()
    norm1 = make_norm("n1")
    nT1 = ar.alloc("nT1", [16, TT1], BF16, buf=False)
    NT1B = [f.buf(nT1[:, kc, :], "nT1_%d" % kc) for kc in range(16)]
    wsl = [ar.alloc("wsl%d" % i, [16, 512], BF16) for i in range(3)]
    stg = [ar.alloc("stg%d" % i, [4, TT1], BF16) for i in range(2)]
    vst = [ar.alloc("vst%d" % i, [512], BF16) for i in range(2)]
    wg, WG = ar.alloc("wg", [16, 32], BF16)
    f.dma(SP, wg, winb_d.ap()[:, COL_G:COL_G + 32].rearrange("(k p) n -> p k n", p=128), reads=[WINB], writes=[WG])
    feat_slabs = [c for c in range(0, 4096, 512)] + [COL_BQ + c for c in range(0, 3072, 512)] + [COL_BK + c for c in range(0, 3072, 512)]
    evac_rr = [0]

    def evac(out_ap, in_ap, reads, writes):
        evac_rr[0] += 1
        if evac_rr[0] % 2 == 0:
            f.op(ACT, lambda e: e.activation(out=out_ap, in_=in_ap, func=AF.Copy), reads=reads, writes=writes)
        else:
            f.op(DVE, lambda e: e.tensor_copy(out=out_ap, in_=in_ap), reads=reads, writes=writes)

    si = 0
    for tile in range(NT // TT1):
        t0 = tile * TT1
        blocks = [(x_d.ap()[t0 + b * 128:t0 + (b + 1) * 128, :], 128, b * 128, 0, Xd) for b in range(TT1 // 128)]
        norm1(blocks, nmw, NMW, nT1, NT1B)
        for sidx, c0 in enumerate(feat_slabs):
            w_, W_ = wsl[si % 3]
            sg_, SG_ = stg[si % 2]
            si += 1
            f.dma(SP, w_, winb_d.ap()[:, c0:c0 + 512].rearrange("(k p) n -> p k n", p=128), reads=[WINB], writes=[W_])
            for j in range(4):
                for h in range(NH1):
                    bk, BK = next_bank(2, 8)
                    for kc in range(16):
                        f.op(PE, lambda e, bk=bk, w_=w_, kc=kc, j=j, h=h: e.matmul(
                            bk, lhsT=w_[:, kc, j * 128:(j + 1) * 128], rhs=nT1[:, kc, h * 512:(h + 1) * 512], start=(kc == 0), stop=(kc == 15)),
                            reads=[W_, NT1B[kc]], writes=[BK])
                    evac(sg_[:, j, h * 512:(h + 1) * 512], bk, [BK], [SG_])
            f.dma(POOL, projT_d.ap()[sidx * 512:(sidx + 1) * 512, t0:t0 + TT1].rearrange("(j p) t -> p j t", p=128), sg_,
                  reads=[SG_], writes=[PROJT])
        for sb in range(6):
            c0 = COL_BV + sb * 512
            w_, W_ = wsl[si % 3]
            si += 1
            f.dma(SP, w_, winb_d.ap()[:, c0:c0 + 512].rearrange("(k p) n -> p k n", p=128), reads=[WINB], writes=[W_])
            for b in range(TT1 // 128):
                bk, BK = next_bank(2, 8)
                for kc in range(16):
                    f.op(PE, lambda e, bk=bk, w_=w_, kc=kc, b=b: e.matmul(
                        bk, lhsT=nT1[:, kc, b * 128:(b + 1) * 128], rhs=w_[:, kc, :], start=(kc == 0), stop=(kc == 15)),
                        reads=[W_, NT1B[kc]], writes=[BK])
                v_, V_ = vst[(sb * 8 + b) % 2]
                evac(v_, bk, [BK], [V_])
                f.dma(POOL, vtok_d.ap()[t0 + b * 128:t0 + (b + 1) * 128, sb * 512:(sb + 1) * 512], v_, reads=[V_], writes=[VTOK])
        for b in range(TT1 // 128):
            bk, BK = next_bank(2, 8)
            for kc in range(16):
                f.op(PE, lambda e, bk=bk, kc=kc, b=b: e.matmul(
                    bk[:, 0:32], lhsT=nT1[:, kc, b * 128:(b + 1) * 128], rhs=wg[:, kc, :], start=(kc == 0), stop=(kc == 15)),
                    reads=[WG, NT1B[kc]], writes=[BK])
            gb = t0 // 128 + b
            f.op(DVE, lambda e, bk=bk, gb=gb: e.tensor_copy(out=gates_sb[:, gb, :], in_=bk[:, 0:32]), reads=[BK], writes=[GSB])

    if upto <= 1:
        f.barrier(); f.emit(); f.close()
        return nc
    if debug:
        f.dma(SP, gates_d.ap(), gates_sb, reads=[GSB], writes=[GATES])
    f.barrier()
    ar.reset(m1)
    if stub_mixers:
        z_, Z_ = ar.alloc("zeros", [NT], BF16)
        f.op(DVE, lambda e: e.memset(z_, 0.0), writes=[Z_])
        for c in range(16):
            f.dma(SP, mixT_d.ap()[c * 128:(c + 1) * 128, :], z_, reads=[Z_], writes=[MIXT])
    else:
        if not skip_delta:

            maskN = [cst[:, 576:704], cst[:, 832:960]]
            maskQ = [cst[:, 704:832], cst[:, 960:1088]]
            Lmat = [cst[:, 1088:1216], cst[:, 192:320]]
            cq = []
            for j in range(5):
                rows, ROWS = ar.alloc("cq_r%d" % j, [128], F32)
                f.dma(SP, rows[0:24, :], cqkv_d.ap()[j], reads=[SMALL], writes=[ROWS])
                res, RES = ar.alloc("cq%d" % j, [24], F32)
                bk, BK = next_bank()
                f.op(PE, lambda e, bk=bk, rows=rows: e.transpose(bk[:, 0:24], rows[0:24, :], ident[0:24, 0:24]), reads=[ROWS, IDENT], writes=[BK])
                f.op(DVE, lambda e, bk=bk, res=res: e.tensor_copy(out=res, in_=bk[:, 0:24]), reads=[BK], writes=[RES])
                cq.append((res, RES))
            prm, PRM = ar.alloc("prm", [2, 2, 8], F32)
            f.dma(SP, prm[:, 0, 0, :], dtf_d.ap().to_broadcast([128, 8]), reads=[SMALL], writes=[PRM])
            f.dma(SP, prm[:, 0, 1, :], dtb_d.ap().to_broadcast([128, 8]), reads=[SMALL], writes=[PRM])
            f.dma(SP, prm[:, 1, 0, :], alf_d.ap().to_broadcast([128, 8]), reads=[SMALL], writes=[PRM])
            f.dma(SP, prm[:, 1, 1, :], alb_d.ap().to_broadcast([128, 8]), reads=[SMALL], writes=[PRM])
            f.op(ACT, lambda e: e.activation(out=prm[:, 1, :, :], in_=prm[:, 1, :, :], func=AF.Exp), reads=[PRM], writes=[PRM])
            f.op(DVE, lambda e: e.tensor_scalar(out=prm[:, 1, :, :], in0=prm[:, 1, :, :], scalar1=-1.0, scalar2=None, op0=ALU.mult), reads=[PRM], writes=[PRM])
            DTB, DTBB = ar.alloc("DTB", [2, 16, 8], F32)
            NEGAf, NEGAB = ar.alloc("NEGA", [256], F32); NEGA = NEGAf.rearrange("p (d c h) -> p d c h", d=2, c=16)
            for d_ in range(2):
                for c in range(16):
                    f.op(POOL, lambda e, d_=d_, c=c: e.tensor_copy(out=DTB[:, d_, c, :], in_=prm[:, 0, d_, :]), reads=[PRM], writes=[DTBB])
                    f.op(POOL, lambda e, d_=d_, c=c: e.tensor_copy(out=NEGA[:, d_, c, :], in_=prm[:, 1, d_, :]), reads=[PRM], writes=[NEGAB])
            dnw, DNW = ar.alloc("dnw", [128], F32)
            f.dma(SP, dnw, dnw_d.ap().to_broadcast([128, 128]), reads=[SMALL], writes=[DNW])
            onesf, ONESF = ar.alloc("onesf", [128], F32)
            f.op(DVE, lambda e: e.memset(onesf, 1.0), writes=[ONESF])
            onesb, ONESB = ar.alloc("onesb2", [128], BF16)
            f.op(DVE, lambda e: e.memset(onesb, 1.0), writes=[ONESB])
            V4 = lambda t: t.rearrange("p (d c h) -> p d c h", d=2, c=16)
            T1f, T1B = ar.alloc("T1", [256], F32); T1 = V4(T1f)
            GT_f, GTB = ar.alloc("GT_", [256], F32); GT_ = V4(GT_f)
            NEGBf, NEGBB = ar.alloc("NEGB", [256], F32); NEGB = V4(NEGBf)
            BETAf, BETAB = ar.alloc("BETA", [256], F32); BETA = V4(BETAf)
            CGf, CGB = ar.alloc("CG", [256], F32); CG = V4(CGf)
            CGc, CGcB = ar.alloc("CGc", [16, 16], F32)
            TOTf, TOTB = ar.alloc("TOT", [256], F32); TOT = V4(TOTf)
            GEXf, GEXB = ar.alloc("GEX", [256], F32); GEX = V4(GEXf)
            NEGGf, NEGGB = ar.alloc("NEGG", [256], F32); NEGG = V4(NEGGf)
            W2f, W2B = ar.alloc("W2", [256], F32); W2 = V4(W2f)
            DECf, DECB = ar.alloc("DEC", [256], F32); DEC = V4(DECf)
            xp = [ar.alloc("xp%d" % i, [S + 4], BF16) for i in range(3)]
            zt, ZT = ar.alloc("zt", [S], BF16)
            for i in range(3):
                f.op(DVE, lambda e, i=i: e.memset(xp[i][0], 0.0), writes=[xp[i][1]])
            acc = [ar.alloc("cacc%d" % i, [S], F32) for i in range(2)]
            sq, SQ = ar.alloc("sq", [S], BF16)
            rstd, RSTD = ar.alloc("rstd", [S], F32)
            QT, QTB = ar.alloc("QT", [S], BF16)
            KT, KTB = ar.alloc("KT", [S], BF16)
            KT32, KT32B = ar.alloc("KT32", [S], F32)
            VT, VTB = ar.alloc("VT", [S], BF16)
            zs, ZS = ar.alloc("zs", [S], BF16)
            Ktok, KTOK = ar.alloc("Ktok", [16, 128], BF16)
            Vtok, VTOK_ = ar.alloc("Vtok", [16, 128], BF16)
            Oacc, OACC = ar.alloc("Oacc", [16, 128], F32)
            onb, ONB = ar.alloc("onb", [16, 128], BF16)
            mo2, MO2 = ar.alloc("mo2", [S], BF16)
            osq, OSQ = ar.alloc("osq", [16, 4], F32)
            TiTb = ar.alloc("TiTb", [2, 16, 128], BF16, buf=False)
            QKm = ar.alloc("QKm", [2, 16, 128], BF16, buf=False)
            TIB = [[f.buf(TiTb[:, d_, c4 * 4:(c4 + 1) * 4, :], "tib") for c4 in range(4)] for d_ in range(2)]
            QKB2 = [[f.buf(QKm[:, d_, c4 * 4:(c4 + 1) * 4, :], "qkmb") for c4 in range(4)] for d_ in range(2)]
            dgm, DGM = ar.alloc("dgm", [4, 128], F32)
            t1, T1b = ar.alloc("t1", [4, 128], F32)
            t3, T3b = ar.alloc("t3", [4, 128], F32)
            gam1, GAM1 = ar.alloc("gam1", [4, 128], F32)
            Nm, NM = ar.alloc("Nm", [4, 128], F32)
            Pa = [ar.alloc("Pa%d" % i, [4, 128], F32) for i in range(2)]
            PaT = [ar.alloc("PaT%d" % i, [4, 128], F32) for i in range(2)]
            Xa = [ar.alloc("Xa%d" % i, [4, 128], F32) for i in range(2)]
            Sst = [ar.alloc("Sst%d" % i, [128], F32) for i in range(2)]
            Sbf = [ar.alloc("Sbf%d" % i, [128], BF16) for i in range(2)]
            Rb = [ar.alloc("Rb%d" % i, [128], BF16) for i in range(2)]
            Vn = [ar.alloc("Vn%d" % i, [2, 128], BF16) for i in range(2)]
            bfv = lambda bk: bk.bitcast(BF16)
            for s_i in range(NS):
                c0 = s_i * S
                gs = gates_sb[:, s_i * 16:(s_i + 1) * 16, :]
                for d_ in range(2):
                    f.op(DVE, lambda e, gs=gs, d_=d_: e.tensor_tensor(out=T1[:, d_, :, :], in0=gs[:, :, d_ * 8:(d_ + 1) * 8], in1=DTB[:, d_, :, :], op=ALU.add), reads=[GSB, DTBB], writes=[T1B])
                    f.op(ACT, lambda e, gs=gs, d_=d_: e.activation(out=BETA[:, d_, :, :], in_=gs[:, :, 16 + d_ * 8:16 + (d_ + 1) * 8], func=AF.Sigmoid), reads=[GSB], writes=[BETAB])
                f.op(ACT, lambda e: e.activation(out=T1f, in_=T1f, func=AF.Exp), reads=[T1B], writes=[T1B])
                f.op(ACT, lambda e: e.activation(out=T1f, in_=T1f, func=AF.Ln, bias=1.0), reads=[T1B], writes=[T1B])
                f.op(DVE, lambda e: e.tensor_tensor(out=GT_f, in0=T1f, in1=NEGAf, op=ALU.mult), reads=[T1B, NEGAB], writes=[GTB])
                f.op(DVE, lambda e: e.tensor_scalar(out=NEGBf, in0=BETAf, scalar1=-1.0, scalar2=None, op0=ALU.mult), reads=[BETAB], writes=[NEGBB])
                bk, BK = next_bank()
                for d_ in range(2):
                    f.op(PE, lambda e, bk=bk, d_=d_: e.matmul(bk[:, d_ * 128:(d_ + 1) * 128], lhsT=Lmat[d_], rhs=GT_f[:, d_ * 128:(d_ + 1) * 128], start=True, stop=True), reads=[CST, GTB], writes=[BK])
                f.op(DVE, lambda e, bk=bk: e.tensor_copy(out=CGf, in_=bk[:, 0:256]), reads=[BK], writes=[CGB])
                bk, BK = next_bank()
                f.op(PE, lambda e, bk=bk: e.matmul(bk[:, 0:256], lhsT=onesf, rhs=GT_f, start=True, stop=True), reads=[ONESF, GTB], writes=[BK])
                f.op(DVE, lambda e, bk=bk: e.tensor_copy(out=TOTf, in_=bk[:, 0:256]), reads=[BK], writes=[TOTB])
                for d_ in range(2):
                    f.op(DVE, lambda e, d_=d_: e.tensor_copy(out=CGc[:, :, d_ * 8:(d_ + 1) * 8], in_=CG[:, d_, :, :]), reads=[CGB], writes=[CGcB])
                f.op(ACT, lambda e: e.activation(out=GEXf, in_=CGf, func=AF.Exp), reads=[CGB], writes=[GEXB])
                f.op(DVE, lambda e: e.tensor_scalar(out=NEGGf, in0=GEXf, scalar1=-1.0, scalar2=None, op0=ALU.mult), reads=[GEXB], writes=[NEGGB])
                f.op(DVE, lambda e: e.tensor_tensor(out=W2f, in0=TOTf, in1=CGf, op=ALU.subtract), reads=[TOTB, CGB], writes=[W2B])
                f.op(ACT, lambda e: e.activation(out=W2f, in_=W2f, func=AF.Exp), reads=[W2B], writes=[W2B])
                f.op(ACT, lambda e: e.activation(out=DECf, in_=TOTf, func=AF.Exp), reads=[TOTB], writes=[DECB])
                for h in range(8):
                    for i in range(3):
                        ch = i * 8 + h
                        f.dma(SP, xp[i][0][:, 2:S + 2], projT_d.ap()[ch * 128:(ch + 1) * 128, c0:c0 + S], reads=[PROJT], writes=[xp[i][1]])
                    f.dma(SP, zt, projT_d.ap()[(24 + h) * 128:(25 + h) * 128, c0:c0 + S], reads=[PROJT], writes=[ZT])
                    f.op(ACT, lambda e: e.activation(out=zs, in_=zt, func=AF.Silu), reads=[ZT], writes=[ZS])
                    for i in range(3):
                        ch = i * 8 + h
                        x_, X_ = xp[i]
                        a_, A_ = acc[i % 2]
                        f.op(DVE, lambda e, x_=x_, a_=a_, ch=ch: e.tensor_scalar(out=a_, in0=x_[:, 2:S + 2], scalar1=cq[2][0][:, ch:ch + 1], scalar2=None, op0=ALU.mult),
                             reads=[X_, cq[2][1]], writes=[A_])
                        for j in (0, 1, 3, 4):
                            f.op(DVE, lambda e, x_=x_, a_=a_, ch=ch, j=j: e.scalar_tensor_tensor(out=a_, in0=x_[:, j:j + S], scalar=cq[j][0][:, ch:ch + 1], in1=a_, op0=ALU.mult, op1=ALU.add),
                                 reads=[X_, cq[j][1], A_], writes=[A_])
                        if i == 2:
                            f.op(ACT, lambda e, a_=a_: e.activation(out=VT, in_=a_, func=AF.Silu), reads=[A_], writes=[VTB])
                            continue
                        f.op(ACT, lambda e, a_=a_: e.activation(out=a_, in_=a_, func=AF.Silu), reads=[A_], writes=[A_])
                        f.op(ACT, lambda e, a_=a_: e.activation(out=sq, in_=a_, func=AF.Square), reads=[A_], writes=[SQ])
                        for t in range(4):
                            bk, BK = next_bank()
                            f.op(PE, lambda e, bk=bk, t=t: e.matmul(bk, lhsT=onesb, rhs=sq[:, t * 512:(t + 1) * 512], start=True, stop=True), reads=[ONESB, SQ], writes=[BK])
                            f.op(ACT, lambda e, bk=bk, t=t: e.activation(out=rstd[:, t * 512:(t + 1) * 512], in_=bk, func=AF.Sqrt, bias=cst[:, 162:163]), reads=[BK, CST], writes=[RSTD])
                        f.op(DVE, lambda e: e.reciprocal(out=rstd, in_=rstd), reads=[RSTD], writes=[RSTD])
                        if i == 0:
                            f.op(DVE, lambda e, a_=a_: e.scalar_tensor_tensor(out=QT, in0=a_, scalar=128.0 ** -0.5, in1=rstd, op0=ALU.mult, op1=ALU.mult), reads=[A_, RSTD], writes=[QTB])
                        else:
                            f.op(DVE, lambda e, a_=a_: e.tensor_tensor(out=KT, in0=a_, in1=rstd, op=ALU.mult), reads=[A_, RSTD], writes=[KTB])
                            f.op(POOL, lambda e, a_=a_: e.tensor_tensor(out=KT32, in0=a_, in1=rstd, op=ALU.mult), reads=[A_, RSTD], writes=[KT32B])
                    for (src, SRCB, dst, DSTB) in ((KT, KTB, Ktok, KTOK), (VT, VTB, Vtok, VTOK_)):
                        for c4 in range(4):
                            bk, BK = next_bank()
                            for j in range(4):
                                c = c4 * 4 + j
                                f.op(PE, lambda e, bk=bk, src=src, c=c, j=j: e.transpose(bfv(bk)[:, j * 128:(j + 1) * 128], src[:, c * 128:(c + 1) * 128], identb),
                                     reads=[SRCB, IDENTB], writes=[BK])
                            f.op(DVE, lambda e, bk=bk, dst=dst, c4=c4: e.tensor_copy(out=dst[:, c4 * 4:(c4 + 1) * 4, :], in_=bfv(bk)[:, 0:512]), reads=[BK], writes=[DSTB])
                    for c4 in range(4):
                        bX, BX = banks[0]
                        bY, BY = banks[1]
                        for j in range(4):
                            c = c4 * 4 + j
                            cs = slice(c * 128, (c + 1) * 128)
                            f.op(PE, lambda e, cs=cs, j=j: e.matmul(bX[:, j * 128:(j + 1) * 128], lhsT=KT32[:, cs], rhs=KT32[:, cs], start=True, stop=True), reads=[KT32B], writes=[BX])
                            f.op(PE, lambda e, cs=cs, j=j: e.matmul(bY[:, j * 128:(j + 1) * 128], lhsT=KT[:, cs], rhs=QT[:, cs], start=True, stop=True), reads=[KTB, QTB], writes=[BY])
                        for d_ in range(2):
                            bD, BD = banks[2]
                            for j in range(4):
                                c = c4 * 4 + j
                                dh = d_ * 8 + h
                                f.op(POOL, lambda e, h=h, c=c, j=j, d_=d_: e.tensor_scalar(out=dgm[:, j, :], in0=ident, scalar1=CG[:, d_, c, h:h + 1], scalar2=None, op0=ALU.mult),
                                     reads=[IDENT, CGB], writes=[DGM])
                                f.op(PE, lambda e, c=c, j=j, dh=dh: e.matmul(bD[:, j * 128:(j + 1) * 128], lhsT=onesf, rhs=dgm[:, j, :], start=True, stop=True),
                                     reads=[ONESF, DGM], writes=[BD])
                            for j in range(4):
                                c = c4 * 4 + j
                                f.op(DVE, lambda e, h=h, c=c, j=j, d_=d_: e.scalar_tensor_tensor(out=t1[:, j, :], in0=bD[:, j * 128:(j + 1) * 128], scalar=CG[:, d_, c, h:h + 1], in1=maskN[d_], op0=ALU.subtract, op1=ALU.max),
                                     reads=[BD, CGB, CST], writes=[T1b])
                                f.op(DVE, lambda e, h=h, c=c, j=j, d_=d_: e.scalar_tensor_tensor(out=t3[:, j, :], in0=bD[:, j * 128:(j + 1) * 128], scalar=CG[:, d_, c, h:h + 1], in1=maskQ[d_], op0=ALU.subtract, op1=ALU.min),
                                     reads=[BD, CGB, CST], writes=[T3b])
                            f.op(ACT, lambda e: e.activation(out=gam1, in_=t1, func=AF.Exp, scale=-1.0), reads=[T1b], writes=[GAM1])
                            f.op(ACT, lambda e: e.activation(out=t3, in_=t3, func=AF.Exp), reads=[T3b], writes=[T3b])
                            for j in range(4):
                                c = c4 * 4 + j
                                f.op(DVE, lambda e, h=h, c=c, j=j, d_=d_: e.scalar_tensor_tensor(out=Nm[:, j, :], in0=bX[:, j * 128:(j + 1) * 128], scalar=NEGB[:, d_, c, h:h + 1], in1=gam1[:, j, :], op0=ALU.mult, op1=ALU.mult),
                                     reads=[BX, NEGBB, GAM1], writes=[NM])
                            f.op(DVE, lambda e, d_=d_, c4=c4: e.tensor_tensor(out=QKm[:, d_, c4 * 4:(c4 + 1) * 4, :], in0=bY.rearrange("p (a b) -> p a b", b=128), in1=t3, op=ALU.mult),
                                 reads=[BY, T3b], writes=[QKB2[d_][c4]])
                            bT, BT = banks[3]
                            for j in range(4):
                                f.op(PE, lambda e, j=j: e.transpose(bT[:, j * 128:(j + 1) * 128], Nm[:, j, :], ident), reads=[NM, IDENT], writes=[BT])
                            P_, PB_ = Nm, NM
                            PT_, PTB_ = PaT[0]
                            X_, XB_ = Xa[0]
                            f.op(ACT, lambda e, PT_=PT_: e.activation(out=PT_, in_=bT.rearrange("p (a b) -> p a b", b=128), func=AF.Copy), reads=[BT], writes=[PTB_])
                            for j in range(4):
                                f.op(POOL, lambda e, j=j, X_=X_, PT_=PT_: e.tensor_tensor(out=X_[:, j, :], in0=PT_[:, j, :], in1=ident, op=ALU.add), reads=[PTB_, IDENT], writes=[XB_])
                            for lv in range(6):
                                last = (lv == 5)
                                bA, BA = banks[4]
                                bB, BB = banks[5]
                                bC, BC = banks[6]
                                P2, P2B = Pa[lv % 2]
                                P2T, P2TB = PaT[(lv + 1) % 2]
                                X2, X2B = Xa[(lv + 1) % 2]
                                for j in range(4):
                                    f.op(PE, lambda e, j=j, P_=P_, PT_=PT_: e.matmul(bA[:, j * 128:(j + 1) * 128], lhsT=PT_[:, j, :], rhs=P_[:, j, :], start=True, stop=True), reads=[PB_, PTB_], writes=[BA])
                                if not last:
                                    for j in range(4):
                                        f.op(PE, lambda e, j=j, P_=P_, PT_=PT_: e.matmul(bB[:, j * 128:(j + 1) * 128], lhsT=P_[:, j, :], rhs=PT_[:, j, :], start=True, stop=True), reads=[PB_, PTB_], writes=[BB])
                                f.op(ACT, lambda e, P2=P2: e.activation(out=P2, in_=bA.rearrange("p (a b) -> p a b", b=128), func=AF.Copy), reads=[BA], writes=[P2B])
                                if not last:
                                    f.op(DVE, lambda e, P2T=P2T: e.tensor_copy(out=P2T, in_=bB.rearrange("p (a b) -> p a b", b=128)), reads=[BB], writes=[P2TB])
                                for j in range(4):
                                    f.op(PE, lambda e, j=j, P2=P2, X_=X_: e.matmul(bC[:, j * 128:(j + 1) * 128], lhsT=P2[:, j, :], rhs=X_[:, j, :], start=True, stop=True), reads=[P2B, XB_], writes=[BC])
                                if not last:
                                    f.op(DVE, lambda e, X2=X2, X_=X_: e.tensor_tensor(out=X2, in0=bC.rearrange("p (a b) -> p a b", b=128), in1=X_, op=ALU.add), reads=[BC, XB_], writes=[X2B])
                                else:
                                    f.op(DVE, lambda e, X2=X2, X_=X_: e.tensor_tensor(out=X2, in0=bC.rearrange("p (a b) -> p a b", b=128), in1=X_, op=ALU.add), reads=[BC, XB_], writes=[X2B])
                                P_, PB_ = P2, P2B
                                PT_, PTB_ = P2T, P2TB
                                X_, XB_ = X2, X2B
                            for j in range(4):
                                c = c4 * 4 + j
                                f.op(ACT, lambda e, h=h, c=c, j=j, d_=d_, X_=X_: e.activation(out=TiTb[:, d_, c, :], in_=X_[:, j, :], func=AF.Copy, scale=BETA[:, d_, c, h:h + 1]),
                                     reads=[XB_, BETAB], writes=[TIB[d_][c4]])
                    f.op(DVE, lambda e: e.memset(Oacc, 0.0), writes=[OACC])
                    for d_ in range(2):
                        f.op(DVE, lambda e, d_=d_: e.memset(Sst[d_][0], 0.0), writes=[Sst[d_][1]])
                        f.op(POOL, lambda e, d_=d_: e.memset(Sbf[d_][0], 0.0), writes=[Sbf[d_][1]])
                    for step in range(16):
                        for d_ in range(2):
                            c = step if d_ == 0 else 15 - step
                            cs = slice(c * 128, (c + 1) * 128)
                            c4 = c // 4
                            S_, SB_ = Sst[d_]
                            sb_, SBB_ = Sbf[d_]
                            r_, RB_ = Rb[d_]
                            v_, VB_ = Vn[d_]
                            bP, BP = banks[d_ * 3 + 0]
                            bQ, BQ = banks[d_ * 3 + 1]
                            bR, BR = banks[d_ * 3 + 2]
                            f.op(PE, lambda e, bP=bP, cs=cs, sb_=sb_: e.matmul(bP[:, 0:128], lhsT=KT[:, cs], rhs=sb_, start=True, stop=True), reads=[KTB, SBB_], writes=[BP])
                            f.op(DVE, lambda e, h=h, bP=bP, r_=r_, c=c, d_=d_: e.scalar_tensor_tensor(out=r_, in0=bP[:, 0:128], scalar=NEGG[:, d_, c, h:h + 1], in1=Vtok[:, c, :], op0=ALU.mult, op1=ALU.add),
                                 reads=[BP, NEGGB, VTOK_], writes=[RB_])
                            f.op(PE, lambda e, bP=bP, r_=r_, c=c, d_=d_: e.matmul(bP[:, 128:256], lhsT=TiTb[:, d_, c, :], rhs=r_, start=True, stop=True), reads=[TIB[d_][c4], RB_], writes=[BP])
                            f.op(ACT, lambda e, bP=bP, v_=v_: e.activation(out=v_[:, 0, :], in_=bP[:, 128:256], func=AF.Copy), reads=[BP], writes=[VB_])
                            f.op(ACT, lambda e, h=h, bP=bP, v_=v_, c=c, d_=d_: e.activation(out=v_[:, 1, :], in_=bP[:, 128:256], func=AF.Copy, scale=W2[:, d_, c, h:h + 1]), reads=[BP, W2B], writes=[VB_])
                            f.op(PE, lambda e, bQ=bQ, cs=cs, sb_=sb_: e.matmul(bQ[:, 0:128], lhsT=QT[:, cs], rhs=sb_, start=True, stop=True), reads=[QTB, SBB_], writes=[BQ])
                            f.op(PE, lambda e, bQ=bQ, v_=v_, c=c, d_=d_: e.matmul(bQ[:, 128:256], lhsT=QKm[:, d_, c, :], rhs=v_[:, 0, :], start=True, stop=True), reads=[QKB2[d_][c4], VB_], writes=[BQ])
                            f.op(DVE, lambda e, h=h, bQ=bQ, c=c, d_=d_: e.scalar_tensor_tensor(out=Oacc[:, c, :], in0=bQ[:, 0:128], scalar=GEX[:, d_, c, h:h + 1], in1=Oacc[:, c, :], op0=ALU.mult, op1=ALU.add),
                                 reads=[BQ, GEXB, OACC], writes=[OACC])
                            f.op(DVE, lambda e, bQ=bQ, c=c: e.tensor_tensor(out=Oacc[:, c, :], in0=bQ[:, 128:256], in1=Oacc[:, c, :], op=ALU.add), reads=[BQ, OACC], writes=[OACC])
                            f.op(PE, lambda e, bR=bR, v_=v_, c=c: e.matmul(bR[:, 0:128], lhsT=Ktok[:, c, :], rhs=v_[:, 1, :], start=True, stop=True), reads=[KTOK, VB_], writes=[BR])
                            f.op(DVE, lambda e, h=h, bR=bR, S_=S_, c=c, d_=d_: e.scalar_tensor_tensor(out=S_, in0=S_, scalar=DEC[:, d_, c, h:h + 1], in1=bR[:, 0:128], op0=ALU.mult, op1=ALU.add),
                                 reads=[BR, DECB, SB_], writes=[SB_])
                            f.op(ACT, lambda e, S_=S_, sb_=sb_: e.activation(out=sb_, in_=S_, func=AF.Copy), reads=[SB_], writes=[SBB_])
                    if False:
                        dbg("gam1", gam1, GAM1, [4, 128], BF16); dbg("gam3", t3, T3b, [4, 128], F32); dbg("Nm", Nm, NM, [4, 128], BF16)
                        dbg("QT", QT, QTB, [S], BF16); dbg("KT", KT, KTB, [S], BF16); dbg("VT", VT, VTB, [S], BF16)
                        dbg("Oacc", Oacc, OACC, [16, 128], F32)
                        dbg("CG", CGf, CGB, [256], F32); dbg("W2", W2f, W2B, [256], F32); dbg("GEX", GEXf, GEXB, [256], F32)
                        dbg("DEC", DECf, DECB, [256], F32); dbg("BETA", BETAf, BETAB, [256], F32); dbg("TOT", TOTf, TOTB, [256], F32)
                        dbg("G", GT_f, GTB, [256], F32)
                        dbg("TiTb0", TiTb[:, 0, 0:4, :], TIB[0][0], [4, 128], BF16); dbg("TiTb1", TiTb[:, 1, 12:16, :], TIB[1][3], [4, 128], BF16)
                        dbg("QKm0", QKm[:, 0, 0:4, :], QKB2[0][0], [4, 128], BF16); dbg("QKm1", QKm[:, 1, 12:16, :], QKB2[1][3], [4, 128], BF16)
                        dbg("Ktok", Ktok, KTOK, [16, 128], BF16); dbg("Vtok", Vtok, VTOK_, [16, 128], BF16)
                    for c in range(16):
                        f.op(ACT, lambda e, c=c: e.activation(out=onb[:, c, :], in_=Oacc[:, c, :], func=AF.Square, accum_out=osq[:, c, 0:1]), reads=[OACC], writes=[ONB, OSQ])
                    f.op(DVE, lambda e: e.tensor_scalar(out=osq[:, :, 1:2], in0=osq[:, :, 0:1], scalar1=1.0 / 128, scalar2=EPS, op0=ALU.mult, op1=ALU.add), reads=[OSQ], writes=[OSQ])
                    f.op(ACT, lambda e: e.activation(out=osq[:, :, 2:3], in_=osq[:, :, 1:2], func=AF.Sqrt), reads=[OSQ], writes=[OSQ])
                    f.op(DVE, lambda e: e.reciprocal(out=osq[:, :, 3:4], in_=osq[:, :, 2:3]), reads=[OSQ], writes=[OSQ])
                    for c in range(16):
                        f.op(DVE, lambda e, c=c: e.scalar_tensor_tensor(out=onb[:, c, :], in0=Oacc[:, c, :], scalar=osq[:, c, 3:4], in1=dnw, op0=ALU.mult, op1=ALU.mult),
                             reads=[OACC, OSQ, DNW], writes=[ONB])
                    for c4 in range(4):
                        bk, BK = banks[7]
                        for j in range(4):
                            c = c4 * 4 + j
                            f.op(PE, lambda e, bk=bk, c=c, j=j: e.transpose(bfv(bk)[:, j * 128:(j + 1) * 128], onb[:, c, :], identb), reads=[ONB, IDENTB], writes=[BK])
                        f.op(DVE, lambda e, bk=bk, c4=c4: e.tensor_tensor(out=mo2[:, c4 * 512:(c4 + 1) * 512], in0=bfv(bk)[:, 0:512], in1=zs[:, c4 * 512:(c4 + 1) * 512], op=ALU.mult),
                             reads=[BK, ZS], writes=[MO2])
                    f.dma(POOL, mixT_d.ap()[h * 128:(h + 1) * 128, c0:c0 + S], mo2, reads=[MO2], writes=[MIXT])
        else:
            z_, Z_ = ar.alloc("zeros", [NT], BF16)
            f.op(DVE, lambda e: e.memset(z_, 0.0), writes=[Z_])
            for c in range(8):
                f.dma(SP, mixT_d.ap()[c * 128:(c + 1) * 128, :], z_, reads=[Z_], writes=[MIXT])
        f.barrier()
        ar.reset(m1)
        cstb, CSTB = ar.alloc("cstb", [NCST], BF16)
        f.op(DVE, lambda e: e.tensor_copy(out=cstb, in_=cst), reads=[CST], writes=[CSTB])
        Rm = cstb[0:32, 128:160]
        maskA, maskB, maskAc = cstb[:, 192:320], cstb[:, 320:448], cstb[0:64, 448:576]
        onesb, ONESB = ar.alloc("onesb", [128], BF16)
        f.op(DVE, lambda e: e.memset(onesb, 1.0), writes=[ONESB])
        posi, POSI = ar.alloc("posi", [S], I32)
        ang, ANG = ar.alloc("ang", [S], F32)
        tmpa, TMPA = ar.alloc("tmpa", [S], F32)
        tmpi, TMPI = posi, POSI
        cosT, COS = ar.alloc("cosT", [S], F32)
        sinT, SIN = ar.alloc("sinT", [S], F32)
        qk = [[[ar.alloc("qk%d_%d_%d" % (sl, g, i), [S], BF16, buf=False) for i in range(2)] for g in range(3)] for sl in range(2)]
        QKB = [[[[f.buf(qk[sl][g][i][:, t * 512:(t + 1) * 512], "qkb") for t in range(4)] for i in range(2)] for g in range(3)] for sl in range(2)]
        DIL = [1, 4, 16]
        vh = [[ar.alloc("vh%d_%d" % (sl, g), [16, 128], BF16) for g in range(3)] for sl in range(1)] * 2
        v0 = [[ar.alloc("v0%d_%d" % (sl, g), [16, 128], BF16) for g in range(3)] for sl in range(1)] * 2
        accn, ACCN = ar.alloc("accn", [S], F32)
        accd, ACCD = ar.alloc("accd", [S], F32)
        mo = [ar.alloc("mo%d" % i, [S], BF16) for i in range(2)]
        rt = [ar.alloc("rt%d" % i, [2, 512], F32) for i in range(2)]
        pb = [ar.alloc("pb%d" % i, [2, 128], BF16) for i in range(4)]
        PI = float(np.pi)
        pit, PIT = ar.alloc("pit", [S], F32)
        f.op(DVE, lambda e: e.memset(pit[0:32, :], PI), writes=[PIT])
        it = 0
        for s_i in range(NS):
            f.dma(SP, posi[0:32, :], pos_d.ap()[s_i:s_i + 1, :].to_broadcast([32, S]), reads=[SMALL], writes=[POSI])
            f.op(DVE, lambda e: e.tensor_copy(out=ang[0:32, :], in_=posi[0:32, :]), reads=[POSI], writes=[ANG])
            f.op(DVE, lambda e: e.tensor_scalar(out=ang[0:32, :], in0=ang[0:32, :], scalar1=cst[0:32, 160:161], scalar2=None, op0=ALU.mult), reads=[ANG, CST], writes=[ANG])
            f.op(DVE, lambda e: e.tensor_scalar(out=tmpi[0:32, :], in0=ang[0:32, :], scalar1=1.0 / (2 * PI), scalar2=None, op0=ALU.mult), reads=[ANG], writes=[TMPI])
            f.op(DVE, lambda e: e.tensor_copy(out=tmpa[0:32, :], in_=tmpi[0:32, :]), reads=[TMPI], writes=[TMPA])
            f.op(DVE, lambda e: e.scalar_tensor_tensor(out=ang[0:32, :], in0=tmpa[0:32, :], scalar=-2 * PI, in1=ang[0:32, :], op0=ALU.mult, op1=ALU.add), reads=[TMPA, ANG], writes=[ANG])
            f.op(DVE, lambda e: e.tensor_tensor(out=tmpa[0:32, :], in0=ang[0:32, :], in1=pit[0:32, :], op=ALU.is_gt), reads=[ANG, PIT], writes=[TMPA])
            f.op(DVE, lambda e: e.scalar_tensor_tensor(out=ang[0:32, :], in0=tmpa[0:32, :], scalar=-2 * PI, in1=ang[0:32, :], op0=ALU.mult, op1=ALU.add), reads=[ANG, TMPA], writes=[ANG])
            f.op(ACT, lambda e: e.activation(out=sinT[0:32, :], in_=ang[0:32, :], func=AF.Sin), reads=[ANG], writes=[SIN])
            f.op(DVE, lambda e: e.scalar_tensor_tensor(out=tmpa[0:32, :], in0=ang[0:32, :], scalar=-1.0, in1=ang[0:32, :], op0=ALU.mult, op1=ALU.max), reads=[ANG], writes=[TMPA])
            f.op(ACT, lambda e: e.activation(out=cosT[0:32, :], in_=tmpa[0:32, :], func=AF.Sin, scale=-1.0, bias=cst[0:32, 161:162]), reads=[TMPA, CST], writes=[COS])
            for h in range(8):
                sl = it % 2
                it += 1
                c0 = s_i * S
                for g in range(3):
                    for i in range(2):
                        ch = 32 + i * 24 + g * 8 + h
                        for t in range(4):
                            f.dma(SP, qk[sl][g][i][:, t * 512:(t + 1) * 512], projT_d.ap()[ch * 128:(ch + 1) * 128, c0 + t * 512:c0 + (t + 1) * 512],
                                  reads=[PROJT], writes=[QKB[sl][g][i][t]])
                    d = DIL[g]
                    L = S // d
                    M = L // 128
                    vcol = g * 1024 + h * 128
                    view = vtok_d.ap()[c0:c0 + S, vcol:vcol + 128].rearrange("(i d) c -> d i c", d=d)
                    vh_, VH_ = vh[sl][g]
                    v0_, V0_ = v0[sl][g]
                    vhv = vh_.rearrange("p (r m) c -> p r m c", m=M)
                    for r in range(d):
                        if M > 1:
                            f.dma(SP, vhv[:, r, 0:M - 1, :], view[r, 64:L - 64, :].rearrange("(m j) c -> j m c", j=128), reads=[VTOK], writes=[VH_])
                    f.dma(SP, vhv[0:64, :, M - 1, :], view[:, L - 64:L, :].rearrange("r j c -> j r c"), reads=[VTOK], writes=[VH_])
                    f.dma(SP, v0_[0:64, 0:d, :], view[:, 0:64, :].rearrange("r j c -> j r c"), reads=[VTOK], writes=[V0_])
                for g in range(3):
                    for i in range(2):
                        X = qk[sl][g][i]
                        for t in range(4):
                            XB = QKB[sl][g][i][t]
                            cs = slice(t * 512, (t + 1) * 512)
                            bk, BK = next_bank(0, 4)
                            r_, R_ = rt[(g * 8 + i * 4 + t) % 2]
                            f.op(PE, lambda e, bk=bk, X=X, cs=cs: e.matmul(bk[0:32, :], lhsT=Rm, rhs=X[0:32, cs], start=True, stop=True), reads=[CSTB, XB], writes=[BK])
                            f.op(DVE, lambda e, r_=r_, X=X, cs=cs: e.tensor_tensor(out=r_[0:32, 0, :], in0=X[0:32, cs], in1=cosT[0:32, cs], op=ALU.mult), reads=[XB, COS], writes=[R_])
                            f.op(DVE, lambda e, r_=r_, bk=bk, cs=cs: e.tensor_tensor(out=r_[0:32, 1, :], in0=bk[0:32, :], in1=sinT[0:32, cs], op=ALU.mult), reads=[BK, SIN, R_], writes=[R_])
                            f.op(DVE, lambda e, r_=r_, X=X, cs=cs: e.tensor_tensor(out=X[0:32, cs], in0=r_[0:32, 0, :], in1=r_[0:32, 1, :], op=ALU.add), reads=[R_], writes=[XB])
                un = 0
                for g in range(3):
                    d = DIL[g]
                    L = S // d
                    M = L // 128
                    Q, K = qk[sl][g][0], qk[sl][g][1]
                    QB_, KB_ = QKB[sl][g][0], QKB[sl][g][1]
                    vh_, VH_ = vh[sl][g]
                    v0_, V0_ = v0[sl][g]
                    vhv = vh_.rearrange("p (r m) c -> p r m c", m=M)
                    units = [(r, qb) for r in range(d) for qb in range(M)]
                    for u0 in range(0, len(units), 4):
                        nb_, NB_ = banks[4 + (un // 4) % 2]
                        db_, DB_ = banks[6 + (un // 4) % 2]
                        for j in range(4):
                            r, qb = units[u0 + j]
                            un += 1
                            sb_, SB_ = banks[un % 4]
                            p_, P_ = pb[un % 4]
                            qsl = slice(r + d * 128 * qb, r + d * 128 * qb + d * 127 + 1, d)
                            blocks = []
                            if qb >= 1:
                                blocks.append((128 * qb - 64, 128, maskA, vhv[:, r, qb - 1, :], VH_))
                            else:
                                blocks.append((0, 64, maskAc, v0_[0:64, r, :], V0_))
                            if qb < M - 1:
                                blocks.append((128 * qb + 64, 128, maskB, vhv[:, r, qb, :], VH_))
                            else:
                                blocks.append((128 * qb + 64, 64, maskB[0:64, :], vhv[0:64, r, qb, :], VH_))
                            for bi, (k0, nk, mk, vap, VB_) in enumerate(blocks):
                                ksl = slice(r + d * k0, r + d * k0 + d * (nk - 1) + 1, d)
                                f.op(PE, lambda e, sb_=sb_, K=K, Q=Q, ksl=ksl, qsl=qsl, nk=nk, bi=bi: e.matmul(
                                    sb_[0:nk, bi * 128:(bi + 1) * 128], lhsT=K[:, ksl], rhs=Q[:, qsl], start=True, stop=True),
                                    reads=KB_ + QB_, writes=[SB_])
                                f.op(ACT, lambda e, sb_=sb_, p_=p_, nk=nk, bi=bi: e.activation(
                                    out=p_[0:nk, bi, :], in_=sb_[0:nk, bi * 128:(bi + 1) * 128], func=AF.Exp, scale=128.0 ** -0.5),
                                    reads=[SB_], writes=[P_])
                                f.op(POOL, lambda e, p_=p_, nk=nk, bi=bi, mk=mk: e.tensor_tensor(out=p_[0:nk, bi, :], in0=p_[0:nk, bi, :], in1=mk, op=ALU.mult),
                                     reads=[P_, CSTB], writes=[P_])
                            for bi, (k0, nk, mk, vap, VB_) in enumerate(blocks):
                                f.op(PE, lambda e, nb_=nb_, p_=p_, vap=vap, nk=nk, bi=bi, j=j: e.matmul(
                                    nb_[:, j * 128:(j + 1) * 128], lhsT=vap, rhs=p_[0:nk, bi, :], start=(bi == 0), stop=(bi == 1)),
                                    reads=[P_, VB_], writes=[NB_])
                            for bi, (k0, nk, mk, vap, VB_) in enumerate(blocks):
                                f.op(PE, lambda e, db_=db_, p_=p_, nk=nk, bi=bi, j=j: e.matmul(
                                    db_[:, j * 128:(j + 1) * 128], lhsT=onesb[0:nk, :], rhs=p_[0:nk, bi, :], start=(bi == 0), stop=(bi == 1)),
                                    reads=[P_, ONESB], writes=[DB_])
                        r, qb = units[u0]
                        if g == 0:
                            sl_ = slice(qb * 128, qb * 128 + 512)
                            f.op(DVE, lambda e, nb_=nb_, sl_=sl_: e.tensor_copy(out=accn[:, sl_], in_=nb_), reads=[NB_], writes=[ACCN])
                            f.op(DVE, lambda e, db_=db_, sl_=sl_: e.tensor_copy(out=accd[:, sl_], in_=db_), reads=[DB_], writes=[ACCD])
                        else:
                            if g == 1:
                                av = lambda a, r=r: a[:, r:r + 4 * 511 + 1:4]
                                iv = lambda b: b
                            else:
                                av = lambda a, r=r: a.rearrange("p (i r) -> p r i", r=16)[:, r:r + 4, :]
                                iv = lambda b: b.rearrange("p (r i) -> p r i", r=4)
                            f.op(DVE, lambda e, nb_=nb_, av=av, iv=iv: e.tensor_tensor(out=av(accn), in0=iv(nb_), in1=av(accn), op=ALU.add), reads=[NB_, ACCN], writes=[ACCN])
                            f.op(DVE, lambda e, db_=db_, av=av, iv=iv: e.tensor_tensor(out=av(accd), in0=iv(db_), in1=av(accd), op=ALU.add), reads=[DB_, ACCD], writes=[ACCD])
                m_, M_ = mo[sl]
                f.op(DVE, lambda e: e.reciprocal(out=accd, in_=accd), reads=[ACCD], writes=[ACCD])
                f.op(DVE, lambda e, m_=m_: e.tensor_tensor(out=m_, in0=accn, in1=accd, op=ALU.mult), reads=[ACCN, ACCD], writes=[M_])
                f.dma(POOL, mixT_d.ap()[(8 + h) * 128:(9 + h) * 128, c0:c0 + S], m_, reads=[M_], writes=[MIXT])

    f.barrier()
    ar.reset(m1)
    wo, WO = ar.alloc("wo", [16, D], BF16)
    for q in range(4):
        f.dma(SP, wo[:, q * 4:(q + 1) * 4, :], woutb_d.ap()[q * 512:(q + 1) * 512, :].rearrange("(k p) n -> p k n", p=128), reads=[WOUTB], writes=[WO])
    mx = [ar.alloc("mx%d" % i, [16, 512], BF16) for i in range(2)]
    xr = [ar.alloc("xr%d" % i, [D], F32) for i in range(2)]
    hb = [ar.alloc("hb%d" % i, [D], F32) for i in range(2)]
    for tile in range(NT // 512):
        t0 = tile * 512
        m_, M_ = mx[tile % 2]
        f.dma(SP, m_, mixT_d.ap()[:, t0:t0 + 512].rearrange("(k p) t -> p k t", p=128), reads=[MIXT], writes=[M_])
        for b in range(4):
            r0 = t0 + b * 128
            x_, X_ = xr[b % 2]
            h_, H_ = hb[b % 2]
            f.dma(SP, x_, x_d.ap()[r0:r0 + 128, :], reads=[Xd], writes=[X_])
            for ds in range(4):
                bk, BK = next_bank(2, 8)
                for kc in range(16):
                    f.op(PE, lambda e, bk=bk, m_=m_, kc=kc, b=b, ds=ds: e.matmul(
                        bk, lhsT=m_[:, kc, b * 128:(b + 1) * 128], rhs=wo[:, kc, ds * 512:(ds + 1) * 512], start=(kc == 0), stop=(kc == 15)),
                        reads=[M_, WO], writes=[BK])
                f.op(DVE, lambda e, bk=bk, h_=h_, x_=x_, ds=ds: e.tensor_tensor(
                    out=h_[:, ds * 512:(ds + 1) * 512], in0=bk, in1=x_[:, ds * 512:(ds + 1) * 512], op=ALU.add),
                    reads=[BK, X_], writes=[H_])
            f.dma(POOL, h1_d.ap()[r0:r0 + 128, :], h_, reads=[H_], writes=[H1])

    if upto <= 3:
        f.barrier(); f.emit(); f.close()
        return nc
    f.barrier()
    ar.reset(m1)
    TT4 = 512
    norm4 = make_norm("n4", 2)
    n2T = ar.alloc("n2T", [16, TT4 + 2], BF16, buf=False)
    N2B = [f.buf(n2T[:, kc, :], "n2T_%d" % kc) for kc in range(16)]
    actT = ar.alloc("actT", [NFC, TT4], BF16, buf=False)
    ACTB = [f.buf(actT[:, c, :], "actT_%d" % c) for c in range(NFC)]
    wu = [ar.alloc("wu%d" % i, [16, 2, 256], BF16) for i in range(2)]
    wd = [ar.alloc("wd%d" % i, [11, 512], BF16) for i in range(2)]
    ug = [ar.alloc("ug%d" % i, [TT4 + 2], F32) for i in range(4)]
    ag = [ar.alloc("ag%d" % i, [TT4], F32) for i in range(4)]
    hq = [ar.alloc("hq%d" % i, [512], F32) for i in range(2)]
    oq = [ar.alloc("oq%d" % i, [512], F32) for i in range(2)]
    halo_rr = [0]
    wu_i = 0
    wd_i = 0
    oq_i = 0
    for tile in range(NT // TT4):
        t0 = tile * TT4
        tl = t0 % S
        blocks = [(h1_d.ap()[t0 + b * 128:t0 + (b + 1) * 128, :], 128, 1 + b * 128, 0, H1) for b in range(TT4 // 128)]
        left = h1_d.ap()[t0 - 1:t0, :] if tl > 0 else None
        right = h1_d.ap()[t0 + TT4:t0 + TT4 + 1, :] if tl + TT4 < S else None
        blocks.append(([left, right], 2, 0, TT4 + 1, H1))
        norm4(blocks, nfw, NFW, n2T, N2B)
        for slab in range(NFC // 2):
            w_, W_ = wu[wu_i % 2]
            wu_i += 1
            f.dma(SP, w_[:, :, 0, :], wfib_d.ap()[:, slab * 256:(slab + 1) * 256].rearrange("(k p) n -> p k n", p=128), reads=[WFIB], writes=[W_])
            f.dma(SP, w_[:, :, 1, :], wfib_d.ap()[:, DFF + slab * 256:DFF + (slab + 1) * 256].rearrange("(k p) n -> p k n", p=128), reads=[WFIB], writes=[W_])
            for pj in range(2):
                fc = slab * 2 + pj
                accs = []
                for gv in range(2):
                    bk, BK = next_bank(2, 6)
                    hk, HK = banks[6 + halo_rr[0] % 2]
                    halo_rr[0] += 1
                    for kc in range(16):
                        f.op(PE, lambda e, bk=bk, w_=w_, kc=kc, gv=gv, pj=pj: e.matmul(
                            bk, lhsT=w_[:, kc, gv, pj * 128:(pj + 1) * 128], rhs=n2T[:, kc, 1:TT4 + 1], start=(kc == 0), stop=(kc == 15)),
                            reads=[W_, N2B[kc]], writes=[BK])
                    for kc in range(16):
                        f.op(PE, lambda e, hk=hk, w_=w_, kc=kc, gv=gv, pj=pj: e.matmul(
                            hk[:, 0:2], lhsT=w_[:, kc, gv, pj * 128:(pj + 1) * 128], rhs=n2T[:, kc, 0:TT4 + 2:TT4 + 1], start=(kc == 0), stop=(kc == 15)),
                            reads=[W_, N2B[kc]], writes=[HK])
                    u_, U_ = ug[(fc * 2 + gv) % 4]
                    a_, A_ = ag[(fc * 2 + gv) % 4]
                    f.op(ACT, lambda e, u_=u_, bk=bk: e.activation(out=u_[:, 1:TT4 + 1], in_=bk, func=AF.Copy), reads=[BK], writes=[U_])
                    f.op(ACT, lambda e, u_=u_, hk=hk: e.activation(out=u_[:, 0:TT4 + 2:TT4 + 1], in_=hk[:, 0:2], func=AF.Copy), reads=[HK], writes=[U_])
                    ch = gv * NFC + fc
                    f.op(DVE, lambda e, u_=u_, a_=a_, ch=ch: e.tensor_scalar(out=a_, in0=u_[:, 1:TT4 + 1], scalar1=cffn[1][0][:, ch:ch + 1], scalar2=None, op0=ALU.mult),
                         reads=[U_, cffn[1][1]], writes=[A_])
                    f.op(DVE, lambda e, u_=u_, a_=a_, ch=ch: e.scalar_tensor_tensor(out=a_, in0=u_[:, 0:TT4], scalar=cffn[0][0][:, ch:ch + 1], in1=a_, op0=ALU.mult, op1=ALU.add),
                         reads=[U_, cffn[0][1], A_], writes=[A_])
                    f.op(DVE, lambda e, u_=u_, a_=a_, ch=ch: e.scalar_tensor_tensor(out=a_, in0=u_[:, 2:TT4 + 2], scalar=cffn[2][0][:, ch:ch + 1], in1=a_, op0=ALU.mult, op1=ALU.add),
                         reads=[U_, cffn[2][1], A_], writes=[A_])
                    accs.append((a_, A_))
                (a_g, A_G), (a_v, A_V) = accs
                f.op(ACT, lambda e, a_g=a_g: e.activation(out=a_g, in_=a_g, func=AF.Silu), reads=[A_G], writes=[A_G])
                f.op(DVE, lambda e, a_g=a_g, a_v=a_v, fc=fc: e.tensor_tensor(out=actT[:, fc, :], in0=a_g, in1=a_v, op=ALU.mult),
                     reads=[A_G, A_V], writes=[ACTB[fc]])
        for ds in range(4):
            obanks = [banks[2 + b] for b in range(4)]
            for q in range(4):
                w_, W_ = wd[wd_i % 2]
                wd_i += 1
                f.dma(SP, w_, wfob_d.ap()[q * 11 * 128:(q + 1) * 11 * 128, ds * 512:(ds + 1) * 512].rearrange("(k p) n -> p k n", p=128), reads=[WFOB], writes=[W_])
                for b in range(4):
                    bk, BK = obanks[b]
                    for i in range(11):
                        fc = q * 11 + i
                        f.op(PE, lambda e, bk=bk, w_=w_, i=i, fc=fc, b=b, q=q: e.matmul(
                            bk, lhsT=actT[:, fc, b * 128:(b + 1) * 128], rhs=w_[:, i, :], start=(q == 0 and i == 0), stop=(q == 3 and i == 10)),
                            reads=[W_, ACTB[fc]], writes=[BK])
            for b in range(4):
                bk, BK = obanks[b]
                r0 = t0 + b * 128
                h_, H_ = hq[oq_i % 2]
                o_, O_ = oq[oq_i % 2]
                oq_i += 1
                f.dma(SP, h_, h1_d.ap()[r0:r0 + 128, ds * 512:(ds + 1) * 512], reads=[H1], writes=[H_])
                f.op(DVE, lambda e, bk=bk, h_=h_, o_=o_: e.tensor_tensor(out=o_, in0=bk, in1=h_, op=ALU.add), reads=[BK, H_], writes=[O_])
                f.dma(POOL, out_d.ap()[r0:r0 + 128, ds * 512:(ds + 1) * 512], o_, reads=[O_], writes=[OUT])

    if upto <= 4:
        f.barrier(); f.emit(); f.close()
        return nc
    f.barrier()
    ar.reset(m1)
    fx = [ar.alloc("fx%d" % i, [D], F32) for i in range(3)]
    fy = [ar.alloc("fy%d" % i, [D], F32) for i in range(2)]
    fj, FJ = ar.alloc("fj", [D], BF16)
    fs = [ar.alloc("fs%d" % i, [4], F32) for i in range(2)]
    for b in range(NT // 128):
        r0 = b * 128
        x_, X_ = fx[b % 3]
        y_, Y_ = fy[b % 2]
        s_, S_ = fs[b % 2]
        f.dma(SP, x_, out_d.ap()[r0:r0 + 128, :], reads=[OUT], writes=[X_])
        f.op(ACT, lambda e, x_=x_, s_=s_: e.activation(out=fj, in_=x_, func=AF.Square, accum_out=s_[:, 0:1]), reads=[X_], writes=[FJ, S_])
        f.op(DVE, lambda e, s_=s_: e.tensor_scalar(out=s_[:, 1:2], in0=s_[:, 0:1], scalar1=1.0 / D, scalar2=EPS, op0=ALU.mult, op1=ALU.add), reads=[S_], writes=[S_])
        f.op(ACT, lambda e, s_=s_: e.activation(out=s_[:, 2:3], in_=s_[:, 1:2], func=AF.Sqrt), reads=[S_], writes=[S_])
        f.op(DVE, lambda e, s_=s_: e.reciprocal(out=s_[:, 3:4], in_=s_[:, 2:3]), reads=[S_], writes=[S_])
        f.op(DVE, lambda e, s_=s_, x_=x_, y_=y_: e.scalar_tensor_tensor(out=y_, in0=x_, scalar=s_[:, 3:4], in1=nfin, op0=ALU.mult, op1=ALU.mult),
             reads=[S_, X_, NFIN], writes=[Y_])
        f.dma(SP, out_d.ap()[r0:r0 + 128, :], y_, reads=[Y_], writes=[OUT])
    f.barrier()
    f.emit()
    f.close()
    return nc


_NC_CACHE = {}


def _get_nc(debug=False, stub_mixers=False):
    key = (debug, stub_mixers)
    if key not in _NC_CACHE:
        _NC_CACHE[key] = build(debug=debug, stub_mixers=stub_mixers)
    return _NC_CACHE[key]


def make_in_maps(inputs, ncores=8):
    g = lambda k: np.ascontiguousarray(np.asarray(inputs[k]))
    x = g("x").astype(np.float32, copy=False)
    pos = g("positions").astype(np.int32, copy=False)
    shared = {
        "norm_mix_w": g("norm_mix_w").reshape(16, 128),
        "w_in": g("w_in").reshape(D, PW),
        "conv_qkv_w": g("conv_qkv_w").reshape(5, 24, 128),
        "a_log_fwd": g("a_log_fwd").reshape(1, 8), "a_log_bwd": g("a_log_bwd").reshape(1, 8),
        "dt_bias_fwd": g("dt_bias_fwd").reshape(1, 8), "dt_bias_bwd": g("dt_bias_bwd").reshape(1, 8),
        "delta_norm_w": g("delta_norm_w").reshape(1, 128),
        "w_out": g("w_out").reshape(D, D),
        "norm_ffn_w": g("norm_ffn_w").reshape(16, 128),
        "w_ffn_in": g("w_ffn_in").reshape(D, 2 * DFF),
        "conv_ffn_w": g("conv_ffn_w").reshape(3, 88, 128),
        "w_ffn_out": g("w_ffn_out").reshape(DFF, D),
        "norm_final_w": g("norm_final_w").reshape(1, D),
        "cst": make_cst(),
    }
    shared = {k: np.ascontiguousarray(v, dtype=np.float32) for k, v in shared.items()}
    maps = []
    for c in range(ncores):
        m = dict(shared)
        m["x"] = np.ascontiguousarray(x[c * NS:(c + 1) * NS].reshape(NT, D))
        m["pos"] = np.ascontiguousarray(pos[c * NS:(c + 1) * NS])
        maps.append(m)
    return maps


def kernel(**inputs):
    nc = _get_nc()
    maps = make_in_maps(inputs)
    res = run_bass_kernel_spmd(nc, maps, core_ids=list(range(8)))
    out = np.concatenate([np.asarray(r["out"]).reshape(NS, S, D) for r in res.results], axis=0)
    return out.astype(np.float32, copy=False)
```

```python
import contextlib
import numpy as np
import concourse.bass as bass
import concourse.mybir as mybir
from concourse.bass_utils import run_bass_kernel_spmd

F32 = mybir.dt.float32
BF16 = mybir.dt.bfloat16
I32 = mybir.dt.int32
AF = mybir.ActivationFunctionType
ALU = mybir.AluOpType

PE, ACT, DVE, POOL, SP = "tensor", "scalar", "vector", "gpsimd", "sync"
ENGS = (PE, ACT, DVE, POOL, SP)

S = 2048
D = 2048
NS = 2
NT = NS * S
PW = 13344
DFF = 5632
NFC = DFF // 128
EPS = 1e-6
COL_Z = 3072
COL_G = 4096
COL_BQ = 4128
COL_BK = 7200
COL_BV = 10272
NFEAT = 80
NCST = 1216


def make_cst():
    c = np.zeros((128, NCST), np.float32)
    c[:, 0:128] = np.eye(128)
    for p in range(16):
        c[p + 16, 128 + p] = -1.0
        c[p, 128 + 16 + p] = 1.0
    inv = 500000.0 ** (-np.arange(0, 32, 2, dtype=np.float32) / 32.0)
    c[0:16, 160] = inv
    c[16:32, 160] = inv
    c[:, 161] = np.pi / 2
    c[:, 162] = EPS
    k = np.arange(128)[:, None]
    q = np.arange(128)[None, :]
    c[:, 192:320] = (k >= q)
    c[:, 320:448] = (k <= q)
    c[0:64, 448:576] = (k[0:64] + 64 >= q)
    BIG = 30000.0
    c[:, 576:704] = np.where(q < k, 0.0, BIG)
    c[:, 704:832] = np.where(q >= k, 0.0, -BIG)
    c[:, 832:960] = np.where(q > k, 0.0, BIG)
    c[:, 960:1088] = np.where(q <= k, 0.0, -BIG)
    c[:, 1088:1216] = (k <= q)
    return c


class Buf:
    def __init__(self, ap, name, dram=False):
        self.ap = ap
        self.name = name
        self.dram = dram
        self.w = {}
        self.r = {}
        self.dsem = None
        self.dcnt = 0


class FW:
    def __init__(self, nc):
        self.nc = nc
        self.es = contextlib.ExitStack()
        self.sem = {}
        self.cnt = {e: 0 for e in ENGS}
        self.ops = {e: [] for e in ENGS}
        self.waited = {e: {} for e in ENGS}
        self.allsems = {}
        for e in (PE, ACT, DVE, POOL):
            self.sem[e] = self.new_sem("e_" + e)
            self.allsems[e] = [self.sem[e], 0]
        self.nbuf = 0
        self.dfree = []
        self.dbufs = []
        self.nds = 0

    def get_dsem(self):
        if self.dfree:
            return self.dfree.pop()
        self.nds += 1
        key = "ds%d" % self.nds
        ent = [self.new_sem(key), 0, key]
        self.allsems[key] = ent
        return ent

    def new_sem(self, name):
        return self.es.enter_context(self.nc.semaphore(name))

    def sbuf(self, name, shape, dtype):
        return self.es.enter_context(self.nc.sbuf_tensor(name, list(shape), dtype))

    def psum(self, name, shape, dtype):
        return self.es.enter_context(self.nc.psum_tensor(name, list(shape), dtype))

    def buf(self, ap, name=None, dram=False):
        self.nbuf += 1
        return Buf(ap, (name or "b") + "_%d" % self.nbuf, dram=dram)

    def _deps(self, eng, reads, writes, skip_key=None):
        need = {}

        def add(key, sem, val):
            if key == skip_key:
                return
            if key not in need or need[key][1] < val:
                need[key] = (sem, val)

        for b in reads:
            for key, (sem, val) in b.w.items():
                add(key, sem, val)
        for b in writes:
            for key, (sem, val) in b.w.items():
                if eng == PE and key == PE:
                    continue
                add(key, sem, val)
            for key, (sem, val) in b.r.items():
                add(key, sem, val)
        out = []
        wd = self.waited[eng]
        for key, (sem, val) in need.items():
            if wd.get(key, 0) >= val:
                continue
            wd[key] = val
            out.append((sem, val))
        return out

    def op(self, eng, fn, reads=(), writes=()):
        waits = self._deps(eng, reads, writes)
        self.cnt[eng] += 1
        c = self.cnt[eng]
        sem = self.sem[eng]
        self.allsems[eng][1] = c
        self.ops[eng].append((waits, fn, sem, 1))
        for b in reads:
            b.r[eng] = (sem, c)
        for b in writes:
            b.w = {eng: (sem, c)}
            b.r = {}

    def dma(self, q, out_ap, in_ap, reads=(), writes=(), **kw):
        owner = None
        for b in list(writes) + list(reads):
            if not b.dram:
                owner = b
                break
        if owner is None:
            owner = writes[0]
        if owner.dsem is None:
            owner.dsem = self.get_dsem()
            self.dbufs.append(owner)
        ent = owner.dsem
        key = ent[2]
        waits = self._deps(q, reads, writes, skip_key=key)
        ent[1] += 16
        sem, c = ent[0], ent[1]

        def fn(e, out_ap=out_ap, in_ap=in_ap, kw=kw):
            return e.dma_start(out=out_ap, in_=in_ap, **kw)

        self.ops[q].append((waits, fn, sem, 16))
        for b in reads:
            b.r[key] = (sem, c)
        for b in writes:
            if b.dram:
                b.w[key] = (sem, c)
            else:
                b.w = {key: (sem, c)}
                b.r = {}

    def barrier(self, engs=ENGS):
        for e in engs:
            waits = []
            wd = self.waited[e]
            for key, ent in self.allsems.items():
                sem, val = ent[0], ent[1]
                if val > 0 and wd.get(key, 0) < val:
                    wd[key] = val
                    waits.append((sem, val))
            if waits:
                self.ops[e].append((waits, None, None, 0))
        if tuple(engs) == tuple(ENGS):
            for b in self.dbufs:
                self.dfree.append(b.dsem)
                b.dsem = None
            self.dbufs = []

    def emit(self):
        print("ops per engine:", {e: len(v) for e, v in self.ops.items()}, "dsems", self.nds, flush=True)
        with self.nc.Block() as block:
            def mk(eng):
                def body(e):
                    for waits, fn, sem, inc in self.ops[eng]:
                        for (s, v) in waits:
                            e.wait_ge(s, v)
                        if fn is not None:
                            fn(e).then_inc(sem, inc)
                return body
            block.tensor(mk(PE))
            block.scalar(mk(ACT))
            block.vector(mk(DVE))
            block.gpsimd(mk(POOL))
            block.sync(mk(SP))

    def close(self):
        self.es.close()


class Arena:
    def __init__(self, f, nbytes):
        self.f = f
        self.n = nbytes // 2
        self.t = f.sbuf("arena", [128, self.n], BF16)
        self.off = 0

    def mark(self):
        return self.off

    def reset(self, m):
        self.off = m

    def alloc(self, name, free_shape, dtype, buf=True):
        nel = int(np.prod(free_shape))
        units = nel * (2 if dtype in (F32, I32) else 1)
        self.off = (self.off + 7) // 8 * 8
        assert self.off + units <= self.n, ("arena overflow", name, self.off, units, self.n)
        ap = self.t[:, self.off:self.off + units]
        self.off += units
        if dtype != BF16:
            ap = ap.bitcast(dtype)
        if len(free_shape) == 2:
            ap = ap.rearrange("p (a b) -> p a b", b=free_shape[1])
        elif len(free_shape) == 3:
            ap = ap.rearrange("p (a b c) -> p a b c", b=free_shape[1], c=free_shape[2])
        if buf:
            return ap, self.f.buf(ap, name)
        return ap


def build(debug=False, stub_mixers=False, upto=99, skip_delta=False):
    nc = bass.Bass("TRN2", target_bir_lowering=False)
    f = FW(nc)

    def din(name, shape, dt=F32):
        return nc.dram_tensor(name, list(shape), dt, kind="ExternalInput")

    def dscr(name, shape, dt):
        return nc.dram_tensor(name, list(shape), dt, kind=("ExternalOutput" if debug else "Internal"))

    x_d = din("x", [NT, D])
    pos_d = din("pos", [NS, S], I32)
    nmw_d = din("norm_mix_w", [16, 128])
    win_d = din("w_in", [D, PW])
    cqkv_d = din("conv_qkv_w", [5, 24, 128])
    alf_d = din("a_log_fwd", [1, 8]); alb_d = din("a_log_bwd", [1, 8])
    dtf_d = din("dt_bias_fwd", [1, 8]); dtb_d = din("dt_bias_bwd", [1, 8])
    dnw_d = din("delta_norm_w", [1, 128])
    wout_d = din("w_out", [D, D])
    nfw_d = din("norm_ffn_w", [16, 128])
    wfi_d = din("w_ffn_in", [D, 2 * DFF])
    cffn_d = din("conv_ffn_w", [3, 88, 128])
    wfo_d = din("w_ffn_out", [DFF, D])
    nfin_d = din("norm_final_w", [1, D])
    cst_d = din("cst", [128, NCST])
    out_d = nc.dram_tensor("out", [NT, D], F32, kind="ExternalOutput")

    winb_d = nc.dram_tensor("w_in_bf", [D, PW], BF16, kind="Internal")
    woutb_d = nc.dram_tensor("w_out_bf", [D, D], BF16, kind="Internal")
    wfib_d = nc.dram_tensor("w_ffn_in_bf", [D, 2 * DFF], BF16, kind="Internal")
    wfob_d = nc.dram_tensor("w_ffn_out_bf", [DFF, D], BF16, kind="Internal")
    wfit_d = nc.dram_tensor("w_ffn_in_tiled", [NFC // 2, 128, 16 * 2 * 256], BF16, kind="Internal")
    wfot_d = nc.dram_tensor("w_ffn_out_tiled", [16, 128, 11 * 512], BF16, kind="Internal")
    projT_d = dscr("projT", [NFEAT * 128, NT], BF16)
    vtok_d = dscr("vtok", [NT, 3072], BF16)
    gates_d = dscr("gates", [128, NT // 128, 32], F32)
    mixT_d = dscr("mixT", [D, NT], BF16)
    h1_d = dscr("h1", [NT, D], F32)

    B = lambda t, n: f.buf(t.ap(), n, dram=True)
    Xd, WIN, WOUT, WFI, WFO = B(x_d, "x"), B(win_d, "win"), B(wout_d, "wout"), B(wfi_d, "wfi"), B(wfo_d, "wfo")
    WINB, WOUTB, WFIB, WFOB = B(winb_d, "winb"), B(woutb_d, "woutb"), B(wfib_d, "wfib"), B(wfob_d, "wfob")
    WFIT, WFOT = B(wfit_d, "wfit"), B(wfot_d, "wfot")
    PROJT, VTOK, GATES, MIXT, H1, OUT = B(projT_d, "projT"), B(vtok_d, "vtok"), B(gates_d, "gates"), B(mixT_d, "mixT"), B(h1_d, "h1"), B(out_d, "out")
    SMALL = f.buf(None, "small", dram=True)

    dbg_list = []

    def dbg(name, ap, BUF, shape, dt):
        if not debug:
            return
        t = nc.dram_tensor("dbg_" + name, [128] + list(shape), dt, kind="ExternalOutput")
        DB = f.buf(t.ap(), "dbg_" + name, dram=True)
        f.dma(SP, t.ap(), ap, reads=[BUF], writes=[DB])

    ar = Arena(f, 190 * 1024)
    banks = []
    for i in range(8):
        t = f.psum("bank%d" % i, [128, 512], F32)
        banks.append((t[:], f.buf(t[:], "bank%d" % i)))
    bank_rr = [0]

    def next_bank(lo=0, hi=8):
        i = lo + bank_rr[0] % (hi - lo)
        bank_rr[0] += 1
        return banks[i]

    cst, CST = ar.alloc("cst", [NCST], F32)
    f.dma(SP, cst, cst_d.ap(), reads=[SMALL], writes=[CST])
    ident, IDENT = cst[:, 0:128], CST
    identb, IDENTB = ar.alloc("identb", [128], BF16)
    f.op(DVE, lambda e: e.tensor_copy(out=identb, in_=ident), reads=[IDENT], writes=[IDENTB])
    gates_sb, GSB = ar.alloc("gates_sb", [NT // 128, 32], F32)

    def load_featvec(dram_t, nrow, name):
        rows, ROWS = ar.alloc(name + "_r", [128], F32)
        f.dma(SP, rows[0:nrow, :], dram_t.ap(), reads=[SMALL], writes=[ROWS])
        res, RES = ar.alloc(name, [nrow], F32)
        bk, BK = next_bank()
        f.op(PE, lambda e: e.transpose(bk[:, 0:nrow], rows[0:nrow, :], ident[0:nrow, 0:nrow]), reads=[ROWS, IDENT], writes=[BK])
        f.op(DVE, lambda e: e.tensor_copy(out=res, in_=bk[:, 0:nrow]), reads=[BK], writes=[RES])
        return res, RES

    nmw, NMW = load_featvec(nmw_d, 16, "nmw")
    nfw, NFW = load_featvec(nfw_d, 16, "nfw")
    cffn = []
    for j in range(3):
        t_ = nc.dram_tensor("cffn_view%d" % j, [1], F32, kind="Internal") if False else None
        rows, ROWS = ar.alloc("cffn_r%d" % j, [128], F32)
        f.dma(SP, rows[0:88, :], cffn_d.ap()[j], reads=[SMALL], writes=[ROWS])
        res, RES = ar.alloc("cffn%d" % j, [88], F32)
        bk, BK = next_bank()
        f.op(PE, lambda e, bk=bk, rows=rows: e.transpose(bk[:, 0:88], rows[0:88, :], ident[0:88, 0:88]), reads=[ROWS, IDENT], writes=[BK])
        f.op(DVE, lambda e, bk=bk, res=res: e.tensor_copy(out=res, in_=bk[:, 0:88]), reads=[BK], writes=[RES])
        cffn.append((res, RES))
    nfin, NFIN = ar.alloc("nfin", [D], F32)
    f.dma(SP, nfin, nfin_d.ap().to_broadcast([128, D]), reads=[SMALL], writes=[NFIN])

    base_mark = ar.mark()

    def cast_w(src, dst, SRC, DST, rows, cols, b):
        for r0 in range(0, rows, 128):
            f.dma(POOL, dst.ap()[r0:r0 + 128, :].rearrange("r (a b) -> r a b", b=b),
                  src.ap()[r0:r0 + 128, :].rearrange("r (a b) -> r a b", b=b), reads=[SRC], writes=[DST])
    cast_w(win_d, winb_d, WIN, WINB, D, PW, 834)
    cast_w(wout_d, woutb_d, WOUT, WOUTB, D, D, 1024)
    cast_w(wfi_d, wfib_d, WFI, WFIB, D, 2 * DFF, 1024)
    cast_w(wfo_d, wfob_d, WFO, WFOB, DFF, D, 1024)

    if upto <= 0:
        f.barrier(); f.emit(); f.close()
        return nc
    def make_norm(tag, nslot=3):
        xs = [ar.alloc("%s_x%d" % (tag, i), [D], F32) for i in range(nslot)]
        junk, JUNK = ar.alloc(tag + "_junk", [D], BF16)
        st = [ar.alloc("%s_st%d" % (tag, i), [4], F32) for i in range(2)]
        dg = [ar.alloc("%s_dg%d" % (tag, i), [128], F32) for i in range(2)]
        ctr = [0]

        def run(blocks, wn, WN, nT, NTB):
            for blk in blocks:
                one(blk, wn, WN, nT, NTB)

        def one(blk, wn, WN, nT, NTB):
            rows, npart, col, step, SRCB = blk
            bi = ctr[0]
            ctr[0] += 1
            if True:
                xt, XT = xs[bi % nslot]
                s_, ST = st[bi % 2]
                d_, DG = dg[bi % 2]
                if npart == 128:
                    f.dma(SP, xt, rows, reads=[SRCB], writes=[XT])
                else:
                    f.op(POOL, lambda e, xt=xt: e.memset(xt[0:2, :], 0.0), writes=[XT])
                    for ri, r in enumerate(rows):
                        if r is not None:
                            f.dma(SP, xt[ri:ri + 1, :], r, reads=[SRCB], writes=[XT])
                p = npart
                f.op(ACT, lambda e, xt=xt, s_=s_, p=p: e.activation(out=junk[0:p, :], in_=xt[0:p, :], func=AF.Square, accum_out=s_[0:p, 0:1]),
                     reads=[XT], writes=[JUNK, ST])
                f.op(DVE, lambda e, s_=s_, p=p: e.tensor_scalar(out=s_[0:p, 1:2], in0=s_[0:p, 0:1], scalar1=1.0 / D, scalar2=EPS, op0=ALU.mult, op1=ALU.add),
                     reads=[ST], writes=[ST])
                f.op(ACT, lambda e, s_=s_, p=p: e.activation(out=s_[0:p, 2:3], in_=s_[0:p, 1:2], func=AF.Sqrt), reads=[ST], writes=[ST])
                f.op(DVE, lambda e, s_=s_, p=p: e.reciprocal(out=s_[0:p, 3:4], in_=s_[0:p, 2:3]), reads=[ST], writes=[ST])
                f.op(DVE, lambda e, s_=s_, d_=d_, p=p: e.tensor_scalar(out=d_[0:p, 0:p], in0=ident[0:p, 0:p], scalar1=s_[0:p, 3:4], scalar2=None, op0=ALU.mult),
                     reads=[ST, IDENT], writes=[DG])
                for g4 in range(4):
                    bk, BK = next_bank(0, 2)
                    for j in range(4):
                        kc = g4 * 4 + j
                        f.op(PE, lambda e, bk=bk, xt=xt, d_=d_, kc=kc, j=j, p=p: e.matmul(
                            bk[:, j * 128:j * 128 + p], lhsT=xt[0:p, kc * 128:(kc + 1) * 128], rhs=d_[0:p, 0:p], start=True, stop=True),
                            reads=[XT, DG], writes=[BK])
                    for j in range(4):
                        kc = g4 * 4 + j
                        if p == 128:
                            o_ap = nT[:, kc, col:col + 128]
                        else:
                            o_ap = nT[:, kc, col:col + step + 1:step]
                        i_ap = bk[:, j * 128:j * 128 + p]
                        if g4 % 2 == 0:
                            f.op(ACT, lambda e, o_ap=o_ap, i_ap=i_ap, kc=kc: e.activation(out=o_ap, in_=i_ap, func=AF.Copy, scale=wn[:, kc:kc + 1]),
                                 reads=[BK, WN], writes=[NTB[kc]])
                        else:
                            f.op(DVE, lambda e, o_ap=o_ap, i_ap=i_ap, kc=kc: e.tensor_scalar(out=o_ap, in0=i_ap, scalar1=wn[:, kc:kc + 1], scalar2=None, op0=ALU.mult),
                                 reads=[BK, WN], writes=[NTB[kc]])
        return run

    TT1 = 1024
    NH1 = TT1 // 512
    m1 = ar.mark()
    norm1 = make_norm("n1")
    nT1 = ar.alloc("nT1", [16, TT1], BF16, buf=False)
    NT1B = [f.buf(nT1[:, kc, :], "nT1_%d" % kc) for kc in range(16)]
    wsl = [ar.alloc("wsl%d" % i, [16, 512], BF16) for i in range(3)]
    stg = [ar.alloc("stg%d" % i, [4, TT1], BF16) for i in range(2)]
    vst = [ar.alloc("vst%d" % i, [512], BF16) for i in range(2)]
    wg, WG = ar.alloc("wg", [16, 32], BF16)
    f.dma(SP, wg, winb_d.ap()[:, COL_G:COL_G + 32].rearrange("(k p) n -> p k n", p=128), reads=[WINB], writes=[WG])
    feat_slabs = [c for c in range(0, 4096, 512)] + [COL_BQ + c for c in range(0, 3072, 512)] + [COL_BK + c for c in range(0, 3072, 512)]
    evac_rr = [0]

    def evac(out_ap, in_ap, reads, writes):
        evac_rr[0] += 1
        if evac_rr[0] % 2 == 0:
            f.op(ACT, lambda e: e.activation(out=out_ap, in_=in_ap, func=AF.Copy), reads=reads, writes=writes)
        else:
            f.op(DVE, lambda e: e.tensor_copy(out=out_ap, in_=in_ap), reads=reads, writes=writes)

    si = 0
    for tile in range(NT // TT1):
        t0 = tile * TT1
        blocks = [(x_d.ap()[t0 + b * 128:t0 + (b + 1) * 128, :], 128, b * 128, 0, Xd) for b in range(TT1 // 128)]
        norm1(blocks, nmw, NMW, nT1, NT1B)
        for sidx, c0 in enumerate(feat_slabs):
            w_, W_ = wsl[si % 3]
            sg_, SG_ = stg[si % 2]
            si += 1
            f.dma(SP, w_, winb_d.ap()[:, c0:c0 + 512].rearrange("(k p) n -> p k n", p=128), reads=[WINB], writes=[W_])
            for j in range(4):
                for h in range(NH1):
                    bk, BK = next_bank(2, 8)
                    for kc in range(16):
                        f.op(PE, lambda e, bk=bk, w_=w_, kc=kc, j=j, h=h: e.matmul(
                            bk, lhsT=w_[:, kc, j * 128:(j + 1) * 128], rhs=nT1[:, kc, h * 512:(h + 1) * 512], start=(kc == 0), stop=(kc == 15)),
                            reads=[W_, NT1B[kc]], writes=[BK])
                    evac(sg_[:, j, h * 512:(h + 1) * 512], bk, [BK], [SG_])
            f.dma(POOL, projT_d.ap()[sidx * 512:(sidx + 1) * 512, t0:t0 + TT1].rearrange("(j p) t -> p j t", p=128), sg_,
                  reads=[SG_], writes=[PROJT])
        for sb in range(6):
            c0 = COL_BV + sb * 512
            w_, W_ = wsl[si % 3]
            si += 1
            f.dma(SP, w_, winb_d.ap()[:, c0:c0 + 512].rearrange("(k p) n -> p k n", p=128), reads=[WINB], writes=[W_])
            for b in range(TT1 // 128):
                bk, BK = next_bank(2, 8)
                for kc in range(16):
                    f.op(PE, lambda e, bk=bk, w_=w_, kc=kc, b=b: e.matmul(
                        bk, lhsT=nT1[:, kc, b * 128:(b + 1) * 128], rhs=w_[:, kc, :], start=(kc == 0), stop=(kc == 15)),
                        reads=[W_, NT1B[kc]], writes=[BK])
                v_, V_ = vst[(sb * 8 + b) % 2]
                evac(v_, bk, [BK], [V_])
                f.dma(POOL, vtok_d.ap()[t0 + b * 128:t0 + (b + 1) * 128, sb * 512:(sb + 1) * 512], v_, reads=[V_], writes=[VTOK])
        for b in range(TT1 // 128):
            bk, BK = next_bank(2, 8)
            for kc in range(16):
                f.op(PE, lambda e, bk=bk, kc=kc, b=b: e.matmul(
                    bk[:, 0:32], lhsT=nT1[:, kc, b * 128:(b + 1) * 128], rhs=wg[:, kc, :], start=(kc == 0), stop=(kc == 15)),
                    reads=[WG, NT1B[kc]], writes=[BK])
            gb = t0 // 128 + b
            f.op(DVE, lambda e, bk=bk, gb=gb: e.tensor_copy(out=gates_sb[:, gb, :], in_=bk[:, 0:32]), reads=[BK], writes=[GSB])

    if upto <= 1:
        f.barrier(); f.emit(); f.close()
        return nc
    if debug:
        f.dma(SP, gates_d.ap(), gates_sb, reads=[GSB], writes=[GATES])
    f.barrier()
    ar.reset(m1)
    for slab in range(NFC // 2):
        for gv in range(2):
            c0_ = gv * DFF + slab * 256
            f.dma(ACT, wfit_d.ap()[slab].rearrange("p (k g n) -> p k g n", k=16, g=2)[:, :, gv, :],
                  wfib_d.ap()[:, c0_:c0_ + 256].rearrange("(k p) n -> p k n", p=128), reads=[WFIB], writes=[WFIT])
    for ds in range(4):
        for q in range(4):
            f.dma(ACT, wfot_d.ap()[ds * 4 + q].rearrange("p (k n) -> p k n", k=11),
                  wfob_d.ap()[q * 11 * 128:(q + 1) * 11 * 128, ds * 512:(ds + 1) * 512].rearrange("(k p) n -> p k n", p=128), reads=[WFOB], writes=[WFOT])
    if stub_mixers:
        z_, Z_ = ar.alloc("zeros", [NT], BF16)
        f.op(DVE, lambda e: e.memset(z_, 0.0), writes=[Z_])
        for c in range(16):
            f.dma(SP, mixT_d.ap()[c * 128:(c + 1) * 128, :], z_, reads=[Z_], writes=[MIXT])
    else:
        if not skip_delta:

            maskN = [cst[:, 576:704], cst[:, 832:960]]
            maskQ = [cst[:, 704:832], cst[:, 960:1088]]
            Lmat = [cst[:, 1088:1216], cst[:, 192:320]]
            cq = []
            for j in range(5):
                rows, ROWS = ar.alloc("cq_r%d" % j, [128], F32)
                f.dma(SP, rows[0:24, :], cqkv_d.ap()[j], reads=[SMALL], writes=[ROWS])
                res, RES = ar.alloc("cq%d" % j, [24], F32)
                bk, BK = next_bank()
                f.op(PE, lambda e, bk=bk, rows=rows: e.transpose(bk[:, 0:24], rows[0:24, :], ident[0:24, 0:24]), reads=[ROWS, IDENT], writes=[BK])
                f.op(DVE, lambda e, bk=bk, res=res: e.tensor_copy(out=res, in_=bk[:, 0:24]), reads=[BK], writes=[RES])
                cq.append((res, RES))
            prm, PRM = ar.alloc("prm", [2, 2, 8], F32)
            f.dma(SP, prm[:, 0, 0, :], dtf_d.ap().to_broadcast([128, 8]), reads=[SMALL], writes=[PRM])
            f.dma(SP, prm[:, 0, 1, :], dtb_d.ap().to_broadcast([128, 8]), reads=[SMALL], writes=[PRM])
            f.dma(SP, prm[:, 1, 0, :], alf_d.ap().to_broadcast([128, 8]), reads=[SMALL], writes=[PRM])
            f.dma(SP, prm[:, 1, 1, :], alb_d.ap().to_broadcast([128, 8]), reads=[SMALL], writes=[PRM])
            f.op(ACT, lambda e: e.activation(out=prm[:, 1, :, :], in_=prm[:, 1, :, :], func=AF.Exp), reads=[PRM], writes=[PRM])
            f.op(DVE, lambda e: e.tensor_scalar(out=prm[:, 1, :, :], in0=prm[:, 1, :, :], scalar1=-1.0, scalar2=None, op0=ALU.mult), reads=[PRM], writes=[PRM])
            DTB, DTBB = ar.alloc("DTB", [2, 16, 8], F32)
            NEGAf, NEGAB = ar.alloc("NEGA", [256], F32); NEGA = NEGAf.rearrange("p (d c h) -> p d c h", d=2, c=16)
            for d_ in range(2):
                for c in range(16):
                    f.op(POOL, lambda e, d_=d_, c=c: e.tensor_copy(out=DTB[:, d_, c, :], in_=prm[:, 0, d_, :]), reads=[PRM], writes=[DTBB])
                    f.op(POOL, lambda e, d_=d_, c=c: e.tensor_copy(out=NEGA[:, d_, c, :], in_=prm[:, 1, d_, :]), reads=[PRM], writes=[NEGAB])
            dnw, DNW = ar.alloc("dnw", [128], F32)
            f.dma(SP, dnw, dnw_d.ap().to_broadcast([128, 128]), reads=[SMALL], writes=[DNW])
            onesf, ONESF = ar.alloc("onesf", [128], F32)
            f.op(DVE, lambda e: e.memset(onesf, 1.0), writes=[ONESF])
            onesb, ONESB = ar.alloc("onesb2", [128], BF16)
            f.op(DVE, lambda e: e.memset(onesb, 1.0), writes=[ONESB])
            V4 = lambda t: t.rearrange("p (d c h) -> p d c h", d=2, c=16)
            T1f, T1B = ar.alloc("T1", [256], F32); T1 = V4(T1f)
            GT_f, GTB = ar.alloc("GT_", [256], F32); GT_ = V4(GT_f)
            NEGBf, NEGBB = ar.alloc("NEGB", [256], F32); NEGB = V4(NEGBf)
            BETAf, BETAB = ar.alloc("BETA", [256], F32); BETA = V4(BETAf)
            CGf, CGB = ar.alloc("CG", [256], F32); CG = V4(CGf)
            CGc, CGcB = ar.alloc("CGc", [16, 16], F32)
            TOTf, TOTB = ar.alloc("TOT", [256], F32); TOT = V4(TOTf)
            GEXf, GEXB = ar.alloc("GEX", [256], F32); GEX = V4(GEXf)
            NEGGf, NEGGB = ar.alloc("NEGG", [256], F32); NEGG = V4(NEGGf)
            W2f, W2B = ar.alloc("W2", [256], F32); W2 = V4(W2f)
            DECf, DECB = ar.alloc("DEC", [256], F32); DEC = V4(DECf)
            xp = [ar.alloc("xp%d" % i, [S + 4], BF16) for i in range(3)]
            zt, ZT = ar.alloc("zt", [S], BF16)
            for i in range(3):
                f.op(DVE, lambda e, i=i: e.memset(xp[i][0], 0.0), writes=[xp[i][1]])
            acc = [ar.alloc("cacc%d" % i, [S], F32) for i in range(2)]
            sq, SQ = ar.alloc("sq", [S], BF16)
            rstd, RSTD = ar.alloc("rstd", [S], F32)
            QT, QTB = ar.alloc("QT", [S], BF16)
            KT, KTB = ar.alloc("KT", [S], BF16)
            KT32, KT32B = ar.alloc("KT32", [S], F32)
            VT, VTB = ar.alloc("VT", [S], BF16)
            zs, ZS = ar.alloc("zs", [S], BF16)
            Ktok, KTOK = ar.alloc("Ktok", [16, 128], BF16)
            Vtok, VTOK_ = ar.alloc("Vtok", [16, 128], BF16)
            Oacc, OACC = acc[0][0].rearrange("p (a b) -> p a b", b=128), acc[0][1]
            onb, ONB = ar.alloc("onb", [16, 128], BF16)
            mo2, MO2 = sq, SQ
            osq, OSQ = ar.alloc("osq", [16, 4], F32)
            TiTb = ar.alloc("TiTb", [2, 16, 128], BF16, buf=False)
            QKm = ar.alloc("QKm", [2, 16, 128], BF16, buf=False)
            TIB = [[f.buf(TiTb[:, d_, c4 * 4:(c4 + 1) * 4, :], "tib") for c4 in range(4)] for d_ in range(2)]
            QKB2 = [[f.buf(QKm[:, d_, c4 * 4:(c4 + 1) * 4, :], "qkmb") for c4 in range(4)] for d_ in range(2)]
            dgm2 = [ar.alloc("dgm%d" % i, [4, 128], F32) for i in range(2)]
            t12 = [ar.alloc("t1_%d" % i, [4, 128], F32) for i in range(2)]
            t32 = [ar.alloc("t3_%d" % i, [4, 128], F32) for i in range(2)]
            gam12 = [ar.alloc("gam1_%d" % i, [4, 128], F32) for i in range(2)]
            Nm2 = [ar.alloc("Nm%d" % i, [4, 128], F32) for i in range(2)]
            Pa2 = [[ar.alloc("Pa%d_%d" % (d_, i), [4, 128], F32) for i in range(2)] for d_ in range(2)]
            PaT2 = [[ar.alloc("PaT%d_%d" % (d_, i), [4, 128], F32) for i in range(2)] for d_ in range(2)]
            Xa2 = [[ar.alloc("Xa%d_%d" % (d_, i), [4, 128], F32) for i in range(2)] for d_ in range(2)]
            KKs, KKSB = ar.alloc("KKs", [4, 128], F32)
            KQs, KQSB = ar.alloc("KQs", [4, 128], F32)
            ident4, ID4B = ar.alloc("ident4", [4, 128], F32)
            for j in range(4):
                f.op(DVE, lambda e, j=j: e.tensor_copy(out=ident4[:, j, :], in_=ident), reads=[IDENT], writes=[ID4B])
            print("phase2 arena KiB", ar.off * 2 / 1024)
            Sst = [ar.alloc("Sst%d" % i, [128], F32) for i in range(2)]
            Sbf = [ar.alloc("Sbf%d" % i, [128], BF16) for i in range(2)]
            Rb = [ar.alloc("Rb%d" % i, [128], BF16) for i in range(2)]
            Vn = [ar.alloc("Vn%d" % i, [2, 128], BF16) for i in range(2)]
            bfv = lambda bk: bk.bitcast(BF16)
            for s_i in range(NS):
                c0 = s_i * S
                gs = gates_sb[:, s_i * 16:(s_i + 1) * 16, :]
                for d_ in range(2):
                    f.op(DVE, lambda e, gs=gs, d_=d_: e.tensor_tensor(out=T1[:, d_, :, :], in0=gs[:, :, d_ * 8:(d_ + 1) * 8], in1=DTB[:, d_, :, :], op=ALU.add), reads=[GSB, DTBB], writes=[T1B])
                    f.op(ACT, lambda e, gs=gs, d_=d_: e.activation(out=BETA[:, d_, :, :], in_=gs[:, :, 16 + d_ * 8:16 + (d_ + 1) * 8], func=AF.Sigmoid), reads=[GSB], writes=[BETAB])
                f.op(ACT, lambda e: e.activation(out=T1f, in_=T1f, func=AF.Exp), reads=[T1B], writes=[T1B])
                f.op(ACT, lambda e: e.activation(out=T1f, in_=T1f, func=AF.Ln, bias=1.0), reads=[T1B], writes=[T1B])
                f.op(DVE, lambda e: e.tensor_tensor(out=GT_f, in0=T1f, in1=NEGAf, op=ALU.mult), reads=[T1B, NEGAB], writes=[GTB])
                f.op(DVE, lambda e: e.tensor_scalar(out=NEGBf, in0=BETAf, scalar1=-1.0, scalar2=None, op0=ALU.mult), reads=[BETAB], writes=[NEGBB])
                bk, BK = next_bank()
                for d_ in range(2):
                    f.op(PE, lambda e, bk=bk, d_=d_: e.matmul(bk[:, d_ * 128:(d_ + 1) * 128], lhsT=Lmat[d_], rhs=GT_f[:, d_ * 128:(d_ + 1) * 128], start=True, stop=True), reads=[CST, GTB], writes=[BK])
                f.op(DVE, lambda e, bk=bk: e.tensor_copy(out=CGf, in_=bk[:, 0:256]), reads=[BK], writes=[CGB])
                bk, BK = next_bank()
                f.op(PE, lambda e, bk=bk: e.matmul(bk[:, 0:256], lhsT=onesf, rhs=GT_f, start=True, stop=True), reads=[ONESF, GTB], writes=[BK])
                f.op(DVE, lambda e, bk=bk: e.tensor_copy(out=TOTf, in_=bk[:, 0:256]), reads=[BK], writes=[TOTB])
                for d_ in range(2):
                    f.op(DVE, lambda e, d_=d_: e.tensor_copy(out=CGc[:, :, d_ * 8:(d_ + 1) * 8], in_=CG[:, d_, :, :]), reads=[CGB], writes=[CGcB])
                f.op(ACT, lambda e: e.activation(out=GEXf, in_=CGf, func=AF.Exp), reads=[CGB], writes=[GEXB])
                f.op(DVE, lambda e: e.tensor_scalar(out=NEGGf, in0=GEXf, scalar1=-1.0, scalar2=None, op0=ALU.mult), reads=[GEXB], writes=[NEGGB])
                f.op(DVE, lambda e: e.tensor_tensor(out=W2f, in0=TOTf, in1=CGf, op=ALU.subtract), reads=[TOTB, CGB], writes=[W2B])
                f.op(ACT, lambda e: e.activation(out=W2f, in_=W2f, func=AF.Exp), reads=[W2B], writes=[W2B])
                f.op(ACT, lambda e: e.activation(out=DECf, in_=TOTf, func=AF.Exp), reads=[TOTB], writes=[DECB])
                for h in range(8):
                    for i in range(3):
                        ch = i * 8 + h
                        f.dma(SP, xp[i][0][:, 2:S + 2], projT_d.ap()[ch * 128:(ch + 1) * 128, c0:c0 + S], reads=[PROJT], writes=[xp[i][1]])
                    f.dma(SP, zt, projT_d.ap()[(24 + h) * 128:(25 + h) * 128, c0:c0 + S], reads=[PROJT], writes=[ZT])
                    f.op(ACT, lambda e: e.activation(out=zs, in_=zt, func=AF.Silu), reads=[ZT], writes=[ZS])
                    for i in range(3):
                        ch = i * 8 + h
                        x_, X_ = xp[i]
                        a_, A_ = acc[i % 2]
                        f.op(DVE, lambda e, x_=x_, a_=a_, ch=ch: e.tensor_scalar(out=a_, in0=x_[:, 2:S + 2], scalar1=cq[2][0][:, ch:ch + 1], scalar2=None, op0=ALU.mult),
                             reads=[X_, cq[2][1]], writes=[A_])
                        for j in (0, 1, 3, 4):
                            f.op(DVE, lambda e, x_=x_, a_=a_, ch=ch, j=j: e.scalar_tensor_tensor(out=a_, in0=x_[:, j:j + S], scalar=cq[j][0][:, ch:ch + 1], in1=a_, op0=ALU.mult, op1=ALU.add),
                                 reads=[X_, cq[j][1], A_], writes=[A_])
                        if i == 2:
                            f.op(ACT, lambda e, a_=a_: e.activation(out=VT, in_=a_, func=AF.Silu), reads=[A_], writes=[VTB])
                            continue
                        f.op(ACT, lambda e, a_=a_: e.activation(out=a_, in_=a_, func=AF.Silu), reads=[A_], writes=[A_])
                        f.op(ACT, lambda e, a_=a_: e.activation(out=sq, in_=a_, func=AF.Square), reads=[A_], writes=[SQ])
                        for t in range(4):
                            bk, BK = next_bank()
                            f.op(PE, lambda e, bk=bk, t=t: e.matmul(bk, lhsT=onesb, rhs=sq[:, t * 512:(t + 1) * 512], start=True, stop=True), reads=[ONESB, SQ], writes=[BK])
                            f.op(ACT, lambda e, bk=bk, t=t: e.activation(out=rstd[:, t * 512:(t + 1) * 512], in_=bk, func=AF.Sqrt, bias=cst[:, 162:163]), reads=[BK, CST], writes=[RSTD])
                        f.op(DVE, lambda e: e.reciprocal(out=rstd, in_=rstd), reads=[RSTD], writes=[RSTD])
                        if i == 0:
                            f.op(DVE, lambda e, a_=a_: e.scalar_tensor_tensor(out=QT, in0=a_, scalar=128.0 ** -0.5, in1=rstd, op0=ALU.mult, op1=ALU.mult), reads=[A_, RSTD], writes=[QTB])
                        else:
                            f.op(DVE, lambda e, a_=a_: e.tensor_tensor(out=KT, in0=a_, in1=rstd, op=ALU.mult), reads=[A_, RSTD], writes=[KTB])
                            f.op(POOL, lambda e, a_=a_: e.tensor_tensor(out=KT32, in0=a_, in1=rstd, op=ALU.mult), reads=[A_, RSTD], writes=[KT32B])
                    for (src, SRCB, dst, DSTB) in ((KT, KTB, Ktok, KTOK), (VT, VTB, Vtok, VTOK_)):
                        for c4 in range(4):
                            bk, BK = next_bank()
                            for j in range(4):
                                c = c4 * 4 + j
                                f.op(PE, lambda e, bk=bk, src=src, c=c, j=j: e.transpose(bfv(bk)[:, j * 128:(j + 1) * 128], src[:, c * 128:(c + 1) * 128], identb),
                                     reads=[SRCB, IDENTB], writes=[BK])
                            f.op(DVE, lambda e, bk=bk, dst=dst, c4=c4: e.tensor_copy(out=dst[:, c4 * 4:(c4 + 1) * 4, :], in_=bfv(bk)[:, 0:512]), reads=[BK], writes=[DSTB])
                    for c4 in range(4):
                        bX, BX = banks[0]
                        bY, BY = banks[1]
                        for j in range(4):
                            c = c4 * 4 + j
                            cs = slice(c * 128, (c + 1) * 128)
                            f.op(PE, lambda e, cs=cs, j=j: e.matmul(bX[:, j * 128:(j + 1) * 128], lhsT=KT32[:, cs], rhs=KT32[:, cs], start=True, stop=True), reads=[KT32B], writes=[BX])
                            f.op(PE, lambda e, cs=cs, j=j: e.matmul(bY[:, j * 128:(j + 1) * 128], lhsT=KT[:, cs], rhs=QT[:, cs], start=True, stop=True), reads=[KTB, QTB], writes=[BY])
                        f.op(DVE, lambda e: e.tensor_copy(out=KKs, in_=bX.rearrange("p (a b) -> p a b", b=128)), reads=[BX], writes=[KKSB])
                        f.op(ACT, lambda e: e.activation(out=KQs, in_=bY.rearrange("p (a b) -> p a b", b=128), func=AF.Copy), reads=[BY], writes=[KQSB])
                        bset = [(banks[2], banks[3], banks[4]), (banks[5], banks[6], banks[7])]
                        DD = (0, 1)
                        for d_ in DD:
                            (bA, BA) = bset[d_][0]
                            dgm, DGM = dgm2[d_]
                            for j in range(4):
                                c = c4 * 4 + j
                                f.op(POOL, lambda e, h=h, c=c, j=j, d_=d_, dgm=dgm: e.tensor_scalar(out=dgm[:, j, :], in0=ident, scalar1=CG[:, d_, c, h:h + 1], scalar2=None, op0=ALU.mult),
                                     reads=[IDENT, CGB], writes=[DGM])
                                f.op(PE, lambda e, j=j, bA=bA, dgm=dgm: e.matmul(bA[:, j * 128:(j + 1) * 128], lhsT=onesf, rhs=dgm[:, j, :], start=True, stop=True),
                                     reads=[ONESF, DGM], writes=[BA])
                        for d_ in DD:
                            (bA, BA) = bset[d_][0]
                            t1, T1b = t12[d_]
                            t3, T3b = t32[d_]
                            gam1, GAM1 = gam12[d_]
                            for j in range(4):
                                c = c4 * 4 + j
                                f.op(DVE, lambda e, h=h, c=c, j=j, d_=d_, bA=bA, t1=t1: e.scalar_tensor_tensor(out=t1[:, j, :], in0=bA[:, j * 128:(j + 1) * 128], scalar=CG[:, d_, c, h:h + 1], in1=maskN[d_], op0=ALU.subtract, op1=ALU.max),
                                     reads=[BA, CGB, CST], writes=[T1b])
                                f.op(DVE, lambda e, h=h, c=c, j=j, d_=d_, bA=bA, t3=t3: e.scalar_tensor_tensor(out=t3[:, j, :], in0=bA[:, j * 128:(j + 1) * 128], scalar=CG[:, d_, c, h:h + 1], in1=maskQ[d_], op0=ALU.subtract, op1=ALU.min),
                                     reads=[BA, CGB, CST], writes=[T3b])
                            f.op(ACT, lambda e, t1=t1, gam1=gam1: e.activation(out=gam1, in_=t1, func=AF.Exp, scale=-1.0), reads=[T1b], writes=[GAM1])
                            f.op(ACT, lambda e, t3=t3: e.activation(out=t3, in_=t3, func=AF.Exp), reads=[T3b], writes=[T3b])
                        for d_ in DD:
                            t3, T3b = t32[d_]
                            gam1, GAM1 = gam12[d_]
                            Nm, NM = Nm2[d_]
                            for j in range(4):
                                c = c4 * 4 + j
                                f.op(DVE, lambda e, h=h, c=c, j=j, d_=d_, Nm=Nm, gam1=gam1: e.scalar_tensor_tensor(out=Nm[:, j, :], in0=KKs[:, j, :], scalar=NEGB[:, d_, c, h:h + 1], in1=gam1[:, j, :], op0=ALU.mult, op1=ALU.mult),
                                     reads=[KKSB, NEGBB, GAM1], writes=[NM])
                            f.op(POOL, lambda e, d_=d_, c4=c4, t3=t3: e.tensor_tensor(out=QKm[:, d_, c4 * 4:(c4 + 1) * 4, :], in0=KQs, in1=t3, op=ALU.mult),
                                 reads=[KQSB, T3b], writes=[QKB2[d_][c4]])
                        st = {}
                        for d_ in DD:
                            (bB, BB) = bset[d_][1]
                            Nm, NM = Nm2[d_]
                            for j in range(4):
                                f.op(PE, lambda e, j=j, bB=bB, Nm=Nm: e.transpose(bB[:, j * 128:(j + 1) * 128], Nm[:, j, :], ident), reads=[NM, IDENT], writes=[BB])
                            PT_, PTB_ = PaT2[d_][0]
                            X_, XB_ = Xa2[d_][0]
                            f.op(ACT, lambda e, PT_=PT_, bB=bB: e.activation(out=PT_, in_=bB.rearrange("p (a b) -> p a b", b=128), func=AF.Copy), reads=[BB], writes=[PTB_])
                            f.op(POOL, lambda e, X_=X_, PT_=PT_: e.tensor_tensor(out=X_, in0=PT_, in1=ident4, op=ALU.add), reads=[PTB_, ID4B], writes=[XB_])
                            st[d_] = [Nm, NM, PT_, PTB_, X_, XB_]
                        for lv in range(6):
                            last = (lv == 5)
                            nxt = {}
                            for d_ in DD:
                                (bA, BA), (bB, BB), (bC, BC) = bset[d_]
                                P_, PB_, PT_, PTB_, X_, XB_ = st[d_]
                                for j in range(4):
                                    f.op(PE, lambda e, j=j, P_=P_, PT_=PT_, bA=bA: e.matmul(bA[:, j * 128:(j + 1) * 128], lhsT=PT_[:, j, :], rhs=P_[:, j, :], start=True, stop=True), reads=[PB_, PTB_], writes=[BA])
                                if not last:
                                    for j in range(4):
                                        f.op(PE, lambda e, j=j, P_=P_, PT_=PT_, bB=bB: e.matmul(bB[:, j * 128:(j + 1) * 128], lhsT=P_[:, j, :], rhs=PT_[:, j, :], start=True, stop=True), reads=[PB_, PTB_], writes=[BB])
                            for d_ in DD:
                                (bA, BA), (bB, BB), (bC, BC) = bset[d_]
                                P2, P2B = Pa2[d_][lv % 2]
                                P2T, P2TB = PaT2[d_][(lv + 1) % 2]
                                f.op(ACT, lambda e, P2=P2, bA=bA: e.activation(out=P2, in_=bA.rearrange("p (a b) -> p a b", b=128), func=AF.Copy), reads=[BA], writes=[P2B])
                                if not last:
                                    f.op(DVE, lambda e, P2T=P2T, bB=bB: e.tensor_copy(out=P2T, in_=bB.rearrange("p (a b) -> p a b", b=128)), reads=[BB], writes=[P2TB])
                                nxt[d_] = [P2, P2B, P2T, P2TB]
                            for d_ in DD:
                                (bA, BA), (bB, BB), (bC, BC) = bset[d_]
                                P2, P2B, P2T, P2TB = nxt[d_]
                                X_, XB_ = st[d_][4], st[d_][5]
                                for j in range(4):
                                    f.op(PE, lambda e, j=j, P2=P2, X_=X_, bC=bC: e.matmul(bC[:, j * 128:(j + 1) * 128], lhsT=P2[:, j, :], rhs=X_[:, j, :], start=True, stop=True), reads=[P2B, XB_], writes=[BC])
                            for d_ in DD:
                                (bA, BA), (bB, BB), (bC, BC) = bset[d_]
                                P2, P2B, P2T, P2TB = nxt[d_]
                                X_, XB_ = st[d_][4], st[d_][5]
                                X2, X2B = Xa2[d_][(lv + 1) % 2]
                                f.op(DVE, lambda e, X2=X2, X_=X_, bC=bC: e.tensor_tensor(out=X2, in0=bC.rearrange("p (a b) -> p a b", b=128), in1=X_, op=ALU.add), reads=[BC, XB_], writes=[X2B])
                                st[d_] = [P2, P2B, P2T, P2TB, X2, X2B]
                        for d_ in DD:
                            X_, XB_ = st[d_][4], st[d_][5]
                            for j in range(4):
                                c = c4 * 4 + j
                                f.op(ACT, lambda e, h=h, c=c, j=j, d_=d_, X_=X_: e.activation(out=TiTb[:, d_, c, :], in_=X_[:, j, :], func=AF.Copy, scale=BETA[:, d_, c, h:h + 1]),
                                     reads=[XB_, BETAB], writes=[TIB[d_][c4]])
                    f.op(DVE, lambda e: e.memset(Oacc, 0.0), writes=[OACC])
                    for d_ in range(2):
                        f.op(DVE, lambda e, d_=d_: e.memset(Sst[d_][0], 0.0), writes=[Sst[d_][1]])
                        f.op(POOL, lambda e, d_=d_: e.memset(Sbf[d_][0], 0.0), writes=[Sbf[d_][1]])
                    for step in range(16):
                        for d_ in range(2):
                            c = step if d_ == 0 else 15 - step
                            cs = slice(c * 128, (c + 1) * 128)
                            c4 = c // 4
                            S_, SB_ = Sst[d_]
                            sb_, SBB_ = Sbf[d_]
                            r_, RB_ = Rb[d_]
                            v_, VB_ = Vn[d_]
                            bP, BP = banks[d_ * 3 + 0]
                            bQ, BQ = banks[d_ * 3 + 1]
                            bR, BR = banks[d_ * 3 + 2]
                            f.op(PE, lambda e, bP=bP, cs=cs, sb_=sb_: e.matmul(bP[:, 0:128], lhsT=KT[:, cs], rhs=sb_, start=True, stop=True), reads=[KTB, SBB_], writes=[BP])
                            f.op(DVE, lambda e, h=h, bP=bP, r_=r_, c=c, d_=d_: e.scalar_tensor_tensor(out=r_, in0=bP[:, 0:128], scalar=NEGG[:, d_, c, h:h + 1], in1=Vtok[:, c, :], op0=ALU.mult, op1=ALU.add),
                                 reads=[BP, NEGGB, VTOK_], writes=[RB_])
                            f.op(PE, lambda e, bP=bP, r_=r_, c=c, d_=d_: e.matmul(bP[:, 128:256], lhsT=TiTb[:, d_, c, :], rhs=r_, start=True, stop=True), reads=[TIB[d_][c4], RB_], writes=[BP])
                            f.op(ACT, lambda e, bP=bP, v_=v_: e.activation(out=v_[:, 0, :], in_=bP[:, 128:256], func=AF.Copy), reads=[BP], writes=[VB_])
                            f.op(ACT, lambda e, h=h, bP=bP, v_=v_, c=c, d_=d_: e.activation(out=v_[:, 1, :], in_=bP[:, 128:256], func=AF.Copy, scale=W2[:, d_, c, h:h + 1]), reads=[BP, W2B], writes=[VB_])
                            f.op(PE, lambda e, bQ=bQ, cs=cs, sb_=sb_: e.matmul(bQ[:, 0:128], lhsT=QT[:, cs], rhs=sb_, start=True, stop=True), reads=[QTB, SBB_], writes=[BQ])
                            f.op(PE, lambda e, bQ=bQ, v_=v_, c=c, d_=d_: e.matmul(bQ[:, 128:256], lhsT=QKm[:, d_, c, :], rhs=v_[:, 0, :], start=True, stop=True), reads=[QKB2[d_][c4], VB_], writes=[BQ])
                            f.op(DVE, lambda e, h=h, bQ=bQ, c=c, d_=d_: e.scalar_tensor_tensor(out=Oacc[:, c, :], in0=bQ[:, 0:128], scalar=GEX[:, d_, c, h:h + 1], in1=Oacc[:, c, :], op0=ALU.mult, op1=ALU.add),
                                 reads=[BQ, GEXB, OACC], writes=[OACC])
                            f.op(DVE, lambda e, bQ=bQ, c=c: e.tensor_tensor(out=Oacc[:, c, :], in0=bQ[:, 128:256], in1=Oacc[:, c, :], op=ALU.add), reads=[BQ, OACC], writes=[OACC])
                            f.op(PE, lambda e, bR=bR, v_=v_, c=c: e.matmul(bR[:, 0:128], lhsT=Ktok[:, c, :], rhs=v_[:, 1, :], start=True, stop=True), reads=[KTOK, VB_], writes=[BR])
                            f.op(DVE, lambda e, h=h, bR=bR, S_=S_, c=c, d_=d_: e.scalar_tensor_tensor(out=S_, in0=S_, scalar=DEC[:, d_, c, h:h + 1], in1=bR[:, 0:128], op0=ALU.mult, op1=ALU.add),
                                 reads=[BR, DECB, SB_], writes=[SB_])
                            f.op(ACT, lambda e, S_=S_, sb_=sb_: e.activation(out=sb_, in_=S_, func=AF.Copy), reads=[SB_], writes=[SBB_])
                    if False:
                        dbg("gam1", gam1, GAM1, [4, 128], BF16); dbg("gam3", t3, T3b, [4, 128], F32); dbg("Nm", Nm, NM, [4, 128], BF16)
                        dbg("QT", QT, QTB, [S], BF16); dbg("KT", KT, KTB, [S], BF16); dbg("VT", VT, VTB, [S], BF16)
                        dbg("Oacc", Oacc, OACC, [16, 128], F32)
                        dbg("CG", CGf, CGB, [256], F32); dbg("W2", W2f, W2B, [256], F32); dbg("GEX", GEXf, GEXB, [256], F32)
                        dbg("DEC", DECf, DECB, [256], F32); dbg("BETA", BETAf, BETAB, [256], F32); dbg("TOT", TOTf, TOTB, [256], F32)
                        dbg("G", GT_f, GTB, [256], F32)
                        dbg("TiTb0", TiTb[:, 0, 0:4, :], TIB[0][0], [4, 128], BF16); dbg("TiTb1", TiTb[:, 1, 12:16, :], TIB[1][3], [4, 128], BF16)
                        dbg("QKm0", QKm[:, 0, 0:4, :], QKB2[0][0], [4, 128], BF16); dbg("QKm1", QKm[:, 1, 12:16, :], QKB2[1][3], [4, 128], BF16)
                        dbg("Ktok", Ktok, KTOK, [16, 128], BF16); dbg("Vtok", Vtok, VTOK_, [16, 128], BF16)
                    for c in range(16):
                        f.op(ACT, lambda e, c=c: e.activation(out=onb[:, c, :], in_=Oacc[:, c, :], func=AF.Square, accum_out=osq[:, c, 0:1]), reads=[OACC], writes=[ONB, OSQ])
                    f.op(DVE, lambda e: e.tensor_scalar(out=osq[:, :, 1:2], in0=osq[:, :, 0:1], scalar1=1.0 / 128, scalar2=EPS, op0=ALU.mult, op1=ALU.add), reads=[OSQ], writes=[OSQ])
                    f.op(ACT, lambda e: e.activation(out=osq[:, :, 2:3], in_=osq[:, :, 1:2], func=AF.Sqrt), reads=[OSQ], writes=[OSQ])
                    f.op(DVE, lambda e: e.reciprocal(out=osq[:, :, 3:4], in_=osq[:, :, 2:3]), reads=[OSQ], writes=[OSQ])
                    for c in range(16):
                        f.op(DVE, lambda e, c=c: e.scalar_tensor_tensor(out=onb[:, c, :], in0=Oacc[:, c, :], scalar=osq[:, c, 3:4], in1=dnw, op0=ALU.mult, op1=ALU.mult),
                             reads=[OACC, OSQ, DNW], writes=[ONB])
                    for c4 in range(4):
                        bk, BK = banks[7]
                        for j in range(4):
                            c = c4 * 4 + j
                            f.op(PE, lambda e, bk=bk, c=c, j=j: e.transpose(bfv(bk)[:, j * 128:(j + 1) * 128], onb[:, c, :], identb), reads=[ONB, IDENTB], writes=[BK])
                        f.op(DVE, lambda e, bk=bk, c4=c4: e.tensor_tensor(out=mo2[:, c4 * 512:(c4 + 1) * 512], in0=bfv(bk)[:, 0:512], in1=zs[:, c4 * 512:(c4 + 1) * 512], op=ALU.mult),
                             reads=[BK, ZS], writes=[MO2])
                    f.dma(POOL, mixT_d.ap()[h * 128:(h + 1) * 128, c0:c0 + S], mo2, reads=[MO2], writes=[MIXT])
        else:
            z_, Z_ = ar.alloc("zeros", [NT], BF16)
            f.op(DVE, lambda e: e.memset(z_, 0.0), writes=[Z_])
            for c in range(8):
                f.dma(SP, mixT_d.ap()[c * 128:(c + 1) * 128, :], z_, reads=[Z_], writes=[MIXT])
        f.barrier()
        ar.reset(m1)
        cstb, CSTB = ar.alloc("cstb", [NCST], BF16)
        f.op(DVE, lambda e: e.tensor_copy(out=cstb, in_=cst), reads=[CST], writes=[CSTB])
        Rm = cstb[0:32, 128:160]
        maskA, maskB, maskAc = cstb[:, 192:320], cstb[:, 320:448], cstb[0:64, 448:576]
        onesb, ONESB = ar.alloc("onesb", [128], BF16)
        f.op(DVE, lambda e: e.memset(onesb, 1.0), writes=[ONESB])
        posi, POSI = ar.alloc("posi", [S], I32)
        ang, ANG = ar.alloc("ang", [S], F32)
        tmpa, TMPA = ar.alloc("tmpa", [S], F32)
        tmpi, TMPI = posi, POSI
        cosT, COS = ar.alloc("cosT", [S], F32)
        sinT, SIN = ar.alloc("sinT", [S], F32)
        qk = [[[ar.alloc("qk%d_%d_%d" % (sl, g, i), [S], BF16, buf=False) for i in range(2)] for g in range(3)] for sl in range(2)]
        QKB = [[[[f.buf(qk[sl][g][i][:, t * 512:(t + 1) * 512], "qkb") for t in range(4)] for i in range(2)] for g in range(3)] for sl in range(2)]
        DIL = [1, 4, 16]
        vh = [[ar.alloc("vh%d_%d" % (sl, g), [16, 128], BF16) for g in range(3)] for sl in range(1)] * 2
        v0 = [[ar.alloc("v0%d_%d" % (sl, g), [16, 128], BF16) for g in range(3)] for sl in range(1)] * 2
        accn, ACCN = ar.alloc("accn", [S], F32)
        accd, ACCD = ar.alloc("accd", [S], F32)
        mo = [ar.alloc("mo%d" % i, [S], BF16) for i in range(2)]
        rt = [ar.alloc("rt%d" % i, [2, 512], F32) for i in range(2)]
        pb = [ar.alloc("pb%d" % i, [2, 128], BF16) for i in range(4)]
        PI = float(np.pi)
        pit, PIT = ar.alloc("pit", [S], F32)
        f.op(DVE, lambda e: e.memset(pit[0:32, :], PI), writes=[PIT])
        it = 0
        for s_i in range(NS):
            f.dma(SP, posi[0:32, :], pos_d.ap()[s_i:s_i + 1, :].to_broadcast([32, S]), reads=[SMALL], writes=[POSI])
            f.op(DVE, lambda e: e.tensor_copy(out=ang[0:32, :], in_=posi[0:32, :]), reads=[POSI], writes=[ANG])
            f.op(DVE, lambda e: e.tensor_scalar(out=ang[0:32, :], in0=ang[0:32, :], scalar1=cst[0:32, 160:161], scalar2=None, op0=ALU.mult), reads=[ANG, CST], writes=[ANG])
            f.op(DVE, lambda e: e.tensor_scalar(out=tmpi[0:32, :], in0=ang[0:32, :], scalar1=1.0 / (2 * PI), scalar2=None, op0=ALU.mult), reads=[ANG], writes=[TMPI])
            f.op(DVE, lambda e: e.tensor_copy(out=tmpa[0:32, :], in_=tmpi[0:32, :]), reads=[TMPI], writes=[TMPA])
            f.op(DVE, lambda e: e.scalar_tensor_tensor(out=ang[0:32, :], in0=tmpa[0:32, :], scalar=-2 * PI, in1=ang[0:32, :], op0=ALU.mult, op1=ALU.add), reads=[TMPA, ANG], writes=[ANG])
            f.op(DVE, lambda e: e.tensor_tensor(out=tmpa[0:32, :], in0=ang[0:32, :], in1=pit[0:32, :], op=ALU.is_gt), reads=[ANG, PIT], writes=[TMPA])
            f.op(DVE, lambda e: e.scalar_tensor_tensor(out=ang[0:32, :], in0=tmpa[0:32, :], scalar=-2 * PI, in1=ang[0:32, :], op0=ALU.mult, op1=ALU.add), reads=[ANG, TMPA], writes=[ANG])
            f.op(ACT, lambda e: e.activation(out=sinT[0:32, :], in_=ang[0:32, :], func=AF.Sin), reads=[ANG], writes=[SIN])
            f.op(DVE, lambda e: e.scalar_tensor_tensor(out=tmpa[0:32, :], in0=ang[0:32, :], scalar=-1.0, in1=ang[0:32, :], op0=ALU.mult, op1=ALU.max), reads=[ANG], writes=[TMPA])
            f.op(ACT, lambda e: e.activation(out=cosT[0:32, :], in_=tmpa[0:32, :], func=AF.Sin, scale=-1.0, bias=cst[0:32, 161:162]), reads=[TMPA, CST], writes=[COS])
            for h in range(8):
                sl = it % 2
                it += 1
                c0 = s_i * S
                for g in range(3):
                    for i in range(2):
                        ch = 32 + i * 24 + g * 8 + h
                        for t in range(4):
                            f.dma(SP, qk[sl][g][i][:, t * 512:(t + 1) * 512], projT_d.ap()[ch * 128:(ch + 1) * 128, c0 + t * 512:c0 + (t + 1) * 512],
                                  reads=[PROJT], writes=[QKB[sl][g][i][t]])
                    d = DIL[g]
                    L = S // d
                    M = L // 128
                    vcol = g * 1024 + h * 128
                    view = vtok_d.ap()[c0:c0 + S, vcol:vcol + 128].rearrange("(i d) c -> d i c", d=d)
                    vh_, VH_ = vh[sl][g]
                    v0_, V0_ = v0[sl][g]
                    vhv = vh_.rearrange("p (r m) c -> p r m c", m=M)
                    for r in range(d):
                        if M > 1:
                            f.dma(SP, vhv[:, r, 0:M - 1, :], view[r, 64:L - 64, :].rearrange("(m j) c -> j m c", j=128), reads=[VTOK], writes=[VH_])
                    f.dma(SP, vhv[0:64, :, M - 1, :], view[:, L - 64:L, :].rearrange("r j c -> j r c"), reads=[VTOK], writes=[VH_])
                    f.dma(SP, v0_[0:64, 0:d, :], view[:, 0:64, :].rearrange("r j c -> j r c"), reads=[VTOK], writes=[V0_])
                for g in range(3):
                    for i in range(2):
                        X = qk[sl][g][i]
                        for t in range(4):
                            XB = QKB[sl][g][i][t]
                            cs = slice(t * 512, (t + 1) * 512)
                            bk, BK = next_bank(0, 4)
                            r_, R_ = rt[(g * 8 + i * 4 + t) % 2]
                            f.op(PE, lambda e, bk=bk, X=X, cs=cs: e.matmul(bk[0:32, :], lhsT=Rm, rhs=X[0:32, cs], start=True, stop=True), reads=[CSTB, XB], writes=[BK])
                            f.op(DVE, lambda e, r_=r_, X=X, cs=cs: e.tensor_tensor(out=r_[0:32, 0, :], in0=X[0:32, cs], in1=cosT[0:32, cs], op=ALU.mult), reads=[XB, COS], writes=[R_])
                            f.op(DVE, lambda e, r_=r_, bk=bk, cs=cs: e.tensor_tensor(out=r_[0:32, 1, :], in0=bk[0:32, :], in1=sinT[0:32, cs], op=ALU.mult), reads=[BK, SIN, R_], writes=[R_])
                            f.op(DVE, lambda e, r_=r_, X=X, cs=cs: e.tensor_tensor(out=X[0:32, cs], in0=r_[0:32, 0, :], in1=r_[0:32, 1, :], op=ALU.add), reads=[R_], writes=[XB])
                un = 0
                for g in range(3):
                    d = DIL[g]
                    L = S // d
                    M = L // 128
                    Q, K = qk[sl][g][0], qk[sl][g][1]
                    QB_, KB_ = QKB[sl][g][0], QKB[sl][g][1]
                    vh_, VH_ = vh[sl][g]
                    v0_, V0_ = v0[sl][g]
                    vhv = vh_.rearrange("p (r m) c -> p r m c", m=M)
                    units = [(r, qb) for r in range(d) for qb in range(M)]
                    for u0 in range(0, len(units), 4):
                        nb_, NB_ = banks[4 + (un // 4) % 2]
                        db_, DB_ = banks[6 + (un // 4) % 2]
                        for j in range(4):
                            r, qb = units[u0 + j]
                            un += 1
                            sb_, SB_ = banks[un % 4]
                            p_, P_ = pb[un % 4]
                            qsl = slice(r + d * 128 * qb, r + d * 128 * qb + d * 127 + 1, d)
                            blocks = []
                            if qb >= 1:
                                blocks.append((128 * qb - 64, 128, maskA, vhv[:, r, qb - 1, :], VH_))
                            else:
                                blocks.append((0, 64, maskAc, v0_[0:64, r, :], V0_))
                            if qb < M - 1:
                                blocks.append((128 * qb + 64, 128, maskB, vhv[:, r, qb, :], VH_))
                            else:
                                blocks.append((128 * qb + 64, 64, maskB[0:64, :], vhv[0:64, r, qb, :], VH_))
                            for bi, (k0, nk, mk, vap, VB_) in enumerate(blocks):
                                ksl = slice(r + d * k0, r + d * k0 + d * (nk - 1) + 1, d)
                                f.op(PE, lambda e, sb_=sb_, K=K, Q=Q, ksl=ksl, qsl=qsl, nk=nk, bi=bi: e.matmul(
                                    sb_[0:nk, bi * 128:(bi + 1) * 128], lhsT=K[:, ksl], rhs=Q[:, qsl], start=True, stop=True),
                                    reads=KB_ + QB_, writes=[SB_])
                                f.op(ACT, lambda e, sb_=sb_, p_=p_, nk=nk, bi=bi: e.activation(
                                    out=p_[0:nk, bi, :], in_=sb_[0:nk, bi * 128:(bi + 1) * 128], func=AF.Exp, scale=128.0 ** -0.5),
                                    reads=[SB_], writes=[P_])
                                f.op(POOL, lambda e, p_=p_, nk=nk, bi=bi, mk=mk: e.tensor_tensor(out=p_[0:nk, bi, :], in0=p_[0:nk, bi, :], in1=mk, op=ALU.mult),
                                     reads=[P_, CSTB], writes=[P_])
                            for bi, (k0, nk, mk, vap, VB_) in enumerate(blocks):
                                f.op(PE, lambda e, nb_=nb_, p_=p_, vap=vap, nk=nk, bi=bi, j=j: e.matmul(
                                    nb_[:, j * 128:(j + 1) * 128], lhsT=vap, rhs=p_[0:nk, bi, :], start=(bi == 0), stop=(bi == 1)),
                                    reads=[P_, VB_], writes=[NB_])
                            for bi, (k0, nk, mk, vap, VB_) in enumerate(blocks):
                                f.op(PE, lambda e, db_=db_, p_=p_, nk=nk, bi=bi, j=j: e.matmul(
                                    db_[:, j * 128:(j + 1) * 128], lhsT=onesb[0:nk, :], rhs=p_[0:nk, bi, :], start=(bi == 0), stop=(bi == 1)),
                                    reads=[P_, ONESB], writes=[DB_])
                        r, qb = units[u0]
                        if g == 0:
                            sl_ = slice(qb * 128, qb * 128 + 512)
                            f.op(DVE, lambda e, nb_=nb_, sl_=sl_: e.tensor_copy(out=accn[:, sl_], in_=nb_), reads=[NB_], writes=[ACCN])
                            f.op(DVE, lambda e, db_=db_, sl_=sl_: e.tensor_copy(out=accd[:, sl_], in_=db_), reads=[DB_], writes=[ACCD])
                        else:
                            if g == 1:
                                av = lambda a, r=r: a[:, r:r + 4 * 511 + 1:4]
                                iv = lambda b: b
                            else:
                                av = lambda a, r=r: a.rearrange("p (i r) -> p r i", r=16)[:, r:r + 4, :]
                                iv = lambda b: b.rearrange("p (r i) -> p r i", r=4)
                            f.op(DVE, lambda e, nb_=nb_, av=av, iv=iv: e.tensor_tensor(out=av(accn), in0=iv(nb_), in1=av(accn), op=ALU.add), reads=[NB_, ACCN], writes=[ACCN])
                            f.op(DVE, lambda e, db_=db_, av=av, iv=iv: e.tensor_tensor(out=av(accd), in0=iv(db_), in1=av(accd), op=ALU.add), reads=[DB_, ACCD], writes=[ACCD])
                m_, M_ = mo[sl]
                f.op(DVE, lambda e: e.reciprocal(out=accd, in_=accd), reads=[ACCD], writes=[ACCD])
                f.op(DVE, lambda e, m_=m_: e.tensor_tensor(out=m_, in0=accn, in1=accd, op=ALU.mult), reads=[ACCN, ACCD], writes=[M_])
                f.dma(POOL, mixT_d.ap()[(8 + h) * 128:(9 + h) * 128, c0:c0 + S], m_, reads=[M_], writes=[MIXT])

    f.barrier()
    ar.reset(m1)
    wo, WO = ar.alloc("wo", [16, D], BF16)
    for q in range(4):
        f.dma(SP, wo[:, q * 4:(q + 1) * 4, :], woutb_d.ap()[q * 512:(q + 1) * 512, :].rearrange("(k p) n -> p k n", p=128), reads=[WOUTB], writes=[WO])
    mx = [ar.alloc("mx%d" % i, [16, 512], BF16) for i in range(2)]
    xr = [ar.alloc("xr%d" % i, [D], F32) for i in range(2)]
    hb = [ar.alloc("hb%d" % i, [D], F32) for i in range(2)]
    for tile in range(NT // 512):
        t0 = tile * 512
        m_, M_ = mx[tile % 2]
        f.dma(SP, m_, mixT_d.ap()[:, t0:t0 + 512].rearrange("(k p) t -> p k t", p=128), reads=[MIXT], writes=[M_])
        for b in range(4):
            r0 = t0 + b * 128
            x_, X_ = xr[b % 2]
            h_, H_ = hb[b % 2]
            f.dma(SP, x_, x_d.ap()[r0:r0 + 128, :], reads=[Xd], writes=[X_])
            for ds in range(4):
                bk, BK = next_bank(2, 8)
                for kc in range(16):
                    f.op(PE, lambda e, bk=bk, m_=m_, kc=kc, b=b, ds=ds: e.matmul(
                        bk, lhsT=m_[:, kc, b * 128:(b + 1) * 128], rhs=wo[:, kc, ds * 512:(ds + 1) * 512], start=(kc == 0), stop=(kc == 15)),
                        reads=[M_, WO], writes=[BK])
                f.op(DVE, lambda e, bk=bk, h_=h_, x_=x_, ds=ds: e.tensor_tensor(
                    out=h_[:, ds * 512:(ds + 1) * 512], in0=bk, in1=x_[:, ds * 512:(ds + 1) * 512], op=ALU.add),
                    reads=[BK, X_], writes=[H_])
            f.dma(POOL, h1_d.ap()[r0:r0 + 128, :], h_, reads=[H_], writes=[H1])

    if upto <= 3:
        f.barrier(); f.emit(); f.close()
        return nc
    f.barrier()
    ar.reset(m1)
    TT4 = 512
    norm4 = make_norm("n4", 2)
    n2T = ar.alloc("n2T", [16, TT4 + 2], BF16, buf=False)
    N2B = [f.buf(n2T[:, kc, :], "n2T_%d" % kc) for kc in range(16)]
    actT = ar.alloc("actT", [NFC, TT4], BF16, buf=False)
    ACTB = [f.buf(actT[:, c, :], "actT_%d" % c) for c in range(NFC)]
    wu = [ar.alloc("wu%d" % i, [16, 2, 256], BF16) for i in range(2)]
    wd = [ar.alloc("wd%d" % i, [11, 512], BF16) for i in range(2)]
    ug = [ar.alloc("ug%d" % i, [TT4 + 2], F32) for i in range(4)]
    ag = [ar.alloc("ag%d" % i, [TT4], F32) for i in range(4)]
    hq = [ar.alloc("hq%d" % i, [512], F32) for i in range(2)]
    oq = [ar.alloc("oq%d" % i, [512], F32) for i in range(2)]
    halo_rr = [0]
    wu_i = 0
    wd_i = 0
    oq_i = 0
    for tile in range(NT // TT4):
        t0 = tile * TT4
        tl = t0 % S
        blocks = [(h1_d.ap()[t0 + b * 128:t0 + (b + 1) * 128, :], 128, 1 + b * 128, 0, H1) for b in range(TT4 // 128)]
        left = h1_d.ap()[t0 - 1:t0, :] if tl > 0 else None
        right = h1_d.ap()[t0 + TT4:t0 + TT4 + 1, :] if tl + TT4 < S else None
        blocks.append(([left, right], 2, 0, TT4 + 1, H1))
        norm4(blocks, nfw, NFW, n2T, N2B)
        for slab in range(NFC // 2):
            w_, W_ = wu[wu_i % 2]
            wu_i += 1
            f.dma(SP, w_.rearrange("p k g n -> p (k g n)"), wfit_d.ap()[slab], reads=[WFIT], writes=[W_])
            for pj in range(2):
                fc = slab * 2 + pj
                accs = []
                for gv in range(2):
                    bk, BK = next_bank(2, 6)
                    hk, HK = banks[6 + halo_rr[0] % 2]
                    halo_rr[0] += 1
                    for kc in range(16):
                        f.op(PE, lambda e, bk=bk, w_=w_, kc=kc, gv=gv, pj=pj: e.matmul(
                            bk, lhsT=w_[:, kc, gv, pj * 128:(pj + 1) * 128], rhs=n2T[:, kc, 1:TT4 + 1], start=(kc == 0), stop=(kc == 15)),
                            reads=[W_, N2B[kc]], writes=[BK])
                    for kc in range(16):
                        f.op(PE, lambda e, hk=hk, w_=w_, kc=kc, gv=gv, pj=pj: e.matmul(
                            hk[:, 0:2], lhsT=w_[:, kc, gv, pj * 128:(pj + 1) * 128], rhs=n2T[:, kc, 0:TT4 + 2:TT4 + 1], start=(kc == 0), stop=(kc == 15)),
                            reads=[W_, N2B[kc]], writes=[HK])
                    u_, U_ = ug[(fc * 2 + gv) % 4]
                    a_, A_ = ag[(fc * 2 + gv) % 4]
                    f.op(ACT, lambda e, u_=u_, bk=bk: e.activation(out=u_[:, 1:TT4 + 1], in_=bk, func=AF.Copy), reads=[BK], writes=[U_])
                    f.op(ACT, lambda e, u_=u_, hk=hk: e.activation(out=u_[:, 0:TT4 + 2:TT4 + 1], in_=hk[:, 0:2], func=AF.Copy), reads=[HK], writes=[U_])
                    ch = gv * NFC + fc
                    f.op(DVE, lambda e, u_=u_, a_=a_, ch=ch: e.tensor_scalar(out=a_, in0=u_[:, 1:TT4 + 1], scalar1=cffn[1][0][:, ch:ch + 1], scalar2=None, op0=ALU.mult),
                         reads=[U_, cffn[1][1]], writes=[A_])
                    f.op(DVE, lambda e, u_=u_, a_=a_, ch=ch: e.scalar_tensor_tensor(out=a_, in0=u_[:, 0:TT4], scalar=cffn[0][0][:, ch:ch + 1], in1=a_, op0=ALU.mult, op1=ALU.add),
                         reads=[U_, cffn[0][1], A_], writes=[A_])
                    f.op(DVE, lambda e, u_=u_, a_=a_, ch=ch: e.scalar_tensor_tensor(out=a_, in0=u_[:, 2:TT4 + 2], scalar=cffn[2][0][:, ch:ch + 1], in1=a_, op0=ALU.mult, op1=ALU.add),
                         reads=[U_, cffn[2][1], A_], writes=[A_])
                    accs.append((a_, A_))
                (a_g, A_G), (a_v, A_V) = accs
                f.op(ACT, lambda e, a_g=a_g: e.activation(out=a_g, in_=a_g, func=AF.Silu), reads=[A_G], writes=[A_G])
                f.op(DVE, lambda e, a_g=a_g, a_v=a_v, fc=fc: e.tensor_tensor(out=actT[:, fc, :], in0=a_g, in1=a_v, op=ALU.mult),
                     reads=[A_G, A_V], writes=[ACTB[fc]])
        for ds in range(4):
            obanks = [banks[2 + b] for b in range(4)]
            for q in range(4):
                w_, W_ = wd[wd_i % 2]
                wd_i += 1
                f.dma(SP, w_.rearrange("p k n -> p (k n)"), wfot_d.ap()[ds * 4 + q], reads=[WFOT], writes=[W_])
                for b in range(4):
                    bk, BK = obanks[b]
                    for i in range(11):
                        fc = q * 11 + i
                        f.op(PE, lambda e, bk=bk, w_=w_, i=i, fc=fc, b=b, q=q: e.matmul(
                            bk, lhsT=actT[:, fc, b * 128:(b + 1) * 128], rhs=w_[:, i, :], start=(q == 0 and i == 0), stop=(q == 3 and i == 10)),
                            reads=[W_, ACTB[fc]], writes=[BK])
            for b in range(4):
                bk, BK = obanks[b]
                r0 = t0 + b * 128
                h_, H_ = hq[oq_i % 2]
                o_, O_ = oq[oq_i % 2]
                oq_i += 1
                f.dma(SP, h_, h1_d.ap()[r0:r0 + 128, ds * 512:(ds + 1) * 512], reads=[H1], writes=[H_])
                f.op(DVE, lambda e, bk=bk, h_=h_, o_=o_: e.tensor_tensor(out=o_, in0=bk, in1=h_, op=ALU.add), reads=[BK, H_], writes=[O_])
                f.dma(POOL, out_d.ap()[r0:r0 + 128, ds * 512:(ds + 1) * 512], o_, reads=[O_], writes=[OUT])

    if upto <= 4:
        f.barrier(); f.emit(); f.close()
        return nc
    f.barrier()
    ar.reset(m1)
    fx = [ar.alloc("fx%d" % i, [D], F32) for i in range(3)]
    fy = [ar.alloc("fy%d" % i, [D], F32) for i in range(2)]
    fj, FJ = ar.alloc("fj", [D], BF16)
    fs = [ar.alloc("fs%d" % i, [4], F32) for i in range(2)]
    for b in range(NT // 128):
        r0 = b * 128
        x_, X_ = fx[b % 3]
        y_, Y_ = fy[b % 2]
        s_, S_ = fs[b % 2]
        f.dma(SP, x_, out_d.ap()[r0:r0 + 128, :], reads=[OUT], writes=[X_])
        f.op(ACT, lambda e, x_=x_, s_=s_: e.activation(out=fj, in_=x_, func=AF.Square, accum_out=s_[:, 0:1]), reads=[X_], writes=[FJ, S_])
        f.op(DVE, lambda e, s_=s_: e.tensor_scalar(out=s_[:, 1:2], in0=s_[:, 0:1], scalar1=1.0 / D, scalar2=EPS, op0=ALU.mult, op1=ALU.add), reads=[S_], writes=[S_])
        f.op(ACT, lambda e, s_=s_: e.activation(out=s_[:, 2:3], in_=s_[:, 1:2], func=AF.Sqrt), reads=[S_], writes=[S_])
        f.op(DVE, lambda e, s_=s_: e.reciprocal(out=s_[:, 3:4], in_=s_[:, 2:3]), reads=[S_], writes=[S_])
        f.op(DVE, lambda e, s_=s_, x_=x_, y_=y_: e.scalar_tensor_tensor(out=y_, in0=x_, scalar=s_[:, 3:4], in1=nfin, op0=ALU.mult, op1=ALU.mult),
             reads=[S_, X_, NFIN], writes=[Y_])
        f.dma(SP, out_d.ap()[r0:r0 + 128, :], y_, reads=[Y_], writes=[OUT])
    f.barrier()
    f.emit()
    f.close()
    return nc


_NC_CACHE = {}


def _get_nc(debug=False, stub_mixers=False):
    key = (debug, stub_mixers)
    if key not in _NC_CACHE:
        _NC_CACHE[key] = build(debug=debug, stub_mixers=stub_mixers)
    return _NC_CACHE[key]


def make_in_maps(inputs, ncores=8):
    g = lambda k: np.ascontiguousarray(np.asarray(inputs[k]))
    x = g("x").astype(np.float32, copy=False)
    pos = g("positions").astype(np.int32, copy=False)
    shared = {
        "norm_mix_w": g("norm_mix_w").reshape(16, 128),
        "w_in": g("w_in").reshape(D, PW),
        "conv_qkv_w": g("conv_qkv_w").reshape(5, 24, 128),
        "a_log_fwd": g("a_log_fwd").reshape(1, 8), "a_log_bwd": g("a_log_bwd").reshape(1, 8),
        "dt_bias_fwd": g("dt_bias_fwd").reshape(1, 8), "dt_bias_bwd": g("dt_bias_bwd").reshape(1, 8),
        "delta_norm_w": g("delta_norm_w").reshape(1, 128),
        "w_out": g("w_out").reshape(D, D),
        "norm_ffn_w": g("norm_ffn_w").reshape(16, 128),
        "w_ffn_in": g("w_ffn_in").reshape(D, 2 * DFF),
        "conv_ffn_w": g("conv_ffn_w").reshape(3, 88, 128),
        "w_ffn_out": g("w_ffn_out").reshape(DFF, D),
        "norm_final_w": g("norm_final_w").reshape(1, D),
        "cst": make_cst(),
    }
    shared = {k: np.ascontiguousarray(v, dtype=np.float32) for k, v in shared.items()}
    maps = []
    for c in range(ncores):
        m = dict(shared)
        m["x"] = np.ascontiguousarray(x[c * NS:(c + 1) * NS].reshape(NT, D))
        m["pos"] = np.ascontiguousarray(pos[c * NS:(c + 1) * NS])
        maps.append(m)
    return maps


def kernel(**inputs):
    nc = _get_nc()
    maps = make_in_maps(inputs)
    res = run_bass_kernel_spmd(nc, maps, core_ids=list(range(8)))
    out = np.concatenate([np.asarray(r["out"]).reshape(NS, S, D) for r in res.results], axis=0)
    return out.astype(np.float32, copy=False)
```

```python
import contextlib
import numpy as np
import concourse.bass as bass
import concourse.mybir as mybir
from concourse.bass_utils import run_bass_kernel_spmd

F32 = mybir.dt.float32
BF16 = mybir.dt.bfloat16
I32 = mybir.dt.int32
AF = mybir.ActivationFunctionType
ALU = mybir.AluOpType

PE, ACT, DVE, POOL, SP = "tensor", "scalar", "vector", "gpsimd", "sync"
ENGS = (PE, ACT, DVE, POOL, SP)

S = 2048
D = 2048
NS = 2
NT = NS * S
PW = 13344
DFF = 5632
NFC = DFF // 128
EPS = 1e-6
COL_Z = 3072
COL_G = 4096
COL_BQ = 4128
COL_BK = 7200
COL_BV = 10272
NFEAT = 80
NCST = 1216


def make_cst():
    c = np.zeros((128, NCST), np.float32)
    c[:, 0:128] = np.eye(128)
    for p in range(16):
        c[p + 16, 128 + p] = -1.0
        c[p, 128 + 16 + p] = 1.0
    inv = 500000.0 ** (-np.arange(0, 32, 2, dtype=np.float32) / 32.0)
    c[0:16, 160] = inv
    c[16:32, 160] = inv
    c[:, 161] = np.pi / 2
    c[:, 162] = EPS
    k = np.arange(128)[:, None]
    q = np.arange(128)[None, :]
    c[:, 192:320] = (k >= q)
    c[:, 320:448] = (k <= q)
    c[0:64, 448:576] = (k[0:64] + 64 >= q)
    BIG = 30000.0
    c[:, 576:704] = np.where(q < k, 0.0, BIG)
    c[:, 704:832] = np.where(q >= k, 0.0, -BIG)
    c[:, 832:960] = np.where(q > k, 0.0, BIG)
    c[:, 960:1088] = np.where(q <= k, 0.0, -BIG)
    c[:, 1088:1216] = (k <= q)
    return c


class Buf:
    def __init__(self, ap, name, dram=False):
        self.ap = ap
        self.name = name
        self.dram = dram
        self.w = {}
        self.r = {}
        self.dsem = None
        self.dcnt = 0


class FW:
    def __init__(self, nc):
        self.nc = nc
        self.es = contextlib.ExitStack()
        self.sem = {}
        self.cnt = {e: 0 for e in ENGS}
        self.ops = {e: [] for e in ENGS}
        self.waited = {e: {} for e in ENGS}
        self.allsems = {}
        for e in (PE, ACT, DVE, POOL):
            self.sem[e] = self.new_sem("e_" + e)
            self.allsems[e] = [self.sem[e], 0]
        self.nbuf = 0
        self.dfree = []
        self.dbufs = []
        self.nds = 0

    def get_dsem(self):
        if self.dfree:
            return self.dfree.pop()
        self.nds += 1
        key = "ds%d" % self.nds
        ent = [self.new_sem(key), 0, key]
        self.allsems[key] = ent
        return ent

    def new_sem(self, name):
        return self.es.enter_context(self.nc.semaphore(name))

    def sbuf(self, name, shape, dtype):
        return self.es.enter_context(self.nc.sbuf_tensor(name, list(shape), dtype))

    def psum(self, name, shape, dtype):
        return self.es.enter_context(self.nc.psum_tensor(name, list(shape), dtype))

    def buf(self, ap, name=None, dram=False):
        self.nbuf += 1
        return Buf(ap, (name or "b") + "_%d" % self.nbuf, dram=dram)

    def _deps(self, eng, reads, writes, skip_key=None):
        need = {}

        def add(key, sem, val):
            if key == skip_key:
                return
            if key not in need or need[key][1] < val:
                need[key] = (sem, val)

        for b in reads:
            for key, (sem, val) in b.w.items():
                add(key, sem, val)
        for b in writes:
            for key, (sem, val) in b.w.items():
                if eng == PE and key == PE:
                    continue
                add(key, sem, val)
            for key, (sem, val) in b.r.items():
                add(key, sem, val)
        out = []
        wd = self.waited[eng]
        for key, (sem, val) in need.items():
            if wd.get(key, 0) >= val:
                continue
            wd[key] = val
            out.append((sem, val))
        return out

    def op(self, eng, fn, reads=(), writes=()):
        waits = self._deps(eng, reads, writes)
        self.cnt[eng] += 1
        c = self.cnt[eng]
        sem = self.sem[eng]
        self.allsems[eng][1] = c
        self.ops[eng].append((waits, fn, sem, 1))
        for b in reads:
            b.r[eng] = (sem, c)
        for b in writes:
            b.w = {eng: (sem, c)}
            b.r = {}

    def dma(self, q, out_ap, in_ap, reads=(), writes=(), **kw):
        owner = None
        for b in list(writes) + list(reads):
            if not b.dram:
                owner = b
                break
        if owner is None:
            owner = writes[0]
        if owner.dsem is None:
            owner.dsem = self.get_dsem()
            self.dbufs.append(owner)
        ent = owner.dsem
        key = ent[2]
        waits = self._deps(q, reads, writes, skip_key=key)
        ent[1] += 16
        sem, c = ent[0], ent[1]

        def fn(e, out_ap=out_ap, in_ap=in_ap, kw=kw):
            return e.dma_start(out=out_ap, in_=in_ap, **kw)

        self.ops[q].append((waits, fn, sem, 16))
        for b in reads:
            b.r[key] = (sem, c)
        for b in writes:
            if b.dram:
                b.w[key] = (sem, c)
            else:
                b.w = {key: (sem, c)}
                b.r = {}

    def barrier(self, engs=ENGS):
        for e in engs:
            waits = []
            wd = self.waited[e]
            for key, ent in self.allsems.items():
                sem, val = ent[0], ent[1]
                if val > 0 and wd.get(key, 0) < val:
                    wd[key] = val
                    waits.append((sem, val))
            if waits:
                self.ops[e].append((waits, None, None, 0))
        if tuple(engs) == tuple(ENGS):
            for b in self.dbufs:
                self.dfree.append(b.dsem)
                b.dsem = None
            self.dbufs = []

    def emit(self):
        print("ops per engine:", {e: len(v) for e, v in self.ops.items()}, "dsems", self.nds, flush=True)
        with self.nc.Block() as block:
            def mk(eng):
                def body(e):
                    for waits, fn, sem, inc in self.ops[eng]:
                        for (s, v) in waits:
                            e.wait_ge(s, v)
                        if fn is not None:
                            fn(e).then_inc(sem, inc)
                return body
            block.tensor(mk(PE))
            block.scalar(mk(ACT))
            block.vector(mk(DVE))
            block.gpsimd(mk(POOL))
            block.sync(mk(SP))

    def close(self):
        self.es.close()


class Arena:
    def __init__(self, f, nbytes):
        self.f = f
        self.n = nbytes // 2
        self.t = f.sbuf("arena", [128, self.n], BF16)
        self.off = 0

    def mark(self):
        return self.off

    def reset(self, m):
        self.off = m

    def alloc(self, name, free_shape, dtype, buf=True):
        nel = int(np.prod(free_shape))
        units = nel * (2 if dtype in (F32, I32) else 1)
        self.off = (self.off + 7) // 8 * 8
        assert self.off + units <= self.n, ("arena overflow", name, self.off, units, self.n)
        ap = self.t[:, self.off:self.off + units]
        self.off += units
        if dtype != BF16:
            ap = ap.bitcast(dtype)
        if len(free_shape) == 2:
            ap = ap.rearrange("p (a b) -> p a b", b=free_shape[1])
        elif len(free_shape) == 3:
            ap = ap.rearrange("p (a b c) -> p a b c", b=free_shape[1], c=free_shape[2])
        if buf:
            return ap, self.f.buf(ap, name)
        return ap


def build(debug=False, stub_mixers=False, upto=99, skip_delta=False):
    nc = bass.Bass("TRN2", target_bir_lowering=False)
    f = FW(nc)

    def din(name, shape, dt=F32):
        return nc.dram_tensor(name, list(shape), dt, kind="ExternalInput")

    def dscr(name, shape, dt):
        return nc.dram_tensor(name, list(shape), dt, kind=("ExternalOutput" if debug else "Internal"))

    x_d = din("x", [NT, D])
    pos_d = din("pos", [NS, S], I32)
    nmw_d = din("norm_mix_w", [16, 128])
    win_d = din("w_in", [D, PW])
    cqkv_d = din("conv_qkv_w", [5, 24, 128])
    alf_d = din("a_log_fwd", [1, 8]); alb_d = din("a_log_bwd", [1, 8])
    dtf_d = din("dt_bias_fwd", [1, 8]); dtb_d = din("dt_bias_bwd", [1, 8])
    dnw_d = din("delta_norm_w", [1, 128])
    wout_d = din("w_out", [D, D])
    nfw_d = din("norm_ffn_w", [16, 128])
    wfi_d = din("w_ffn_in", [D, 2 * DFF])
    cffn_d = din("conv_ffn_w", [3, 88, 128])
    wfo_d = din("w_ffn_out", [DFF, D])
    nfin_d = din("norm_final_w", [1, D])
    cst_d = din("cst", [128, NCST])
    out_d = nc.dram_tensor("out", [NT, D], F32, kind="ExternalOutput")

    winb_d = nc.dram_tensor("w_in_bf", [D, PW], BF16, kind="Internal")
    woutb_d = nc.dram_tensor("w_out_bf", [D, D], BF16, kind="Internal")
    wfib_d = nc.dram_tensor("w_ffn_in_bf", [D, 2 * DFF], BF16, kind="Internal")
    wfob_d = nc.dram_tensor("w_ffn_out_bf", [DFF, D], BF16, kind="Internal")
    wfit_d = nc.dram_tensor("w_ffn_in_tiled", [NFC // 2, 128, 16 * 2 * 256], BF16, kind="Internal")
    wfot_d = nc.dram_tensor("w_ffn_out_tiled", [16, 128, 11 * 512], BF16, kind="Internal")
    projT_d = dscr("projT", [NFEAT * 128, NT], BF16)
    vtok_d = dscr("vtok", [NT, 3072], BF16)
    gates_d = dscr("gates", [128, NT // 128, 32], F32)
    mixT_d = dscr("mixT", [D, NT], BF16)
    h1_d = dscr("h1", [NT, D], F32)

    B = lambda t, n: f.buf(t.ap(), n, dram=True)
    Xd, WIN, WOUT, WFI, WFO = B(x_d, "x"), B(win_d, "win"), B(wout_d, "wout"), B(wfi_d, "wfi"), B(wfo_d, "wfo")
    WINB, WOUTB, WFIB, WFOB = B(winb_d, "winb"), B(woutb_d, "woutb"), B(wfib_d, "wfib"), B(wfob_d, "wfob")
    WFIT, WFOT = B(wfit_d, "wfit"), B(wfot_d, "wfot")
    PROJT, VTOK, GATES, MIXT, H1, OUT = B(projT_d, "projT"), B(vtok_d, "vtok"), B(gates_d, "gates"), B(mixT_d, "mixT"), B(h1_d, "h1"), B(out_d, "out")
    SMALL = f.buf(None, "small", dram=True)

    dbg_list = []

    def dbg(name, ap, BUF, shape, dt):
        if not debug:
            return
        t = nc.dram_tensor("dbg_" + name, [128] + list(shape), dt, kind="ExternalOutput")
        DB = f.buf(t.ap(), "dbg_" + name, dram=True)
        f.dma(SP, t.ap(), ap, reads=[BUF], writes=[DB])

    ar = Arena(f, 190 * 1024)
    banks = []
    for i in range(8):
        t = f.psum("bank%d" % i, [128, 512], F32)
        banks.append((t[:], f.buf(t[:], "bank%d" % i)))
    bank_rr = [0]

    def next_bank(lo=0, hi=8):
        i = lo + bank_rr[0] % (hi - lo)
        bank_rr[0] += 1
        return banks[i]

    cst, CST = ar.alloc("cst", [NCST], F32)
    f.dma(SP, cst, cst_d.ap(), reads=[SMALL], writes=[CST])
    ident, IDENT = cst[:, 0:128], CST
    identb, IDENTB = ar.alloc("identb", [128], BF16)
    f.op(DVE, lambda e: e.tensor_copy(out=identb, in_=ident), reads=[IDENT], writes=[IDENTB])
    gates_sb, GSB = ar.alloc("gates_sb", [NT // 128, 32], F32)

    def load_featvec(dram_t, nrow, name):
        rows, ROWS = ar.alloc(name + "_r", [128], F32)
        f.dma(SP, rows[0:nrow, :], dram_t.ap(), reads=[SMALL], writes=[ROWS])
        res, RES = ar.alloc(name, [nrow], F32)
        bk, BK = next_bank()
        f.op(PE, lambda e: e.transpose(bk[:, 0:nrow], rows[0:nrow, :], ident[0:nrow, 0:nrow]), reads=[ROWS, IDENT], writes=[BK])
        f.op(DVE, lambda e: e.tensor_copy(out=res, in_=bk[:, 0:nrow]), reads=[BK], writes=[RES])
        return res, RES

    nmw, NMW = load_featvec(nmw_d, 16, "nmw")
    nfw, NFW = load_featvec(nfw_d, 16, "nfw")
    cffn = []
    for j in range(3):
        t_ = nc.dram_tensor("cffn_view%d" % j, [1], F32, kind="Internal") if False else None
        rows, ROWS = ar.alloc("cffn_r%d" % j, [128], F32)
        f.dma(SP, rows[0:88, :], cffn_d.ap()[j], reads=[SMALL], writes=[ROWS])
        res, RES = ar.alloc("cffn%d" % j, [88], F32)
        bk, BK = next_bank()
        f.op(PE, lambda e, bk=bk, rows=rows: e.transpose(bk[:, 0:88], rows[0:88, :], ident[0:88, 0:88]), reads=[ROWS, IDENT], writes=[BK])
        f.op(DVE, lambda e, bk=bk, res=res: e.tensor_copy(out=res, in_=bk[:, 0:88]), reads=[BK], writes=[RES])
        cffn.append((res, RES))
    nfin, NFIN = ar.alloc("nfin", [D], F32)
    f.dma(SP, nfin, nfin_d.ap().to_broadcast([128, D]), reads=[SMALL], writes=[NFIN])

    base_mark = ar.mark()

    def cast_w(src, dst, SRC, DST, rows, cols, b):
        for r0 in range(0, rows, 128):
            f.dma(POOL, dst.ap()[r0:r0 + 128, :].rearrange("r (a b) -> r a b", b=b),
                  src.ap()[r0:r0 + 128, :].rearrange("r (a b) -> r a b", b=b), reads=[SRC], writes=[DST])
    cast_w(win_d, winb_d, WIN, WINB, D, PW, 834)
    cast_w(wout_d, woutb_d, WOUT, WOUTB, D, D, 1024)
    cast_w(wfi_d, wfib_d, WFI, WFIB, D, 2 * DFF, 1024)
    cast_w(wfo_d, wfob_d, WFO, WFOB, DFF, D, 1024)

    if upto <= 0:
        f.barrier(); f.emit(); f.close()
        return nc
    def make_norm(tag, nslot=3):
        xs = [ar.alloc("%s_x%d" % (tag, i), [D], F32) for i in range(nslot)]
        junk, JUNK = ar.alloc(tag + "_junk", [D], BF16)
        st = [ar.alloc("%s_st%d" % (tag, i), [4], F32) for i in range(2)]
        dg = [ar.alloc("%s_dg%d" % (tag, i), [128], F32) for i in range(2)]
        ctr = [0]

        def run(blocks, wn, WN, nT, NTB):
            for blk in blocks:
                one(blk, wn, WN, nT, NTB)

        def one(blk, wn, WN, nT, NTB):
            rows, npart, col, step, SRCB = blk
            bi = ctr[0]
            ctr[0] += 1
            if True:
                xt, XT = xs[bi % nslot]
                s_, ST = st[bi % 2]
                d_, DG = dg[bi % 2]
                if npart == 128:
                    f.dma(SP, xt, rows, reads=[SRCB], writes=[XT])
                else:
                    f.op(POOL, lambda e, xt=xt: e.memset(xt[0:2, :], 0.0), writes=[XT])
                    for ri, r in enumerate(rows):
                        if r is not None:
                            f.dma(SP, xt[ri:ri + 1, :], r, reads=[SRCB], writes=[XT])
                p = npart
                f.op(ACT, lambda e, xt=xt, s_=s_, p=p: e.activation(out=junk[0:p, :], in_=xt[0:p, :], func=AF.Square, accum_out=s_[0:p, 0:1]),
                     reads=[XT], writes=[JUNK, ST])
                f.op(DVE, lambda e, s_=s_, p=p: e.tensor_scalar(out=s_[0:p, 1:2], in0=s_[0:p, 0:1], scalar1=1.0 / D, scalar2=EPS, op0=ALU.mult, op1=ALU.add),
                     reads=[ST], writes=[ST])
                f.op(ACT, lambda e, s_=s_, p=p: e.activation(out=s_[0:p, 2:3], in_=s_[0:p, 1:2], func=AF.Sqrt), reads=[ST], writes=[ST])
                f.op(DVE, lambda e, s_=s_, p=p: e.reciprocal(out=s_[0:p, 3:4], in_=s_[0:p, 2:3]), reads=[ST], writes=[ST])
                f.op(DVE, lambda e, s_=s_, d_=d_, p=p: e.tensor_scalar(out=d_[0:p, 0:p], in0=ident[0:p, 0:p], scalar1=s_[0:p, 3:4], scalar2=None, op0=ALU.mult),
                     reads=[ST, IDENT], writes=[DG])
                for g4 in range(4):
                    bk, BK = next_bank(0, 2)
                    for j in range(4):
                        kc = g4 * 4 + j
                        f.op(PE, lambda e, bk=bk, xt=xt, d_=d_, kc=kc, j=j, p=p: e.matmul(
                            bk[:, j * 128:j * 128 + p], lhsT=xt[0:p, kc * 128:(kc + 1) * 128], rhs=d_[0:p, 0:p], start=True, stop=True),
                            reads=[XT, DG], writes=[BK])
                    for j in range(4):
                        kc = g4 * 4 + j
                        if p == 128:
                            o_ap = nT[:, kc, col:col + 128]
                        else:
                            o_ap = nT[:, kc, col:col + step + 1:step]
                        i_ap = bk[:, j * 128:j * 128 + p]
                        if g4 % 2 == 0:
                            f.op(ACT, lambda e, o_ap=o_ap, i_ap=i_ap, kc=kc: e.activation(out=o_ap, in_=i_ap, func=AF.Copy, scale=wn[:, kc:kc + 1]),
                                 reads=[BK, WN], writes=[NTB[kc]])
                        else:
                            f.op(DVE, lambda e, o_ap=o_ap, i_ap=i_ap, kc=kc: e.tensor_scalar(out=o_ap, in0=i_ap, scalar1=wn[:, kc:kc + 1], scalar2=None, op0=ALU.mult),
                                 reads=[BK, WN], writes=[NTB[kc]])
        return run

    TT1 = 1024
    NH1 = TT1 // 512
    m1 = ar.mark()
    norm1 = make_norm("n1")
    nT1 = ar.alloc("nT1", [16, TT1], BF16, buf=False)
    NT1B = [f.buf(nT1[:, kc, :], "nT1_%d" % kc) for kc in range(16)]
    wsl = [ar.alloc("wsl%d" % i, [16, 512], BF16) for i in range(3)]
    stg = [ar.alloc("stg%d" % i, [4, TT1], BF16) for i in range(2)]
    vst = [ar.alloc("vst%d" % i, [512], BF16) for i in range(2)]
    wg, WG = ar.alloc("wg", [16, 32], BF16)
    f.dma(SP, wg, winb_d.ap()[:, COL_G:COL_G + 32].rearrange("(k p) n -> p k n", p=128), reads=[WINB], writes=[WG])
    feat_slabs = [c for c in range(0, 4096, 512)] + [COL_BQ + c for c in range(0, 3072, 512)] + [COL_BK + c for c in range(0, 3072, 512)]
    evac_rr = [0]

    def evac(out_ap, in_ap, reads, writes):
        evac_rr[0] += 1
        if evac_rr[0] % 2 == 0:
            f.op(ACT, lambda e: e.activation(out=out_ap, in_=in_ap, func=AF.Copy), reads=reads, writes=writes)
        else:
            f.op(DVE, lambda e: e.tensor_copy(out=out_ap, in_=in_ap), reads=reads, writes=writes)

    si = 0
    for tile in range(NT // TT1):
        t0 = tile * TT1
        blocks = [(x_d.ap()[t0 + b * 128:t0 + (b + 1) * 128, :], 128, b * 128, 0, Xd) for b in range(TT1 // 128)]
        norm1(blocks, nmw, NMW, nT1, NT1B)
        for sidx, c0 in enumerate(feat_slabs):
            w_, W_ = wsl[si % 3]
            sg_, SG_ = stg[si % 2]
            si += 1
            f.dma(SP, w_, winb_d.ap()[:, c0:c0 + 512].rearrange("(k p) n -> p k n", p=128), reads=[WINB], writes=[W_])
            for j in range(4):
                for h in range(NH1):
                    bk, BK = next_bank(2, 8)
                    for kc in range(16):
                        f.op(PE, lambda e, bk=bk, w_=w_, kc=kc, j=j, h=h: e.matmul(
                            bk, lhsT=w_[:, kc, j * 128:(j + 1) * 128], rhs=nT1[:, kc, h * 512:(h + 1) * 512], start=(kc == 0), stop=(kc == 15)),
                            reads=[W_, NT1B[kc]], writes=[BK])
                    evac(sg_[:, j, h * 512:(h + 1) * 512], bk, [BK], [SG_])
            f.dma(POOL, projT_d.ap()[sidx * 512:(sidx + 1) * 512, t0:t0 + TT1].rearrange("(j p) t -> p j t", p=128), sg_,
                  reads=[SG_], writes=[PROJT])
        for sb in range(6):
            c0 = COL_BV + sb * 512
            w_, W_ = wsl[si % 3]
            si += 1
            f.dma(SP, w_, winb_d.ap()[:, c0:c0 + 512].rearrange("(k p) n -> p k n", p=128), reads=[WINB], writes=[W_])
            for b in range(TT1 // 128):
                bk, BK = next_bank(2, 8)
                for kc in range(16):
                    f.op(PE, lambda e, bk=bk, w_=w_, kc=kc, b=b: e.matmul(
                        bk, lhsT=nT1[:, kc, b * 128:(b + 1) * 128], rhs=w_[:, kc, :], start=(kc == 0), stop=(kc == 15)),
                        reads=[W_, NT1B[kc]], writes=[BK])
                v_, V_ = vst[(sb * 8 + b) % 2]
                evac(v_, bk, [BK], [V_])
                f.dma(POOL, vtok_d.ap()[t0 + b * 128:t0 + (b + 1) * 128, sb * 512:(sb + 1) * 512], v_, reads=[V_], writes=[VTOK])
        for b in range(TT1 // 128):
            bk, BK = next_bank(2, 8)
            for kc in range(16):
                f.op(PE, lambda e, bk=bk, kc=kc, b=b: e.matmul(
                    bk[:, 0:32], lhsT=nT1[:, kc, b * 128:(b + 1) * 128], rhs=wg[:, kc, :], start=(kc == 0), stop=(kc == 15)),
                    reads=[WG, NT1B[kc]], writes=[BK])
            gb = t0 // 128 + b
            f.op(DVE, lambda e, bk=bk, gb=gb: e.tensor_copy(out=gates_sb[:, gb, :], in_=bk[:, 0:32]), reads=[BK], writes=[GSB])

    if upto <= 1:
        f.barrier(); f.emit(); f.close()
        return nc
    if debug:
        f.dma(SP, gates_d.ap(), gates_sb, reads=[GSB], writes=[GATES])
    f.barrier()
    ar.reset(m1)
    for slab in range(NFC // 2):
        for gv in range(2):
            c0_ = gv * DFF + slab * 256
            f.dma(ACT, wfit_d.ap()[slab].rearrange("p (k g n) -> p k g n", k=16, g=2)[:, :, gv, :],
                  wfib_d.ap()[:, c0_:c0_ + 256].rearrange("(k p) n -> p k n", p=128), reads=[WFIB], writes=[WFIT])
    for ds in range(4):
        for q in range(4):
            f.dma(ACT, wfot_d.ap()[ds * 4 + q].rearrange("p (k n) -> p k n", k=11),
                  wfob_d.ap()[q * 11 * 128:(q + 1) * 11 * 128, ds * 512:(ds + 1) * 512].rearrange("(k p) n -> p k n", p=128), reads=[WFOB], writes=[WFOT])
    if stub_mixers:
        z_, Z_ = ar.alloc("zeros", [NT], BF16)
        f.op(DVE, lambda e: e.memset(z_, 0.0), writes=[Z_])
        for c in range(16):
            f.dma(SP, mixT_d.ap()[c * 128:(c + 1) * 128, :], z_, reads=[Z_], writes=[MIXT])
    else:
        if not skip_delta:

            maskN = [cst[:, 576:704], cst[:, 832:960]]
            maskQ = [cst[:, 704:832], cst[:, 960:1088]]
            Lmat = [cst[:, 1088:1216], cst[:, 192:320]]
            cq = []
            for j in range(5):
                rows, ROWS = ar.alloc("cq_r%d" % j, [128], F32)
                f.dma(SP, rows[0:24, :], cqkv_d.ap()[j], reads=[SMALL], writes=[ROWS])
                res, RES = ar.alloc("cq%d" % j, [24], F32)
                bk, BK = next_bank()
                f.op(PE, lambda e, bk=bk, rows=rows: e.transpose(bk[:, 0:24], rows[0:24, :], ident[0:24, 0:24]), reads=[ROWS, IDENT], writes=[BK])
                f.op(DVE, lambda e, bk=bk, res=res: e.tensor_copy(out=res, in_=bk[:, 0:24]), reads=[BK], writes=[RES])
                cq.append((res, RES))
            prm, PRM = ar.alloc("prm", [2, 2, 8], F32)
            f.dma(SP, prm[:, 0, 0, :], dtf_d.ap().to_broadcast([128, 8]), reads=[SMALL], writes=[PRM])
            f.dma(SP, prm[:, 0, 1, :], dtb_d.ap().to_broadcast([128, 8]), reads=[SMALL], writes=[PRM])
            f.dma(SP, prm[:, 1, 0, :], alf_d.ap().to_broadcast([128, 8]), reads=[SMALL], writes=[PRM])
            f.dma(SP, prm[:, 1, 1, :], alb_d.ap().to_broadcast([128, 8]), reads=[SMALL], writes=[PRM])
            f.op(ACT, lambda e: e.activation(out=prm[:, 1, :, :], in_=prm[:, 1, :, :], func=AF.Exp), reads=[PRM], writes=[PRM])
            f.op(DVE, lambda e: e.tensor_scalar(out=prm[:, 1, :, :], in0=prm[:, 1, :, :], scalar1=-1.0, scalar2=None, op0=ALU.mult), reads=[PRM], writes=[PRM])
            DTB, DTBB = ar.alloc("DTB", [2, 16, 8], F32)
            NEGAf, NEGAB = ar.alloc("NEGA", [256], F32); NEGA = NEGAf.rearrange("p (d c h) -> p d c h", d=2, c=16)
            for d_ in range(2):
                for c in range(16):
                    f.op(POOL, lambda e, d_=d_, c=c: e.tensor_copy(out=DTB[:, d_, c, :], in_=prm[:, 0, d_, :]), reads=[PRM], writes=[DTBB])
                    f.op(POOL, lambda e, d_=d_, c=c: e.tensor_copy(out=NEGA[:, d_, c, :], in_=prm[:, 1, d_, :]), reads=[PRM], writes=[NEGAB])
            dnw, DNW = ar.alloc("dnw", [128], F32)
            f.dma(SP, dnw, dnw_d.ap().to_broadcast([128, 128]), reads=[SMALL], writes=[DNW])
            onesf, ONESF = ar.alloc("onesf", [128], F32)
            f.op(DVE, lambda e: e.memset(onesf, 1.0), writes=[ONESF])
            onesb, ONESB = ar.alloc("onesb2", [128], BF16)
            f.op(DVE, lambda e: e.memset(onesb, 1.0), writes=[ONESB])
            V4 = lambda t: t.rearrange("p (d c h) -> p d c h", d=2, c=16)
            T1f, T1B = ar.alloc("T1", [256], F32); T1 = V4(T1f)
            GT_f, GTB = ar.alloc("GT_", [256], F32); GT_ = V4(GT_f)
            NEGBf, NEGBB = ar.alloc("NEGB", [256], F32); NEGB = V4(NEGBf)
            BETAf, BETAB = ar.alloc("BETA", [256], F32); BETA = V4(BETAf)
            CGf, CGB = ar.alloc("CG", [256], F32); CG = V4(CGf)
            CGc, CGcB = ar.alloc("CGc", [16, 16], F32)
            TOTf, TOTB = ar.alloc("TOT", [256], F32); TOT = V4(TOTf)
            GEXf, GEXB = ar.alloc("GEX", [256], F32); GEX = V4(GEXf)
            NEGGf, NEGGB = ar.alloc("NEGG", [256], F32); NEGG = V4(NEGGf)
            W2f, W2B = ar.alloc("W2", [256], F32); W2 = V4(W2f)
            DECf, DECB = ar.alloc("DEC", [256], F32); DEC = V4(DECf)
            xp = [ar.alloc("xp%d" % i, [S + 4], BF16) for i in range(3)]
            zt, ZT = ar.alloc("zt", [S], BF16)
            for i in range(3):
                f.op(DVE, lambda e, i=i: e.memset(xp[i][0], 0.0), writes=[xp[i][1]])
            acc = [ar.alloc("cacc%d" % i, [S], F32) for i in range(2)]
            sq, SQ = ar.alloc("sq", [S], BF16)
            rstd, RSTD = ar.alloc("rstd", [S], F32)
            QT, QTB = ar.alloc("QT", [S], BF16)
            KT, KTB = ar.alloc("KT", [S], BF16)
            KT32, KT32B = ar.alloc("KT32", [S], F32)
            VT, VTB = ar.alloc("VT", [S], BF16)
            zs, ZS = ar.alloc("zs", [S], BF16)
            Ktok, KTOK = ar.alloc("Ktok", [16, 128], BF16)
            Vtok, VTOK_ = ar.alloc("Vtok", [16, 128], BF16)
            Oacc, OACC = acc[0][0].rearrange("p (a b) -> p a b", b=128), acc[0][1]
            OACCB = [f.buf(Oacc[:, c, :], "oacc%d" % c) for c in range(16)]
            onb, ONB = ar.alloc("onb", [16, 128], BF16)
            mo2, MO2 = sq, SQ
            osq, OSQ = ar.alloc("osq", [16, 4], F32)
            TiTb = ar.alloc("TiTb", [2, 16, 128], BF16, buf=False)
            QKm = ar.alloc("QKm", [2, 16, 128], BF16, buf=False)
            TIB = [[f.buf(TiTb[:, d_, c4 * 4:(c4 + 1) * 4, :], "tib") for c4 in range(4)] for d_ in range(2)]
            QKB2 = [[f.buf(QKm[:, d_, c4 * 4:(c4 + 1) * 4, :], "qkmb") for c4 in range(4)] for d_ in range(2)]
            dgm2 = [ar.alloc("dgm%d" % i, [4, 128], F32) for i in range(2)]
            t12 = [ar.alloc("t1_%d" % i, [4, 128], F32) for i in range(2)]
            t32 = [ar.alloc("t3_%d" % i, [4, 128], F32) for i in range(2)]
            gam12 = [ar.alloc("gam1_%d" % i, [4, 128], F32) for i in range(2)]
            Nm2 = [ar.alloc("Nm%d" % i, [4, 128], F32) for i in range(2)]
            Pa2 = [[ar.alloc("Pa%d_%d" % (d_, i), [4, 128], F32) for i in range(2)] for d_ in range(2)]
            PaT2 = [[ar.alloc("PaT%d_%d" % (d_, i), [4, 128], F32) for i in range(2)] for d_ in range(2)]
            Xa2 = [[ar.alloc("Xa%d_%d" % (d_, i), [4, 128], F32) for i in range(2)] for d_ in range(2)]
            KKs, KKSB = ar.alloc("KKs", [4, 128], F32)
            KQs, KQSB = ar.alloc("KQs", [4, 128], F32)
            ident4, ID4B = ar.alloc("ident4", [4, 128], F32)
            for j in range(4):
                f.op(DVE, lambda e, j=j: e.tensor_copy(out=ident4[:, j, :], in_=ident), reads=[IDENT], writes=[ID4B])
            print("phase2 arena KiB", ar.off * 2 / 1024)
            Sst = [ar.alloc("Sst%d" % i, [128], F32) for i in range(2)]
            Sbf = [ar.alloc("Sbf%d" % i, [128], BF16) for i in range(2)]
            Rb = [ar.alloc("Rb%d" % i, [128], BF16) for i in range(2)]
            Vn = [ar.alloc("Vn%d" % i, [2, 128], BF16) for i in range(2)]
            bfv = lambda bk: bk.bitcast(BF16)
            for s_i in range(NS):
                c0 = s_i * S
                gs = gates_sb[:, s_i * 16:(s_i + 1) * 16, :]
                for d_ in range(2):
                    f.op(DVE, lambda e, gs=gs, d_=d_: e.tensor_tensor(out=T1[:, d_, :, :], in0=gs[:, :, d_ * 8:(d_ + 1) * 8], in1=DTB[:, d_, :, :], op=ALU.add), reads=[GSB, DTBB], writes=[T1B])
                    f.op(ACT, lambda e, gs=gs, d_=d_: e.activation(out=BETA[:, d_, :, :], in_=gs[:, :, 16 + d_ * 8:16 + (d_ + 1) * 8], func=AF.Sigmoid), reads=[GSB], writes=[BETAB])
                f.op(ACT, lambda e: e.activation(out=T1f, in_=T1f, func=AF.Exp), reads=[T1B], writes=[T1B])
                f.op(ACT, lambda e: e.activation(out=T1f, in_=T1f, func=AF.Ln, bias=1.0), reads=[T1B], writes=[T1B])
                f.op(DVE, lambda e: e.tensor_tensor(out=GT_f, in0=T1f, in1=NEGAf, op=ALU.mult), reads=[T1B, NEGAB], writes=[GTB])
                f.op(DVE, lambda e: e.tensor_scalar(out=NEGBf, in0=BETAf, scalar1=-1.0, scalar2=None, op0=ALU.mult), reads=[BETAB], writes=[NEGBB])
                bk, BK = next_bank()
                for d_ in range(2):
                    f.op(PE, lambda e, bk=bk, d_=d_: e.matmul(bk[:, d_ * 128:(d_ + 1) * 128], lhsT=Lmat[d_], rhs=GT_f[:, d_ * 128:(d_ + 1) * 128], start=True, stop=True), reads=[CST, GTB], writes=[BK])
                f.op(DVE, lambda e, bk=bk: e.tensor_copy(out=CGf, in_=bk[:, 0:256]), reads=[BK], writes=[CGB])
                bk, BK = next_bank()
                f.op(PE, lambda e, bk=bk: e.matmul(bk[:, 0:256], lhsT=onesf, rhs=GT_f, start=True, stop=True), reads=[ONESF, GTB], writes=[BK])
                f.op(DVE, lambda e, bk=bk: e.tensor_copy(out=TOTf, in_=bk[:, 0:256]), reads=[BK], writes=[TOTB])
                for d_ in range(2):
                    f.op(DVE, lambda e, d_=d_: e.tensor_copy(out=CGc[:, :, d_ * 8:(d_ + 1) * 8], in_=CG[:, d_, :, :]), reads=[CGB], writes=[CGcB])
                f.op(ACT, lambda e: e.activation(out=GEXf, in_=CGf, func=AF.Exp), reads=[CGB], writes=[GEXB])
                f.op(DVE, lambda e: e.tensor_scalar(out=NEGGf, in0=GEXf, scalar1=-1.0, scalar2=None, op0=ALU.mult), reads=[GEXB], writes=[NEGGB])
                f.op(DVE, lambda e: e.tensor_tensor(out=W2f, in0=TOTf, in1=CGf, op=ALU.subtract), reads=[TOTB, CGB], writes=[W2B])
                f.op(ACT, lambda e: e.activation(out=W2f, in_=W2f, func=AF.Exp), reads=[W2B], writes=[W2B])
                f.op(ACT, lambda e: e.activation(out=DECf, in_=TOTf, func=AF.Exp), reads=[TOTB], writes=[DECB])
                for h in range(8):
                    for i in range(3):
                        ch = i * 8 + h
                        f.dma(SP, xp[i][0][:, 2:S + 2], projT_d.ap()[ch * 128:(ch + 1) * 128, c0:c0 + S], reads=[PROJT], writes=[xp[i][1]])
                    f.dma(SP, zt, projT_d.ap()[(24 + h) * 128:(25 + h) * 128, c0:c0 + S], reads=[PROJT], writes=[ZT])
                    f.op(ACT, lambda e: e.activation(out=zs, in_=zt, func=AF.Silu), reads=[ZT], writes=[ZS])
                    for i in range(3):
                        ch = i * 8 + h
                        x_, X_ = xp[i]
                        a_, A_ = acc[i % 2]
                        f.op(DVE, lambda e, x_=x_, a_=a_, ch=ch: e.tensor_scalar(out=a_, in0=x_[:, 2:S + 2], scalar1=cq[2][0][:, ch:ch + 1], scalar2=None, op0=ALU.mult),
                             reads=[X_, cq[2][1]], writes=[A_])
                        for j in (0, 1, 3, 4):
                            f.op(DVE, lambda e, x_=x_, a_=a_, ch=ch, j=j: e.scalar_tensor_tensor(out=a_, in0=x_[:, j:j + S], scalar=cq[j][0][:, ch:ch + 1], in1=a_, op0=ALU.mult, op1=ALU.add),
                                 reads=[X_, cq[j][1], A_], writes=[A_])
                        if i == 2:
                            f.op(ACT, lambda e, a_=a_: e.activation(out=VT, in_=a_, func=AF.Silu), reads=[A_], writes=[VTB])
                            continue
                        f.op(ACT, lambda e, a_=a_: e.activation(out=a_, in_=a_, func=AF.Silu), reads=[A_], writes=[A_])
                        f.op(ACT, lambda e, a_=a_: e.activation(out=sq, in_=a_, func=AF.Square), reads=[A_], writes=[SQ])
                        for t in range(4):
                            bk, BK = next_bank()
                            f.op(PE, lambda e, bk=bk, t=t: e.matmul(bk, lhsT=onesb, rhs=sq[:, t * 512:(t + 1) * 512], start=True, stop=True), reads=[ONESB, SQ], writes=[BK])
                            f.op(ACT, lambda e, bk=bk, t=t: e.activation(out=rstd[:, t * 512:(t + 1) * 512], in_=bk, func=AF.Sqrt, bias=cst[:, 162:163]), reads=[BK, CST], writes=[RSTD])
                        f.op(DVE, lambda e: e.reciprocal(out=rstd, in_=rstd), reads=[RSTD], writes=[RSTD])
                        if i == 0:
                            f.op(DVE, lambda e, a_=a_: e.scalar_tensor_tensor(out=QT, in0=a_, scalar=128.0 ** -0.5, in1=rstd, op0=ALU.mult, op1=ALU.mult), reads=[A_, RSTD], writes=[QTB])
                        else:
                            f.op(DVE, lambda e, a_=a_: e.tensor_tensor(out=KT, in0=a_, in1=rstd, op=ALU.mult), reads=[A_, RSTD], writes=[KTB])
                            f.op(POOL, lambda e, a_=a_: e.tensor_tensor(out=KT32, in0=a_, in1=rstd, op=ALU.mult), reads=[A_, RSTD], writes=[KT32B])
                    for (src, SRCB, dst, DSTB) in ((KT, KTB, Ktok, KTOK), (VT, VTB, Vtok, VTOK_)):
                        for c4 in range(4):
                            bk, BK = next_bank()
                            for j in range(4):
                                c = c4 * 4 + j
                                f.op(PE, lambda e, bk=bk, src=src, c=c, j=j: e.transpose(bfv(bk)[:, j * 128:(j + 1) * 128], src[:, c * 128:(c + 1) * 128], identb),
                                     reads=[SRCB, IDENTB], writes=[BK])
                            f.op(DVE, lambda e, bk=bk, dst=dst, c4=c4: e.tensor_copy(out=dst[:, c4 * 4:(c4 + 1) * 4, :], in_=bfv(bk)[:, 0:512]), reads=[BK], writes=[DSTB])
                    for c4 in range(4):
                        bX, BX = banks[0]
                        bY, BY = banks[1]
                        for j in range(4):
                            c = c4 * 4 + j
                            cs = slice(c * 128, (c + 1) * 128)
                            f.op(PE, lambda e, cs=cs, j=j: e.matmul(bX[:, j * 128:(j + 1) * 128], lhsT=KT32[:, cs], rhs=KT32[:, cs], start=True, stop=True), reads=[KT32B], writes=[BX])
                            f.op(PE, lambda e, cs=cs, j=j: e.matmul(bY[:, j * 128:(j + 1) * 128], lhsT=KT[:, cs], rhs=QT[:, cs], start=True, stop=True), reads=[KTB, QTB], writes=[BY])
                        f.op(DVE, lambda e: e.tensor_copy(out=KKs, in_=bX.rearrange("p (a b) -> p a b", b=128)), reads=[BX], writes=[KKSB])
                        f.op(ACT, lambda e: e.activation(out=KQs, in_=bY.rearrange("p (a b) -> p a b", b=128), func=AF.Copy), reads=[BY], writes=[KQSB])
                        bset = [(banks[2], banks[3], banks[4]), (banks[5], banks[6], banks[7])]
                        DD = (0, 1)
                        for d_ in DD:
                            (bA, BA) = bset[d_][0]
                            dgm, DGM = dgm2[d_]
                            for j in range(4):
                                c = c4 * 4 + j
                                f.op(POOL, lambda e, h=h, c=c, j=j, d_=d_, dgm=dgm: e.tensor_scalar(out=dgm[:, j, :], in0=ident, scalar1=CG[:, d_, c, h:h + 1], scalar2=None, op0=ALU.mult),
                                     reads=[IDENT, CGB], writes=[DGM])
                                f.op(PE, lambda e, j=j, bA=bA, dgm=dgm: e.matmul(bA[:, j * 128:(j + 1) * 128], lhsT=onesf, rhs=dgm[:, j, :], start=True, stop=True),
                                     reads=[ONESF, DGM], writes=[BA])
                        for d_ in DD:
                            (bA, BA) = bset[d_][0]
                            t1, T1b = t12[d_]
                            t3, T3b = t32[d_]
                            gam1, GAM1 = gam12[d_]
                            for j in range(4):
                                c = c4 * 4 + j
                                f.op(DVE, lambda e, h=h, c=c, j=j, d_=d_, bA=bA, t1=t1: e.scalar_tensor_tensor(out=t1[:, j, :], in0=bA[:, j * 128:(j + 1) * 128], scalar=CG[:, d_, c, h:h + 1], in1=maskN[d_], op0=ALU.subtract, op1=ALU.max),
                                     reads=[BA, CGB, CST], writes=[T1b])
                                f.op(DVE, lambda e, h=h, c=c, j=j, d_=d_, bA=bA, t3=t3: e.scalar_tensor_tensor(out=t3[:, j, :], in0=bA[:, j * 128:(j + 1) * 128], scalar=CG[:, d_, c, h:h + 1], in1=maskQ[d_], op0=ALU.subtract, op1=ALU.min),
                                     reads=[BA, CGB, CST], writes=[T3b])
                            f.op(ACT, lambda e, t1=t1, gam1=gam1: e.activation(out=gam1, in_=t1, func=AF.Exp, scale=-1.0), reads=[T1b], writes=[GAM1])
                            f.op(ACT, lambda e, t3=t3: e.activation(out=t3, in_=t3, func=AF.Exp), reads=[T3b], writes=[T3b])
                        for d_ in DD:
                            t3, T3b = t32[d_]
                            gam1, GAM1 = gam12[d_]
                            Nm, NM = Nm2[d_]
                            for j in range(4):
                                c = c4 * 4 + j
                                f.op(DVE, lambda e, h=h, c=c, j=j, d_=d_, Nm=Nm, gam1=gam1: e.scalar_tensor_tensor(out=Nm[:, j, :], in0=KKs[:, j, :], scalar=NEGB[:, d_, c, h:h + 1], in1=gam1[:, j, :], op0=ALU.mult, op1=ALU.mult),
                                     reads=[KKSB, NEGBB, GAM1], writes=[NM])
                            f.op(POOL, lambda e, d_=d_, c4=c4, t3=t3: e.tensor_tensor(out=QKm[:, d_, c4 * 4:(c4 + 1) * 4, :], in0=KQs, in1=t3, op=ALU.mult),
                                 reads=[KQSB, T3b], writes=[QKB2[d_][c4]])
                        st = {}
                        for d_ in DD:
                            (bB, BB) = bset[d_][1]
                            Nm, NM = Nm2[d_]
                            for j in range(4):
                                f.op(PE, lambda e, j=j, bB=bB, Nm=Nm: e.transpose(bB[:, j * 128:(j + 1) * 128], Nm[:, j, :], ident), reads=[NM, IDENT], writes=[BB])
                            PT_, PTB_ = PaT2[d_][0]
                            X_, XB_ = Xa2[d_][0]
                            f.op(ACT, lambda e, PT_=PT_, bB=bB: e.activation(out=PT_, in_=bB.rearrange("p (a b) -> p a b", b=128), func=AF.Copy), reads=[BB], writes=[PTB_])
                            f.op(POOL, lambda e, X_=X_, PT_=PT_: e.tensor_tensor(out=X_, in0=PT_, in1=ident4, op=ALU.add), reads=[PTB_, ID4B], writes=[XB_])
                            st[d_] = [Nm, NM, PT_, PTB_, X_, XB_]
                        for lv in range(6):
                            last = (lv == 5)
                            nxt = {}
                            for d_ in DD:
                                (bA, BA), (bB, BB), (bC, BC) = bset[d_]
                                P_, PB_, PT_, PTB_, X_, XB_ = st[d_]
                                for j in range(4):
                                    f.op(PE, lambda e, j=j, P_=P_, PT_=PT_, bA=bA: e.matmul(bA[:, j * 128:(j + 1) * 128], lhsT=PT_[:, j, :], rhs=P_[:, j, :], start=True, stop=True), reads=[PB_, PTB_], writes=[BA])
                                if not last:
                                    for j in range(4):
                                        f.op(PE, lambda e, j=j, P_=P_, PT_=PT_, bB=bB: e.matmul(bB[:, j * 128:(j + 1) * 128], lhsT=P_[:, j, :], rhs=PT_[:, j, :], start=True, stop=True), reads=[PB_, PTB_], writes=[BB])
                            for d_ in DD:
                                (bA, BA), (bB, BB), (bC, BC) = bset[d_]
                                P2, P2B = Pa2[d_][lv % 2]
                                P2T, P2TB = PaT2[d_][(lv + 1) % 2]
                                f.op(ACT, lambda e, P2=P2, bA=bA: e.activation(out=P2, in_=bA.rearrange("p (a b) -> p a b", b=128), func=AF.Copy), reads=[BA], writes=[P2B])
                                if not last:
                                    f.op(DVE, lambda e, P2T=P2T, bB=bB: e.tensor_copy(out=P2T, in_=bB.rearrange("p (a b) -> p a b", b=128)), reads=[BB], writes=[P2TB])
                                nxt[d_] = [P2, P2B, P2T, P2TB]
                            for d_ in DD:
                                (bA, BA), (bB, BB), (bC, BC) = bset[d_]
                                P2, P2B, P2T, P2TB = nxt[d_]
                                X_, XB_ = st[d_][4], st[d_][5]
                                for j in range(4):
                                    f.op(PE, lambda e, j=j, P2=P2, X_=X_, bC=bC: e.matmul(bC[:, j * 128:(j + 1) * 128], lhsT=P2[:, j, :], rhs=X_[:, j, :], start=True, stop=True), reads=[P2B, XB_], writes=[BC])
                            for d_ in DD:
                                (bA, BA), (bB, BB), (bC, BC) = bset[d_]
                                P2, P2B, P2T, P2TB = nxt[d_]
                                X_, XB_ = st[d_][4], st[d_][5]
                                X2, X2B = Xa2[d_][(lv + 1) % 2]
                                f.op(DVE, lambda e, X2=X2, X_=X_, bC=bC: e.tensor_tensor(out=X2, in0=bC.rearrange("p (a b) -> p a b", b=128), in1=X_, op=ALU.add), reads=[BC, XB_], writes=[X2B])
                                st[d_] = [P2, P2B, P2T, P2TB, X2, X2B]
                        for d_ in DD:
                            X_, XB_ = st[d_][4], st[d_][5]
                            for j in range(4):
                                c = c4 * 4 + j
                                f.op(ACT, lambda e, h=h, c=c, j=j, d_=d_, X_=X_: e.activation(out=TiTb[:, d_, c, :], in_=X_[:, j, :], func=AF.Copy, scale=BETA[:, d_, c, h:h + 1]),
                                     reads=[XB_, BETAB], writes=[TIB[d_][c4]])
                    f.op(DVE, lambda e: e.memset(Oacc, 0.0), writes=[OACC] + OACCB)
                    for d_ in range(2):
                        f.op(DVE, lambda e, d_=d_: e.memset(Sst[d_][0], 0.0), writes=[Sst[d_][1]])
                        f.op(POOL, lambda e, d_=d_: e.memset(Sbf[d_][0], 0.0), writes=[Sbf[d_][1]])
                    for step in range(16):
                        stages = [[] for _ in range(7)]
                        for d_ in range(2):
                            c = step if d_ == 0 else 15 - step
                            cs = slice(c * 128, (c + 1) * 128)
                            c4 = c // 4
                            S_, SB_ = Sst[d_]
                            sb_, SBB_ = Sbf[d_]
                            r_, RB_ = Rb[d_]
                            v_, VB_ = Vn[d_]
                            bP, BP = banks[d_ * 3 + 0]
                            bQ, BQ = banks[d_ * 3 + 1]
                            bR, BR = banks[d_ * 3 + 2]

                            def mk(d_=d_, c=c, cs=cs, c4=c4, S_=S_, SB_=SB_, sb_=sb_, SBB_=SBB_, r_=r_, RB_=RB_, v_=v_, VB_=VB_, bP=bP, BP=BP, bQ=bQ, BQ=BQ, bR=bR, BR=BR, h=h):
                                st0 = lambda: f.op(PE, lambda e: e.matmul(bP[:, 0:128], lhsT=KT[:, cs], rhs=sb_, start=True, stop=True), reads=[KTB, SBB_], writes=[BP])
                                st1 = lambda: f.op(DVE, lambda e: e.scalar_tensor_tensor(out=r_, in0=bP[:, 0:128], scalar=NEGG[:, d_, c, h:h + 1], in1=Vtok[:, c, :], op0=ALU.mult, op1=ALU.add),
                                                   reads=[BP, NEGGB, VTOK_], writes=[RB_])
                                st2 = lambda: f.op(PE, lambda e: e.matmul(bP[:, 128:256], lhsT=TiTb[:, d_, c, :], rhs=r_, start=True, stop=True), reads=[TIB[d_][c4], RB_], writes=[BP])

                                def st3():
                                    f.op(ACT, lambda e: e.activation(out=v_[:, 0, :], in_=bP[:, 128:256], func=AF.Copy), reads=[BP], writes=[VB_])
                                    f.op(ACT, lambda e: e.activation(out=v_[:, 1, :], in_=bP[:, 128:256], func=AF.Copy, scale=W2[:, d_, c, h:h + 1]), reads=[BP, W2B], writes=[VB_])

                                def st4():
                                    f.op(PE, lambda e: e.matmul(bQ[:, 0:128], lhsT=QT[:, cs], rhs=sb_, start=True, stop=True), reads=[QTB, SBB_], writes=[BQ])
                                    f.op(PE, lambda e: e.matmul(bQ[:, 128:256], lhsT=QKm[:, d_, c, :], rhs=v_[:, 0, :], start=True, stop=True), reads=[QKB2[d_][c4], VB_], writes=[BQ])
                                    f.op(PE, lambda e: e.matmul(bR[:, 0:128], lhsT=Ktok[:, c, :], rhs=v_[:, 1, :], start=True, stop=True), reads=[KTOK, VB_], writes=[BR])

                                def st5():
                                    f.op(DVE, lambda e: e.scalar_tensor_tensor(out=S_, in0=S_, scalar=DEC[:, d_, c, h:h + 1], in1=bR[:, 0:128], op0=ALU.mult, op1=ALU.add),
                                         reads=[BR, DECB, SB_], writes=[SB_])
                                    f.op(DVE, lambda e: e.scalar_tensor_tensor(out=Oacc[:, c, :], in0=bQ[:, 0:128], scalar=GEX[:, d_, c, h:h + 1], in1=Oacc[:, c, :], op0=ALU.mult, op1=ALU.add),
                                         reads=[BQ, GEXB, OACCB[c]], writes=[OACCB[c]])
                                    f.op(DVE, lambda e: e.tensor_tensor(out=Oacc[:, c, :], in0=bQ[:, 128:256], in1=Oacc[:, c, :], op=ALU.add), reads=[BQ, OACCB[c]], writes=[OACCB[c]])
                                st6 = lambda: f.op(ACT, lambda e: e.activation(out=sb_, in_=S_, func=AF.Copy), reads=[SB_], writes=[SBB_])
                                return [st0, st1, st2, st3, st4, st5, st6]
                            for i_, fn_ in enumerate(mk()):
                                stages[i_].append(fn_)
                        for stg_ in stages:
                            for fn_ in stg_:
                                fn_()
                    if False:
                        dbg("gam1", gam1, GAM1, [4, 128], BF16); dbg("gam3", t3, T3b, [4, 128], F32); dbg("Nm", Nm, NM, [4, 128], BF16)
                        dbg("QT", QT, QTB, [S], BF16); dbg("KT", KT, KTB, [S], BF16); dbg("VT", VT, VTB, [S], BF16)
                        dbg("Oacc", Oacc, OACC, [16, 128], F32)
                        dbg("CG", CGf, CGB, [256], F32); dbg("W2", W2f, W2B, [256], F32); dbg("GEX", GEXf, GEXB, [256], F32)
                        dbg("DEC", DECf, DECB, [256], F32); dbg("BETA", BETAf, BETAB, [256], F32); dbg("TOT", TOTf, TOTB, [256], F32)
                        dbg("G", GT_f, GTB, [256], F32)
                        dbg("TiTb0", TiTb[:, 0, 0:4, :], TIB[0][0], [4, 128], BF16); dbg("TiTb1", TiTb[:, 1, 12:16, :], TIB[1][3], [4, 128], BF16)
                        dbg("QKm0", QKm[:, 0, 0:4, :], QKB2[0][0], [4, 128], BF16); dbg("QKm1", QKm[:, 1, 12:16, :], QKB2[1][3], [4, 128], BF16)
                        dbg("Ktok", Ktok, KTOK, [16, 128], BF16); dbg("Vtok", Vtok, VTOK_, [16, 128], BF16)
                    for c in range(16):
                        f.op(ACT, lambda e, c=c: e.activation(out=onb[:, c, :], in_=Oacc[:, c, :], func=AF.Square, accum_out=osq[:, c, 0:1]), reads=[OACC, OACCB[c]], writes=[ONB, OSQ])
                    f.op(DVE, lambda e: e.tensor_scalar(out=osq[:, :, 1:2], in0=osq[:, :, 0:1], scalar1=1.0 / 128, scalar2=EPS, op0=ALU.mult, op1=ALU.add), reads=[OSQ], writes=[OSQ])
                    f.op(ACT, lambda e: e.activation(out=osq[:, :, 2:3], in_=osq[:, :, 1:2], func=AF.Sqrt), reads=[OSQ], writes=[OSQ])
                    f.op(DVE, lambda e: e.reciprocal(out=osq[:, :, 3:4], in_=osq[:, :, 2:3]), reads=[OSQ], writes=[OSQ])
                    for c in range(16):
                        f.op(DVE, lambda e, c=c: e.scalar_tensor_tensor(out=onb[:, c, :], in0=Oacc[:, c, :], scalar=osq[:, c, 3:4], in1=dnw, op0=ALU.mult, op1=ALU.mult),
                             reads=[OACC, OACCB[c], OSQ, DNW], writes=[ONB])
                    for c4 in range(4):
                        bk, BK = banks[7]
                        for j in range(4):
                            c = c4 * 4 + j
                            f.op(PE, lambda e, bk=bk, c=c, j=j: e.transpose(bfv(bk)[:, j * 128:(j + 1) * 128], onb[:, c, :], identb), reads=[ONB, IDENTB], writes=[BK])
                        f.op(DVE, lambda e, bk=bk, c4=c4: e.tensor_tensor(out=mo2[:, c4 * 512:(c4 + 1) * 512], in0=bfv(bk)[:, 0:512], in1=zs[:, c4 * 512:(c4 + 1) * 512], op=ALU.mult),
                             reads=[BK, ZS], writes=[MO2])
                    f.dma(POOL, mixT_d.ap()[h * 128:(h + 1) * 128, c0:c0 + S], mo2, reads=[MO2], writes=[MIXT])
        else:
            z_, Z_ = ar.alloc("zeros", [NT], BF16)
            f.op(DVE, lambda e: e.memset(z_, 0.0), writes=[Z_])
            for c in range(8):
                f.dma(SP, mixT_d.ap()[c * 128:(c + 1) * 128, :], z_, reads=[Z_], writes=[MIXT])
        f.barrier()
        ar.reset(m1)
        cstb, CSTB = ar.alloc("cstb", [NCST], BF16)
        f.op(DVE, lambda e: e.tensor_copy(out=cstb, in_=cst), reads=[CST], writes=[CSTB])
        Rm = cstb[0:32, 128:160]
        maskA, maskB, maskAc = cstb[:, 192:320], cstb[:, 320:448], cstb[0:64, 448:576]
        onesb, ONESB = ar.alloc("onesb", [128], BF16)
        f.op(DVE, lambda e: e.memset(onesb, 1.0), writes=[ONESB])
        posi, POSI = ar.alloc("posi", [S], I32)
        ang, ANG = ar.alloc("ang", [S], F32)
        tmpa, TMPA = ar.alloc("tmpa", [S], F32)
        tmpi, TMPI = posi, POSI
        cosT, COS = ar.alloc("cosT", [S], F32)
        sinT, SIN = ar.alloc("sinT", [S], F32)
        qk = [[[ar.alloc("qk%d_%d_%d" % (sl, g, i), [S], BF16, buf=False) for i in range(2)] for g in range(3)] for sl in range(2)]
        QKB = [[[[f.buf(qk[sl][g][i][:, t * 512:(t + 1) * 512], "qkb") for t in range(4)] for i in range(2)] for g in range(3)] for sl in range(2)]
        DIL = [1, 4, 16]
        vh = [[ar.alloc("vh%d_%d" % (sl, g), [16, 128], BF16) for g in range(3)] for sl in range(1)] * 2
        v0 = [[ar.alloc("v0%d_%d" % (sl, g), [16, 128], BF16) for g in range(3)] for sl in range(1)] * 2
        accn, ACCN = ar.alloc("accn", [S], F32)
        accd, ACCD = ar.alloc("accd", [S], F32)
        mo = [ar.alloc("mo%d" % i, [S], BF16) for i in range(2)]
        rt = [ar.alloc("rt%d" % i, [2, 512], F32) for i in range(2)]
        pb = [ar.alloc("pb%d" % i, [2, 128], BF16) for i in range(4)]
        PI = float(np.pi)
        pit, PIT = ar.alloc("pit", [S], F32)
        f.op(DVE, lambda e: e.memset(pit[0:32, :], PI), writes=[PIT])
        it = 0
        for s_i in range(NS):
            f.dma(SP, posi[0:32, :], pos_d.ap()[s_i:s_i + 1, :].to_broadcast([32, S]), reads=[SMALL], writes=[POSI])
            f.op(DVE, lambda e: e.tensor_copy(out=ang[0:32, :], in_=posi[0:32, :]), reads=[POSI], writes=[ANG])
            f.op(DVE, lambda e: e.tensor_scalar(out=ang[0:32, :], in0=ang[0:32, :], scalar1=cst[0:32, 160:161], scalar2=None, op0=ALU.mult), reads=[ANG, CST], writes=[ANG])
            f.op(DVE, lambda e: e.tensor_scalar(out=tmpi[0:32, :], in0=ang[0:32, :], scalar1=1.0 / (2 * PI), scalar2=None, op0=ALU.mult), reads=[ANG], writes=[TMPI])
            f.op(DVE, lambda e: e.tensor_copy(out=tmpa[0:32, :], in_=tmpi[0:32, :]), reads=[TMPI], writes=[TMPA])
            f.op(DVE, lambda e: e.scalar_tensor_tensor(out=ang[0:32, :], in0=tmpa[0:32, :], scalar=-2 * PI, in1=ang[0:32, :], op0=ALU.mult, op1=ALU.add), reads=[TMPA, ANG], writes=[ANG])
            f.op(DVE, lambda e: e.tensor_tensor(out=tmpa[0:32, :], in0=ang[0:32, :], in1=pit[0:32, :], op=ALU.is_gt), reads=[ANG, PIT], writes=[TMPA])
            f.op(DVE, lambda e: e.scalar_tensor_tensor(out=ang[0:32, :], in0=tmpa[0:32, :], scalar=-2 * PI, in1=ang[0:32, :], op0=ALU.mult, op1=ALU.add), reads=[ANG, TMPA], writes=[ANG])
            f.op(ACT, lambda e: e.activation(out=sinT[0:32, :], in_=ang[0:32, :], func=AF.Sin), reads=[ANG], writes=[SIN])
            f.op(DVE, lambda e: e.scalar_tensor_tensor(out=tmpa[0:32, :], in0=ang[0:32, :], scalar=-1.0, in1=ang[0:32, :], op0=ALU.mult, op1=ALU.max), reads=[ANG], writes=[TMPA])
            f.op(ACT, lambda e: e.activation(out=cosT[0:32, :], in_=tmpa[0:32, :], func=AF.Sin, scale=-1.0, bias=cst[0:32, 161:162]), reads=[TMPA, CST], writes=[COS])
            for h in range(8):
                sl = it % 2
                it += 1
                c0 = s_i * S
                for g in range(3):
                    for i in range(2):
                        ch = 32 + i * 24 + g * 8 + h
                        for t in range(4):
                            f.dma(SP, qk[sl][g][i][:, t * 512:(t + 1) * 512], projT_d.ap()[ch * 128:(ch + 1) * 128, c0 + t * 512:c0 + (t + 1) * 512],
                                  reads=[PROJT], writes=[QKB[sl][g][i][t]])
                    d = DIL[g]
                    L = S // d
                    M = L // 128
                    vcol = g * 1024 + h * 128
                    view = vtok_d.ap()[c0:c0 + S, vcol:vcol + 128].rearrange("(i d) c -> d i c", d=d)
                    vh_, VH_ = vh[sl][g]
                    v0_, V0_ = v0[sl][g]
                    vhv = vh_.rearrange("p (r m) c -> p r m c", m=M)
                    for r in range(d):
                        if M > 1:
                            f.dma(SP, vhv[:, r, 0:M - 1, :], view[r, 64:L - 64, :].rearrange("(m j) c -> j m c", j=128), reads=[VTOK], writes=[VH_])
                    f.dma(SP, vhv[0:64, :, M - 1, :], view[:, L - 64:L, :].rearrange("r j c -> j r c"), reads=[VTOK], writes=[VH_])
                    f.dma(SP, v0_[0:64, 0:d, :], view[:, 0:64, :].rearrange("r j c -> j r c"), reads=[VTOK], writes=[V0_])
                for g in range(3):
                    for i in range(2):
                        X = qk[sl][g][i]
                        for t in range(4):
                            XB = QKB[sl][g][i][t]
                            cs = slice(t * 512, (t + 1) * 512)
                            bk, BK = next_bank(0, 4)
                            r_, R_ = rt[(g * 8 + i * 4 + t) % 2]
                            f.op(PE, lambda e, bk=bk, X=X, cs=cs: e.matmul(bk[0:32, :], lhsT=Rm, rhs=X[0:32, cs], start=True, stop=True), reads=[CSTB, XB], writes=[BK])
                            f.op(DVE, lambda e, r_=r_, X=X, cs=cs: e.tensor_tensor(out=r_[0:32, 0, :], in0=X[0:32, cs], in1=cosT[0:32, cs], op=ALU.mult), reads=[XB, COS], writes=[R_])
                            f.op(DVE, lambda e, r_=r_, bk=bk, cs=cs: e.tensor_tensor(out=r_[0:32, 1, :], in0=bk[0:32, :], in1=sinT[0:32, cs], op=ALU.mult), reads=[BK, SIN, R_], writes=[R_])
                            f.op(DVE, lambda e, r_=r_, X=X, cs=cs: e.tensor_tensor(out=X[0:32, cs], in0=r_[0:32, 0, :], in1=r_[0:32, 1, :], op=ALU.add), reads=[R_], writes=[XB])
                pendB = []
                un = 0
                for g in range(3):
                    d = DIL[g]
                    L = S // d
                    M = L // 128
                    Q, K = qk[sl][g][0], qk[sl][g][1]
                    QB_, KB_ = QKB[sl][g][0], QKB[sl][g][1]
                    vh_, VH_ = vh[sl][g]
                    v0_, V0_ = v0[sl][g]
                    vhv = vh_.rearrange("p (r m) c -> p r m c", m=M)
                    units = [(r, qb) for r in range(d) for qb in range(M)]
                    for u0 in range(0, len(units), 4):
                        nb_, NB_ = banks[4 + (un // 4) % 2]
                        db_, DB_ = banks[6 + (un // 4) % 2]
                        for j in range(4):
                            r, qb = units[u0 + j]
                            un += 1
                            sb_, SB_ = banks[un % 4]
                            p_, P_ = pb[un % 4]
                            qsl = slice(r + d * 128 * qb, r + d * 128 * qb + d * 127 + 1, d)
                            blocks = []
                            if qb >= 1:
                                blocks.append((128 * qb - 64, 128, maskA, vhv[:, r, qb - 1, :], VH_))
                            else:
                                blocks.append((0, 64, maskAc, v0_[0:64, r, :], V0_))
                            if qb < M - 1:
                                blocks.append((128 * qb + 64, 128, maskB, vhv[:, r, qb, :], VH_))
                            else:
                                blocks.append((128 * qb + 64, 64, maskB[0:64, :], vhv[0:64, r, qb, :], VH_))
                            for bi, (k0, nk, mk, vap, VB_) in enumerate(blocks):
                                ksl = slice(r + d * k0, r + d * k0 + d * (nk - 1) + 1, d)
                                f.op(PE, lambda e, sb_=sb_, K=K, Q=Q, ksl=ksl, qsl=qsl, nk=nk, bi=bi: e.matmul(
                                    sb_[0:nk, bi * 128:(bi + 1) * 128], lhsT=K[:, ksl], rhs=Q[:, qsl], start=True, stop=True),
                                    reads=KB_ + QB_, writes=[SB_])
                                f.op(ACT, lambda e, sb_=sb_, p_=p_, nk=nk, bi=bi: e.activation(
                                    out=p_[0:nk, bi, :], in_=sb_[0:nk, bi * 128:(bi + 1) * 128], func=AF.Exp, scale=128.0 ** -0.5),
                                    reads=[SB_], writes=[P_])
                                f.op(POOL, lambda e, p_=p_, nk=nk, bi=bi, mk=mk: e.tensor_tensor(out=p_[0:nk, bi, :], in0=p_[0:nk, bi, :], in1=mk, op=ALU.mult),
                                     reads=[P_, CSTB], writes=[P_])
                            def stB(blocks=blocks, nb_=nb_, NB_=NB_, db_=db_, DB_=DB_, p_=p_, P_=P_, j=j, g=g, u0=u0, units=units):
                                for bi, (k0, nk, mk, vap, VB_) in enumerate(blocks):
                                    f.op(PE, lambda e, vap=vap, nk=nk, bi=bi: e.matmul(
                                        nb_[:, j * 128:(j + 1) * 128], lhsT=vap, rhs=p_[0:nk, bi, :], start=(bi == 0), stop=(bi == 1)),
                                        reads=[P_, VB_], writes=[NB_])
                                for bi, (k0, nk, mk, vap, VB_) in enumerate(blocks):
                                    f.op(PE, lambda e, nk=nk, bi=bi: e.matmul(
                                        db_[:, j * 128:(j + 1) * 128], lhsT=onesb[0:nk, :], rhs=p_[0:nk, bi, :], start=(bi == 0), stop=(bi == 1)),
                                        reads=[P_, ONESB], writes=[DB_])
                                if j != 3:
                                    return
                                r, qb = units[u0]
                                if g == 0:
                                    sl_ = slice(qb * 128, qb * 128 + 512)
                                    f.op(DVE, lambda e: e.tensor_copy(out=accn[:, sl_], in_=nb_), reads=[NB_], writes=[ACCN])
                                    f.op(DVE, lambda e: e.tensor_copy(out=accd[:, sl_], in_=db_), reads=[DB_], writes=[ACCD])
                                else:
                                    if g == 1:
                                        av = lambda a: a[:, r:r + 4 * 511 + 1:4]
                                        iv = lambda b: b
                                    else:
                                        av = lambda a: a.rearrange("p (i r) -> p r i", r=16)[:, r:r + 4, :]
                                        iv = lambda b: b.rearrange("p (r i) -> p r i", r=4)
                                    f.op(DVE, lambda e: e.tensor_tensor(out=av(accn), in0=iv(nb_), in1=av(accn), op=ALU.add), reads=[NB_, ACCN], writes=[ACCN])
                                    f.op(DVE, lambda e: e.tensor_tensor(out=av(accd), in0=iv(db_), in1=av(accd), op=ALU.add), reads=[DB_, ACCD], writes=[ACCD])
                            if pendB:
                                pendB.pop()()
                            pendB.append(stB)
                if pendB:
                    pendB.pop()()
                m_, M_ = mo[sl]
                f.op(DVE, lambda e: e.reciprocal(out=accd, in_=accd), reads=[ACCD], writes=[ACCD])
                f.op(DVE, lambda e, m_=m_: e.tensor_tensor(out=m_, in0=accn, in1=accd, op=ALU.mult), reads=[ACCN, ACCD], writes=[M_])
                f.dma(POOL, mixT_d.ap()[(8 + h) * 128:(9 + h) * 128, c0:c0 + S], m_, reads=[M_], writes=[MIXT])

    f.barrier()
    ar.reset(m1)
    wo, WO = ar.alloc("wo", [16, D], BF16)
    for q in range(4):
        f.dma(SP, wo[:, q * 4:(q + 1) * 4, :], woutb_d.ap()[q * 512:(q + 1) * 512, :].rearrange("(k p) n -> p k n", p=128), reads=[WOUTB], writes=[WO])
    mx = [ar.alloc("mx%d" % i, [16, 512], BF16) for i in range(2)]
    xr = [ar.alloc("xr%d" % i, [D], F32) for i in range(2)]
    hb = [ar.alloc("hb%d" % i, [D], F32) for i in range(2)]
    for tile in range(NT // 512):
        t0 = tile * 512
        m_, M_ = mx[tile % 2]
        f.dma(SP, m_, mixT_d.ap()[:, t0:t0 + 512].rearrange("(k p) t -> p k t", p=128), reads=[MIXT], writes=[M_])
        for b in range(4):
            r0 = t0 + b * 128
            x_, X_ = xr[b % 2]
            h_, H_ = hb[b % 2]
            f.dma(SP, x_, x_d.ap()[r0:r0 + 128, :], reads=[Xd], writes=[X_])
            for ds in range(4):
                bk, BK = next_bank(2, 8)
                for kc in range(16):
                    f.op(PE, lambda e, bk=bk, m_=m_, kc=kc, b=b, ds=ds: e.matmul(
                        bk, lhsT=m_[:, kc, b * 128:(b + 1) * 128], rhs=wo[:, kc, ds * 512:(ds + 1) * 512], start=(kc == 0), stop=(kc == 15)),
                        reads=[M_, WO], writes=[BK])
                f.op(DVE, lambda e, bk=bk, h_=h_, x_=x_, ds=ds: e.tensor_tensor(
                    out=h_[:, ds * 512:(ds + 1) * 512], in0=bk, in1=x_[:, ds * 512:(ds + 1) * 512], op=ALU.add),
                    reads=[BK, X_], writes=[H_])
            f.dma(POOL, h1_d.ap()[r0:r0 + 128, :], h_, reads=[H_], writes=[H1])

    if upto <= 3:
        f.barrier(); f.emit(); f.close()
        return nc
    f.barrier()
    ar.reset(m1)
    TT4 = 512
    norm4 = make_norm("n4", 2)
    n2T = ar.alloc("n2T", [16, TT4 + 2], BF16, buf=False)
    N2B = [f.buf(n2T[:, kc, :], "n2T_%d" % kc) for kc in range(16)]
    actT = ar.alloc("actT", [NFC, TT4], BF16, buf=False)
    ACTB = [f.buf(actT[:, c, :], "actT_%d" % c) for c in range(NFC)]
    wu = [ar.alloc("wu%d" % i, [16, 2, 256], BF16) for i in range(2)]
    wd = [ar.alloc("wd%d" % i, [11, 512], BF16) for i in range(2)]
    ug = [ar.alloc("ug%d" % i, [TT4 + 2], F32) for i in range(4)]
    ag = [ar.alloc("ag%d" % i, [TT4], F32) for i in range(4)]
    hq = [ar.alloc("hq%d" % i, [512], F32) for i in range(2)]
    oq = [ar.alloc("oq%d" % i, [512], F32) for i in range(2)]
    halo_rr = [0]
    wu_i = 0
    wd_i = 0
    oq_i = 0
    for tile in range(NT // TT4):
        t0 = tile * TT4
        tl = t0 % S
        blocks = [(h1_d.ap()[t0 + b * 128:t0 + (b + 1) * 128, :], 128, 1 + b * 128, 0, H1) for b in range(TT4 // 128)]
        left = h1_d.ap()[t0 - 1:t0, :] if tl > 0 else None
        right = h1_d.ap()[t0 + TT4:t0 + TT4 + 1, :] if tl + TT4 < S else None
        blocks.append(([left, right], 2, 0, TT4 + 1, H1))
        norm4(blocks, nfw, NFW, n2T, N2B)
        for slab in range(NFC // 2):
            w_, W_ = wu[wu_i % 2]
            wu_i += 1
            f.dma(SP, w_.rearrange("p k g n -> p (k g n)"), wfit_d.ap()[slab], reads=[WFIT], writes=[W_])
            for pj in range(2):
                fc = slab * 2 + pj
                accs = []
                for gv in range(2):
                    bk, BK = next_bank(2, 6)
                    hk, HK = banks[6 + halo_rr[0] % 2]
                    halo_rr[0] += 1
                    for kc in range(16):
                        f.op(PE, lambda e, bk=bk, w_=w_, kc=kc, gv=gv, pj=pj: e.matmul(
                            bk, lhsT=w_[:, kc, gv, pj * 128:(pj + 1) * 128], rhs=n2T[:, kc, 1:TT4 + 1], start=(kc == 0), stop=(kc == 15)),
                            reads=[W_, N2B[kc]], writes=[BK])
                    for kc in range(16):
                        f.op(PE, lambda e, hk=hk, w_=w_, kc=kc, gv=gv, pj=pj: e.matmul(
                            hk[:, 0:2], lhsT=w_[:, kc, gv, pj * 128:(pj + 1) * 128], rhs=n2T[:, kc, 0:TT4 + 2:TT4 + 1], start=(kc == 0), stop=(kc == 15)),
                            reads=[W_, N2B[kc]], writes=[HK])
                    u_, U_ = ug[(fc * 2 + gv) % 4]
                    a_, A_ = ag[(fc * 2 + gv) % 4]
                    f.op(ACT, lambda e, u_=u_, bk=bk: e.activation(out=u_[:, 1:TT4 + 1], in_=bk, func=AF.Copy), reads=[BK], writes=[U_])
                    f.op(ACT, lambda e, u_=u_, hk=hk: e.activation(out=u_[:, 0:TT4 + 2:TT4 + 1], in_=hk[:, 0:2], func=AF.Copy), reads=[HK], writes=[U_])
                    ch = gv * NFC + fc
                    f.op(DVE, lambda e, u_=u_, a_=a_, ch=ch: e.tensor_scalar(out=a_, in0=u_[:, 1:TT4 + 1], scalar1=cffn[1][0][:, ch:ch + 1], scalar2=None, op0=ALU.mult),
                         reads=[U_, cffn[1][1]], writes=[A_])
                    f.op(DVE, lambda e, u_=u_, a_=a_, ch=ch: e.scalar_tensor_tensor(out=a_, in0=u_[:, 0:TT4], scalar=cffn[0][0][:, ch:ch + 1], in1=a_, op0=ALU.mult, op1=ALU.add),
                         reads=[U_, cffn[0][1], A_], writes=[A_])
                    f.op(DVE, lambda e, u_=u_, a_=a_, ch=ch: e.scalar_tensor_tensor(out=a_, in0=u_[:, 2:TT4 + 2], scalar=cffn[2][0][:, ch:ch + 1], in1=a_, op0=ALU.mult, op1=ALU.add),
                         reads=[U_, cffn[2][1], A_], writes=[A_])
                    accs.append((a_, A_))
                (a_g, A_G), (a_v, A_V) = accs
                f.op(ACT, lambda e, a_g=a_g: e.activation(out=a_g, in_=a_g, func=AF.Silu), reads=[A_G], writes=[A_G])
                f.op(DVE, lambda e, a_g=a_g, a_v=a_v, fc=fc: e.tensor_tensor(out=actT[:, fc, :], in0=a_g, in1=a_v, op=ALU.mult),
                     reads=[A_G, A_V], writes=[ACTB[fc]])
        for ds in range(4):
            obanks = [banks[2 + b] for b in range(4)]
            for q in range(4):
                w_, W_ = wd[wd_i % 2]
                wd_i += 1
                f.dma(SP, w_.rearrange("p k n -> p (k n)"), wfot_d.ap()[ds * 4 + q], reads=[WFOT], writes=[W_])
                for b in range(4):
                    bk, BK = obanks[b]
                    for i in range(11):
                        fc = q * 11 + i
                        f.op(PE, lambda e, bk=bk, w_=w_, i=i, fc=fc, b=b, q=q: e.matmul(
                            bk, lhsT=actT[:, fc, b * 128:(b + 1) * 128], rhs=w_[:, i, :], start=(q == 0 and i == 0), stop=(q == 3 and i == 10)),
                            reads=[W_, ACTB[fc]], writes=[BK])
            for b in range(4):
                bk, BK = obanks[b]
                r0 = t0 + b * 128
                h_, H_ = hq[oq_i % 2]
                o_, O_ = oq[oq_i % 2]
                oq_i += 1
                f.dma(SP, h_, h1_d.ap()[r0:r0 + 128, ds * 512:(ds + 1) * 512], reads=[H1], writes=[H_])
                f.op(DVE, lambda e, bk=bk, h_=h_, o_=o_: e.tensor_tensor(out=o_, in0=bk, in1=h_, op=ALU.add), reads=[BK, H_], writes=[O_])
                f.dma(POOL, out_d.ap()[r0:r0 + 128, ds * 512:(ds + 1) * 512], o_, reads=[O_], writes=[OUT])

    if upto <= 4:
        f.barrier(); f.emit(); f.close()
        return nc
    f.barrier()
    ar.reset(m1)
    fx = [ar.alloc("fx%d" % i, [D], F32) for i in range(3)]
    fy = [ar.alloc("fy%d" % i, [D], F32) for i in range(2)]
    fj, FJ = ar.alloc("fj", [D], BF16)
    fs = [ar.alloc("fs%d" % i, [4], F32) for i in range(2)]
    for b in range(NT // 128):
        r0 = b * 128
        x_, X_ = fx[b % 3]
        y_, Y_ = fy[b % 2]
        s_, S_ = fs[b % 2]
        f.dma(SP, x_, out_d.ap()[r0:r0 + 128, :], reads=[OUT], writes=[X_])
        f.op(ACT, lambda e, x_=x_, s_=s_: e.activation(out=fj, in_=x_, func=AF.Square, accum_out=s_[:, 0:1]), reads=[X_], writes=[FJ, S_])
        f.op(DVE, lambda e, s_=s_: e.tensor_scalar(out=s_[:, 1:2], in0=s_[:, 0:1], scalar1=1.0 / D, scalar2=EPS, op0=ALU.mult, op1=ALU.add), reads=[S_], writes=[S_])
        f.op(ACT, lambda e, s_=s_: e.activation(out=s_[:, 2:3], in_=s_[:, 1:2], func=AF.Sqrt), reads=[S_], writes=[S_])
        f.op(DVE, lambda e, s_=s_: e.reciprocal(out=s_[:, 3:4], in_=s_[:, 2:3]), reads=[S_], writes=[S_])
        f.op(DVE, lambda e, s_=s_, x_=x_, y_=y_: e.scalar_tensor_tensor(out=y_, in0=x_, scalar=s_[:, 3:4], in1=nfin, op0=ALU.mult, op1=ALU.mult),
             reads=[S_, X_, NFIN], writes=[Y_])
        f.dma(SP, out_d.ap()[r0:r0 + 128, :], y_, reads=[Y_], writes=[OUT])
    f.barrier()
    f.emit()
    f.close()
    return nc


_NC_CACHE = {}


def _get_nc(debug=False, stub_mixers=False):
    key = (debug, stub_mixers)
    if key not in _NC_CACHE:
        _NC_CACHE[key] = build(debug=debug, stub_mixers=stub_mixers)
    return _NC_CACHE[key]


def make_in_maps(inputs, ncores=8):
    g = lambda k: np.ascontiguousarray(np.asarray(inputs[k]))
    x = g("x").astype(np.float32, copy=False)
    pos = g("positions").astype(np.int32, copy=False)
    shared = {
        "norm_mix_w": g("norm_mix_w").reshape(16, 128),
        "w_in": g("w_in").reshape(D, PW),
        "conv_qkv_w": g("conv_qkv_w").reshape(5, 24, 128),
        "a_log_fwd": g("a_log_fwd").reshape(1, 8), "a_log_bwd": g("a_log_bwd").reshape(1, 8),
        "dt_bias_fwd": g("dt_bias_fwd").reshape(1, 8), "dt_bias_bwd": g("dt_bias_bwd").reshape(1, 8),
        "delta_norm_w": g("delta_norm_w").reshape(1, 128),
        "w_out": g("w_out").reshape(D, D),
        "norm_ffn_w": g("norm_ffn_w").reshape(16, 128),
        "w_ffn_in": g("w_ffn_in").reshape(D, 2 * DFF),
        "conv_ffn_w": g("conv_ffn_w").reshape(3, 88, 128),
        "w_ffn_out": g("w_ffn_out").reshape(DFF, D),
        "norm_final_w": g("norm_final_w").reshape(1, D),
        "cst": make_cst(),
    }
    shared = {k: np.ascontiguousarray(v, dtype=np.float32) for k, v in shared.items()}
    maps = []
    for c in range(ncores):
        m = dict(shared)
        m["x"] = np.ascontiguousarray(x[c * NS:(c + 1) * NS].reshape(NT, D))
        m["pos"] = np.ascontiguousarray(pos[c * NS:(c + 1) * NS])
        maps.append(m)
    return maps


def kernel(**inputs):
    nc = _get_nc()
    maps = make_in_maps(inputs)
    res = run_bass_kernel_spmd(nc, maps, core_ids=list(range(8)))
    out = np.concatenate([np.asarray(r["out"]).reshape(NS, S, D) for r in res.results], axis=0)
    return out.astype(np.float32, copy=False)
```

```python
import contextlib
import numpy as np
import concourse.bass as bass
import concourse.mybir as mybir
from concourse.bass_utils import run_bass_kernel_spmd

F32 = mybir.dt.float32
BF16 = mybir.dt.bfloat16
I32 = mybir.dt.int32
AF = mybir.ActivationFunctionType
ALU = mybir.AluOpType

PE, ACT, DVE, POOL, SP = "tensor", "scalar", "vector", "gpsimd", "sync"
ENGS = (PE, ACT, DVE, POOL, SP)

S = 2048
D = 2048
NS = 2
NT = NS * S
PW = 13344
DFF = 5632
NFC = DFF // 128
EPS = 1e-6
COL_Z = 3072
COL_G = 4096
COL_BQ = 4128
COL_BK = 7200
COL_BV = 10272
NFEAT = 80
NCST = 1216


def make_cst():
    c = np.zeros((128, NCST), np.float32)
    c[:, 0:128] = np.eye(128)
    for p in range(16):
        c[p + 16, 128 + p] = -1.0
        c[p, 128 + 16 + p] = 1.0
    inv = 500000.0 ** (-np.arange(0, 32, 2, dtype=np.float32) / 32.0)
    c[0:16, 160] = inv
    c[16:32, 160] = inv
    c[:, 161] = np.pi / 2
    c[:, 162] = EPS
    k = np.arange(128)[:, None]
    q = np.arange(128)[None, :]
    c[:, 192:320] = (k >= q)
    c[:, 320:448] = (k <= q)
    c[0:64, 448:576] = (k[0:64] + 64 >= q)
    BIG = 30000.0
    c[:, 576:704] = np.where(q < k, 0.0, BIG)
    c[:, 704:832] = np.where(q >= k, 0.0, -BIG)
    c[:, 832:960] = np.where(q > k, 0.0, BIG)
    c[:, 960:1088] = np.where(q <= k, 0.0, -BIG)
    c[:, 1088:1216] = (k <= q)
    return c


class Buf:
    def __init__(self, ap, name, dram=False):
        self.ap = ap
        self.name = name
        self.dram = dram
        self.w = {}
        self.r = {}
        self.dsem = None
        self.dcnt = 0


class FW:
    def __init__(self, nc):
        self.nc = nc
        self.es = contextlib.ExitStack()
        self.sem = {}
        self.cnt = {e: 0 for e in ENGS}
        self.ops = {e: [] for e in ENGS}
        self.waited = {e: {} for e in ENGS}
        self.allsems = {}
        for e in (PE, ACT, DVE, POOL):
            self.sem[e] = self.new_sem("e_" + e)
            self.allsems[e] = [self.sem[e], 0]
        self.nbuf = 0
        self.dfree = []
        self.dbufs = []
        self.nds = 0

    def get_dsem(self):
        if self.dfree:
            return self.dfree.pop()
        self.nds += 1
        key = "ds%d" % self.nds
        ent = [self.new_sem(key), 0, key]
        self.allsems[key] = ent
        return ent

    def new_sem(self, name):
        return self.es.enter_context(self.nc.semaphore(name))

    def sbuf(self, name, shape, dtype):
        return self.es.enter_context(self.nc.sbuf_tensor(name, list(shape), dtype))

    def psum(self, name, shape, dtype):
        return self.es.enter_context(self.nc.psum_tensor(name, list(shape), dtype))

    def buf(self, ap, name=None, dram=False):
        self.nbuf += 1
        return Buf(ap, (name or "b") + "_%d" % self.nbuf, dram=dram)

    def _deps(self, eng, reads, writes, skip_key=None):
        need = {}

        def add(key, sem, val):
            if key == skip_key:
                return
            if key not in need or need[key][1] < val:
                need[key] = (sem, val)

        for b in reads:
            for key, (sem, val) in b.w.items():
                add(key, sem, val)
        for b in writes:
            for key, (sem, val) in b.w.items():
                if eng == PE and key == PE:
                    continue
                add(key, sem, val)
            for key, (sem, val) in b.r.items():
                add(key, sem, val)
        out = []
        wd = self.waited[eng]
        for key, (sem, val) in need.items():
            if wd.get(key, 0) >= val:
                continue
            wd[key] = val
            out.append((sem, val))
        return out

    def op(self, eng, fn, reads=(), writes=()):
        waits = self._deps(eng, reads, writes)
        self.cnt[eng] += 1
        c = self.cnt[eng]
        sem = self.sem[eng]
        self.allsems[eng][1] = c
        self.ops[eng].append((waits, fn, sem, 1))
        for b in reads:
            b.r[eng] = (sem, c)
        for b in writes:
            b.w = {eng: (sem, c)}
            b.r = {}

    def dma(self, q, out_ap, in_ap, reads=(), writes=(), **kw):
        owner = None
        for b in list(writes) + list(reads):
            if not b.dram:
                owner = b
                break
        if owner is None:
            owner = writes[0]
        if owner.dsem is None:
            owner.dsem = self.get_dsem()
            self.dbufs.append(owner)
        ent = owner.dsem
        key = ent[2]
        waits = self._deps(q, reads, writes, skip_key=key)
        ent[1] += 16
        sem, c = ent[0], ent[1]

        def fn(e, out_ap=out_ap, in_ap=in_ap, kw=kw):
            return e.dma_start(out=out_ap, in_=in_ap, **kw)

        self.ops[q].append((waits, fn, sem, 16))
        for b in reads:
            b.r[key] = (sem, c)
        for b in writes:
            if b.dram:
                b.w[key] = (sem, c)
            else:
                b.w = {key: (sem, c)}
                b.r = {}

    def barrier(self, engs=ENGS):
        for e in engs:
            waits = []
            wd = self.waited[e]
            for key, ent in self.allsems.items():
                sem, val = ent[0], ent[1]
                if val > 0 and wd.get(key, 0) < val:
                    wd[key] = val
                    waits.append((sem, val))
            if waits:
                self.ops[e].append((waits, None, None, 0))
        if tuple(engs) == tuple(ENGS):
            for b in self.dbufs:
                self.dfree.append(b.dsem)
                b.dsem = None
            self.dbufs = []

    def emit(self):
        print("ops per engine:", {e: len(v) for e, v in self.ops.items()}, "dsems", self.nds, flush=True)
        with self.nc.Block() as block:
            def mk(eng):
                def body(e):
                    for waits, fn, sem, inc in self.ops[eng]:
                        for (s, v) in waits:
                            e.wait_ge(s, v)
                        if fn is not None:
                            fn(e).then_inc(sem, inc)
                return body
            block.tensor(mk(PE))
            block.scalar(mk(ACT))
            block.vector(mk(DVE))
            block.gpsimd(mk(POOL))
            block.sync(mk(SP))

    def close(self):
        self.es.close()


class Arena:
    def __init__(self, f, nbytes):
        self.f = f
        self.n = nbytes // 2
        self.t = f.sbuf("arena", [128, self.n], BF16)
        self.off = 0

    def mark(self):
        return self.off

    def reset(self, m):
        self.off = m

    def alloc(self, name, free_shape, dtype, buf=True):
        nel = int(np.prod(free_shape))
        units = nel * (2 if dtype in (F32, I32) else 1)
        self.off = (self.off + 7) // 8 * 8
        assert self.off + units <= self.n, ("arena overflow", name, self.off, units, self.n)
        ap = self.t[:, self.off:self.off + units]
        self.off += units
        if dtype != BF16:
            ap = ap.bitcast(dtype)
        if len(free_shape) == 2:
            ap = ap.rearrange("p (a b) -> p a b", b=free_shape[1])
        elif len(free_shape) == 3:
            ap = ap.rearrange("p (a b c) -> p a b c", b=free_shape[1], c=free_shape[2])
        if buf:
            return ap, self.f.buf(ap, name)
        return ap


def build(debug=False, stub_mixers=False, upto=99, skip_delta=False):
    nc = bass.Bass("TRN2", target_bir_lowering=False)
    f = FW(nc)

    def din(name, shape, dt=F32):
        return nc.dram_tensor(name, list(shape), dt, kind="ExternalInput")

    def dscr(name, shape, dt):
        return nc.dram_tensor(name, list(shape), dt, kind=("ExternalOutput" if debug else "Internal"))

    x_d = din("x", [NT, D])
    pos_d = din("pos", [NS, S], I32)
    nmw_d = din("norm_mix_w", [16, 128])
    win_d = din("w_in", [D, PW])
    cqkv_d = din("conv_qkv_w", [5, 24, 128])
    alf_d = din("a_log_fwd", [1, 8]); alb_d = din("a_log_bwd", [1, 8])
    dtf_d = din("dt_bias_fwd", [1, 8]); dtb_d = din("dt_bias_bwd", [1, 8])
    dnw_d = din("delta_norm_w", [1, 128])
    wout_d = din("w_out", [D, D])
    nfw_d = din("norm_ffn_w", [16, 128])
    wfi_d = din("w_ffn_in", [D, 2 * DFF])
    cffn_d = din("conv_ffn_w", [3, 88, 128])
    wfo_d = din("w_ffn_out", [DFF, D])
    nfin_d = din("norm_final_w", [1, D])
    cst_d = din("cst", [128, NCST])
    out_d = nc.dram_tensor("out", [NT, D], F32, kind="ExternalOutput")

    winb_d = nc.dram_tensor("w_in_bf", [D, PW], BF16, kind="Internal")
    woutb_d = nc.dram_tensor("w_out_bf", [D, D], BF16, kind="Internal")
    wfib_d = nc.dram_tensor("w_ffn_in_bf", [D, 2 * DFF], BF16, kind="Internal")
    wfob_d = nc.dram_tensor("w_ffn_out_bf", [DFF, D], BF16, kind="Internal")
    wfit_d = nc.dram_tensor("w_ffn_in_tiled", [NFC // 2, 128, 16 * 2 * 256], BF16, kind="Internal")
    wfot_d = nc.dram_tensor("w_ffn_out_tiled", [16, 128, 11 * 512], BF16, kind="Internal")
    projT_d = dscr("projT", [NFEAT * 128, NT], BF16)
    vtok_d = dscr("vtok", [NT, 3072], BF16)
    gates_d = dscr("gates", [128, NT // 128, 32], F32)
    mixT_d = dscr("mixT", [D, NT], BF16)
    h1_d = dscr("h1", [NT, D], F32)

    B = lambda t, n: f.buf(t.ap(), n, dram=True)
    Xd, WIN, WOUT, WFI, WFO = B(x_d, "x"), B(win_d, "win"), B(wout_d, "wout"), B(wfi_d, "wfi"), B(wfo_d, "wfo")
    WINB, WOUTB, WFIB, WFOB = B(winb_d, "winb"), B(woutb_d, "woutb"), B(wfib_d, "wfib"), B(wfob_d, "wfob")
    WFIT, WFOT = B(wfit_d, "wfit"), B(wfot_d, "wfot")
    PROJT, VTOK, GATES, MIXT, H1, OUT = B(projT_d, "projT"), B(vtok_d, "vtok"), B(gates_d, "gates"), B(mixT_d, "mixT"), B(h1_d, "h1"), B(out_d, "out")
    SMALL = f.buf(None, "small", dram=True)

    dbg_list = []

    def dbg(name, ap, BUF, shape, dt):
        if not debug:
            return
        t = nc.dram_tensor("dbg_" + name, [128] + list(shape), dt, kind="ExternalOutput")
        DB = f.buf(t.ap(), "dbg_" + name, dram=True)
        f.dma(SP, t.ap(), ap, reads=[BUF], writes=[DB])

    ar = Arena(f, 190 * 1024)
    banks = []
    for i in range(8):
        t = f.psum("bank%d" % i, [128, 512], F32)
        banks.append((t[:], f.buf(t[:], "bank%d" % i)))
    bank_rr = [0]

    def next_bank(lo=0, hi=8):
        i = lo + bank_rr[0] % (hi - lo)
        bank_rr[0] += 1
        return banks[i]

    cst, CST = ar.alloc("cst", [NCST], F32)
    f.dma(SP, cst, cst_d.ap(), reads=[SMALL], writes=[CST])
    ident, IDENT = cst[:, 0:128], CST
    identb, IDENTB = ar.alloc("identb", [128], BF16)
    f.op(DVE, lambda e: e.tensor_copy(out=identb, in_=ident), reads=[IDENT], writes=[IDENTB])
    gates_sb, GSB = ar.alloc("gates_sb", [NT // 128, 32], F32)

    def load_featvec(dram_t, nrow, name):
        rows, ROWS = ar.alloc(name + "_r", [128], F32)
        f.dma(SP, rows[0:nrow, :], dram_t.ap(), reads=[SMALL], writes=[ROWS])
        res, RES = ar.alloc(name, [nrow], F32)
        bk, BK = next_bank()
        f.op(PE, lambda e: e.transpose(bk[:, 0:nrow], rows[0:nrow, :], ident[0:nrow, 0:nrow]), reads=[ROWS, IDENT], writes=[BK])
        f.op(DVE, lambda e: e.tensor_copy(out=res, in_=bk[:, 0:nrow]), reads=[BK], writes=[RES])
        return res, RES

    nmw, NMW = load_featvec(nmw_d, 16, "nmw")
    nfw, NFW = load_featvec(nfw_d, 16, "nfw")
    cffn = []
    for j in range(3):
        t_ = nc.dram_tensor("cffn_view%d" % j, [1], F32, kind="Internal") if False else None
        rows, ROWS = ar.alloc("cffn_r%d" % j, [128], F32)
        f.dma(SP, rows[0:88, :], cffn_d.ap()[j], reads=[SMALL], writes=[ROWS])
        res, RES = ar.alloc("cffn%d" % j, [88], F32)
        bk, BK = next_bank()
        f.op(PE, lambda e, bk=bk, rows=rows: e.transpose(bk[:, 0:88], rows[0:88, :], ident[0:88, 0:88]), reads=[ROWS, IDENT], writes=[BK])
        f.op(DVE, lambda e, bk=bk, res=res: e.tensor_copy(out=res, in_=bk[:, 0:88]), reads=[BK], writes=[RES])
        cffn.append((res, RES))
    nfin, NFIN = ar.alloc("nfin", [D], F32)
    f.dma(SP, nfin, nfin_d.ap().to_broadcast([128, D]), reads=[SMALL], writes=[NFIN])

    base_mark = ar.mark()

    def cast_w(src, dst, SRC, DST, rows, cols, b):
        for r0 in range(0, rows, 128):
            f.dma(POOL, dst.ap()[r0:r0 + 128, :].rearrange("r (a b) -> r a b", b=b),
                  src.ap()[r0:r0 + 128, :].rearrange("r (a b) -> r a b", b=b), reads=[SRC], writes=[DST])
    cast_w(win_d, winb_d, WIN, WINB, D, PW, 834)
    cast_w(wout_d, woutb_d, WOUT, WOUTB, D, D, 1024)
    cast_w(wfi_d, wfib_d, WFI, WFIB, D, 2 * DFF, 1024)
    cast_w(wfo_d, wfob_d, WFO, WFOB, DFF, D, 1024)

    if upto <= 0:
        f.barrier(); f.emit(); f.close()
        return nc
    def make_norm(tag, nslot=3):
        xs = [ar.alloc("%s_x%d" % (tag, i), [D], F32) for i in range(nslot)]
        junk, JUNK = ar.alloc(tag + "_junk", [D], BF16)
        st = [ar.alloc("%s_st%d" % (tag, i), [4], F32) for i in range(2)]
        dg = [ar.alloc("%s_dg%d" % (tag, i), [128], F32) for i in range(2)]
        ctr = [0]

        def run(blocks, wn, WN, nT, NTB):
            for blk in blocks:
                one(blk, wn, WN, nT, NTB)

        def one(blk, wn, WN, nT, NTB):
            rows, npart, col, step, SRCB = blk
            bi = ctr[0]
            ctr[0] += 1
            if True:
                xt, XT = xs[bi % nslot]
                s_, ST = st[bi % 2]
                d_, DG = dg[bi % 2]
                if npart == 128:
                    f.dma(SP, xt, rows, reads=[SRCB], writes=[XT])
                else:
                    f.op(POOL, lambda e, xt=xt: e.memset(xt[0:2, :], 0.0), writes=[XT])
                    for ri, r in enumerate(rows):
                        if r is not None:
                            f.dma(SP, xt[ri:ri + 1, :], r, reads=[SRCB], writes=[XT])
                p = npart
                f.op(ACT, lambda e, xt=xt, s_=s_, p=p: e.activation(out=junk[0:p, :], in_=xt[0:p, :], func=AF.Square, accum_out=s_[0:p, 0:1]),
                     reads=[XT], writes=[JUNK, ST])
                f.op(DVE, lambda e, s_=s_, p=p: e.tensor_scalar(out=s_[0:p, 1:2], in0=s_[0:p, 0:1], scalar1=1.0 / D, scalar2=EPS, op0=ALU.mult, op1=ALU.add),
                     reads=[ST], writes=[ST])
                f.op(ACT, lambda e, s_=s_, p=p: e.activation(out=s_[0:p, 2:3], in_=s_[0:p, 1:2], func=AF.Sqrt), reads=[ST], writes=[ST])
                f.op(DVE, lambda e, s_=s_, p=p: e.reciprocal(out=s_[0:p, 3:4], in_=s_[0:p, 2:3]), reads=[ST], writes=[ST])
                f.op(DVE, lambda e, s_=s_, d_=d_, p=p: e.tensor_scalar(out=d_[0:p, 0:p], in0=ident[0:p, 0:p], scalar1=s_[0:p, 3:4], scalar2=None, op0=ALU.mult),
                     reads=[ST, IDENT], writes=[DG])
                for g4 in range(4):
                    bk, BK = next_bank(0, 2)
                    for j in range(4):
                        kc = g4 * 4 + j
                        f.op(PE, lambda e, bk=bk, xt=xt, d_=d_, kc=kc, j=j, p=p: e.matmul(
                            bk[:, j * 128:j * 128 + p], lhsT=xt[0:p, kc * 128:(kc + 1) * 128], rhs=d_[0:p, 0:p], start=True, stop=True),
                            reads=[XT, DG], writes=[BK])
                    for j in range(4):
                        kc = g4 * 4 + j
                        if p == 128:
                            o_ap = nT[:, kc, col:col + 128]
                        else:
                            o_ap = nT[:, kc, col:col + step + 1:step]
                        i_ap = bk[:, j * 128:j * 128 + p]
                        if g4 % 2 == 0:
                            f.op(ACT, lambda e, o_ap=o_ap, i_ap=i_ap, kc=kc: e.activation(out=o_ap, in_=i_ap, func=AF.Copy, scale=wn[:, kc:kc + 1]),
                                 reads=[BK, WN], writes=[NTB[kc]])
                        else:
                            f.op(DVE, lambda e, o_ap=o_ap, i_ap=i_ap, kc=kc: e.tensor_scalar(out=o_ap, in0=i_ap, scalar1=wn[:, kc:kc + 1], scalar2=None, op0=ALU.mult),
                                 reads=[BK, WN], writes=[NTB[kc]])
        return run

    TT1 = 1024
    NH1 = TT1 // 512
    m1 = ar.mark()
    norm1 = make_norm("n1")
    nT1 = ar.alloc("nT1", [16, TT1], BF16, buf=False)
    NT1B = [f.buf(nT1[:, kc, :], "nT1_%d" % kc) for kc in range(16)]
    wsl = [ar.alloc("wsl%d" % i, [16, 512], BF16) for i in range(3)]
    stg = [ar.alloc("stg%d" % i, [4, TT1], BF16) for i in range(2)]
    vst = [ar.alloc("vst%d" % i, [512], BF16) for i in range(2)]
    wg, WG = ar.alloc("wg", [16, 32], BF16)
    f.dma(SP, wg, winb_d.ap()[:, COL_G:COL_G + 32].rearrange("(k p) n -> p k n", p=128), reads=[WINB], writes=[WG])
    feat_slabs = [c for c in range(0, 4096, 512)] + [COL_BQ + c for c in range(0, 3072, 512)] + [COL_BK + c for c in range(0, 3072, 512)]
    evac_rr = [0]

    def evac(out_ap, in_ap, reads, writes):
        evac_rr[0] += 1
        if evac_rr[0] % 2 == 0:
            f.op(ACT, lambda e: e.activation(out=out_ap, in_=in_ap, func=AF.Copy), reads=reads, writes=writes)
        else:
            f.op(DVE, lambda e: e.tensor_copy(out=out_ap, in_=in_ap), reads=reads, writes=writes)

    si = 0
    for tile in range(NT // TT1):
        t0 = tile * TT1
        blocks = [(x_d.ap()[t0 + b * 128:t0 + (b + 1) * 128, :], 128, b * 128, 0, Xd) for b in range(TT1 // 128)]
        norm1(blocks, nmw, NMW, nT1, NT1B)
        for sidx, c0 in enumerate(feat_slabs):
            w_, W_ = wsl[si % 3]
            sg_, SG_ = stg[si % 2]
            si += 1
            f.dma(SP, w_, winb_d.ap()[:, c0:c0 + 512].rearrange("(k p) n -> p k n", p=128), reads=[WINB], writes=[W_])
            for j in range(4):
                for h in range(NH1):
                    bk, BK = next_bank(2, 8)
                    for kc in range(16):
                        f.op(PE, lambda e, bk=bk, w_=w_, kc=kc, j=j, h=h: e.matmul(
                            bk, lhsT=w_[:, kc, j * 128:(j + 1) * 128], rhs=nT1[:, kc, h * 512:(h + 1) * 512], start=(kc == 0), stop=(kc == 15)),
                            reads=[W_, NT1B[kc]], writes=[BK])
                    evac(sg_[:, j, h * 512:(h + 1) * 512], bk, [BK], [SG_])
            f.dma(POOL, projT_d.ap()[sidx * 512:(sidx + 1) * 512, t0:t0 + TT1].rearrange("(j p) t -> p j t", p=128), sg_,
                  reads=[SG_], writes=[PROJT])
        for sb in range(6):
            c0 = COL_BV + sb * 512
            w_, W_ = wsl[si % 3]
            si += 1
            f.dma(SP, w_, winb_d.ap()[:, c0:c0 + 512].rearrange("(k p) n -> p k n", p=128), reads=[WINB], writes=[W_])
            for b in range(TT1 // 128):
                bk, BK = next_bank(2, 8)
                for kc in range(16):
                    f.op(PE, lambda e, bk=bk, w_=w_, kc=kc, b=b: e.matmul(
                        bk, lhsT=nT1[:, kc, b * 128:(b + 1) * 128], rhs=w_[:, kc, :], start=(kc == 0), stop=(kc == 15)),
                        reads=[W_, NT1B[kc]], writes=[BK])
                v_, V_ = vst[(sb * 8 + b) % 2]
                evac(v_, bk, [BK], [V_])
                f.dma(POOL, vtok_d.ap()[t0 + b * 128:t0 + (b + 1) * 128, sb * 512:(sb + 1) * 512], v_, reads=[V_], writes=[VTOK])
        for b in range(TT1 // 128):
            bk, BK = next_bank(2, 8)
            for kc in range(16):
                f.op(PE, lambda e, bk=bk, kc=kc, b=b: e.matmul(
                    bk[:, 0:32], lhsT=nT1[:, kc, b * 128:(b + 1) * 128], rhs=wg[:, kc, :], start=(kc == 0), stop=(kc == 15)),
                    reads=[WG, NT1B[kc]], writes=[BK])
            gb = t0 // 128 + b
            f.op(DVE, lambda e, bk=bk, gb=gb: e.tensor_copy(out=gates_sb[:, gb, :], in_=bk[:, 0:32]), reads=[BK], writes=[GSB])

    if upto <= 1:
        f.barrier(); f.emit(); f.close()
        return nc
    if debug:
        f.dma(SP, gates_d.ap(), gates_sb, reads=[GSB], writes=[GATES])
    f.barrier()
    ar.reset(m1)
    for slab in range(NFC // 2):
        for gv in range(2):
            c0_ = gv * DFF + slab * 256
            f.dma(ACT, wfit_d.ap()[slab].rearrange("p (k g n) -> p k g n", k=16, g=2)[:, :, gv, :],
                  wfib_d.ap()[:, c0_:c0_ + 256].rearrange("(k p) n -> p k n", p=128), reads=[WFIB], writes=[WFIT])
    for ds in range(4):
        for q in range(4):
            f.dma(ACT, wfot_d.ap()[ds * 4 + q].rearrange("p (k n) -> p k n", k=11),
                  wfob_d.ap()[q * 11 * 128:(q + 1) * 11 * 128, ds * 512:(ds + 1) * 512].rearrange("(k p) n -> p k n", p=128), reads=[WFOB], writes=[WFOT])
    if stub_mixers:
        z_, Z_ = ar.alloc("zeros", [NT], BF16)
        f.op(DVE, lambda e: e.memset(z_, 0.0), writes=[Z_])
        for c in range(16):
            f.dma(SP, mixT_d.ap()[c * 128:(c + 1) * 128, :], z_, reads=[Z_], writes=[MIXT])
    else:
        if not skip_delta:

            maskN = [cst[:, 576:704], cst[:, 832:960]]
            maskQ = [cst[:, 704:832], cst[:, 960:1088]]
            Lmat = [cst[:, 1088:1216], cst[:, 192:320]]
            cq = []
            for j in range(5):
                rows, ROWS = ar.alloc("cq_r%d" % j, [128], F32)
                f.dma(SP, rows[0:24, :], cqkv_d.ap()[j], reads=[SMALL], writes=[ROWS])
                res, RES = ar.alloc("cq%d" % j, [24], F32)
                bk, BK = next_bank()
                f.op(PE, lambda e, bk=bk, rows=rows: e.transpose(bk[:, 0:24], rows[0:24, :], ident[0:24, 0:24]), reads=[ROWS, IDENT], writes=[BK])
                f.op(DVE, lambda e, bk=bk, res=res: e.tensor_copy(out=res, in_=bk[:, 0:24]), reads=[BK], writes=[RES])
                cq.append((res, RES))
            prm, PRM = ar.alloc("prm", [2, 2, 8], F32)
            f.dma(SP, prm[:, 0, 0, :], dtf_d.ap().to_broadcast([128, 8]), reads=[SMALL], writes=[PRM])
            f.dma(SP, prm[:, 0, 1, :], dtb_d.ap().to_broadcast([128, 8]), reads=[SMALL], writes=[PRM])
            f.dma(SP, prm[:, 1, 0, :], alf_d.ap().to_broadcast([128, 8]), reads=[SMALL], writes=[PRM])
            f.dma(SP, prm[:, 1, 1, :], alb_d.ap().to_broadcast([128, 8]), reads=[SMALL], writes=[PRM])
            f.op(ACT, lambda e: e.activation(out=prm[:, 1, :, :], in_=prm[:, 1, :, :], func=AF.Exp), reads=[PRM], writes=[PRM])
            f.op(DVE, lambda e: e.tensor_scalar(out=prm[:, 1, :, :], in0=prm[:, 1, :, :], scalar1=-1.0, scalar2=None, op0=ALU.mult), reads=[PRM], writes=[PRM])
            DTB, DTBB = ar.alloc("DTB", [2, 16, 8], F32)
            NEGAf, NEGAB = ar.alloc("NEGA", [256], F32); NEGA = NEGAf.rearrange("p (d c h) -> p d c h", d=2, c=16)
            for d_ in range(2):
                for c in range(16):
                    f.op(POOL, lambda e, d_=d_, c=c: e.tensor_copy(out=DTB[:, d_, c, :], in_=prm[:, 0, d_, :]), reads=[PRM], writes=[DTBB])
                    f.op(POOL, lambda e, d_=d_, c=c: e.tensor_copy(out=NEGA[:, d_, c, :], in_=prm[:, 1, d_, :]), reads=[PRM], writes=[NEGAB])
            dnw, DNW = ar.alloc("dnw", [128], F32)
            f.dma(SP, dnw, dnw_d.ap().to_broadcast([128, 128]), reads=[SMALL], writes=[DNW])
            onesf, ONESF = ar.alloc("onesf", [128], F32)
            f.op(DVE, lambda e: e.memset(onesf, 1.0), writes=[ONESF])
            onesb, ONESB = ar.alloc("onesb2", [128], BF16)
            f.op(DVE, lambda e: e.memset(onesb, 1.0), writes=[ONESB])
            V4 = lambda t: t.rearrange("p (d c h) -> p d c h", d=2, c=16)
            T1f, T1B = ar.alloc("T1", [256], F32); T1 = V4(T1f)
            GT_f, GTB = ar.alloc("GT_", [256], F32); GT_ = V4(GT_f)
            NEGBf, NEGBB = ar.alloc("NEGB", [256], F32); NEGB = V4(NEGBf)
            BETAf, BETAB = ar.alloc("BETA", [256], F32); BETA = V4(BETAf)
            CGf, CGB = ar.alloc("CG", [256], F32); CG = V4(CGf)
            CGc, CGcB = ar.alloc("CGc", [16, 16], F32)
            TOTf, TOTB = ar.alloc("TOT", [256], F32); TOT = V4(TOTf)
            GEXf, GEXB = ar.alloc("GEX", [256], F32); GEX = V4(GEXf)
            NEGGf, NEGGB = ar.alloc("NEGG", [256], F32); NEGG = V4(NEGGf)
            W2f, W2B = ar.alloc("W2", [256], F32); W2 = V4(W2f)
            DECf, DECB = ar.alloc("DEC", [256], F32); DEC = V4(DECf)
            xp = [ar.alloc("xp%d" % i, [S + 4], BF16) for i in range(3)]
            zt, ZT = ar.alloc("zt", [S], BF16)
            for i in range(3):
                f.op(DVE, lambda e, i=i: e.memset(xp[i][0], 0.0), writes=[xp[i][1]])
            acc = [ar.alloc("cacc%d" % i, [S], F32) for i in range(2)]
            sq, SQ = ar.alloc("sq", [S], BF16)
            rstd, RSTD = ar.alloc("rstd", [S], F32)
            QT, QTB = ar.alloc("QT", [S], BF16)
            KT, KTB = ar.alloc("KT", [S], BF16)
            KT32, KT32B = ar.alloc("KT32", [S], F32)
            VT, VTB = ar.alloc("VT", [S], BF16)
            zs, ZS = ar.alloc("zs", [S], BF16)
            Ktok, KTOK = ar.alloc("Ktok", [16, 128], BF16)
            Vtok, VTOK_ = ar.alloc("Vtok", [16, 128], BF16)
            Oacc, OACC = acc[0][0].rearrange("p (a b) -> p a b", b=128), acc[0][1]
            OACCB = [f.buf(Oacc[:, c, :], "oacc%d" % c) for c in range(16)]
            onb, ONB = ar.alloc("onb", [16, 128], BF16)
            mo2, MO2 = sq, SQ
            osq, OSQ = ar.alloc("osq", [16, 4], F32)
            TiTb = ar.alloc("TiTb", [2, 16, 128], BF16, buf=False)
            QKm = ar.alloc("QKm", [2, 16, 128], BF16, buf=False)
            TIB = [[f.buf(TiTb[:, d_, c4 * 4:(c4 + 1) * 4, :], "tib") for c4 in range(4)] for d_ in range(2)]
            QKB2 = [[f.buf(QKm[:, d_, c4 * 4:(c4 + 1) * 4, :], "qkmb") for c4 in range(4)] for d_ in range(2)]
            dgm2 = [ar.alloc("dgm%d" % i, [4, 128], F32) for i in range(2)]
            t12 = [ar.alloc("t1_%d" % i, [4, 128], F32) for i in range(2)]
            t32 = [ar.alloc("t3_%d" % i, [4, 128], F32) for i in range(2)]
            gam12 = [ar.alloc("gam1_%d" % i, [4, 128], F32) for i in range(2)]
            Nm2 = [ar.alloc("Nm%d" % i, [4, 128], F32) for i in range(2)]
            Pa2 = [[ar.alloc("Pa%d_%d" % (d_, i), [4, 128], F32) for i in range(2)] for d_ in range(2)]
            PaT2 = [[ar.alloc("PaT%d_%d" % (d_, i), [4, 128], F32) for i in range(2)] for d_ in range(2)]
            Xa2 = [[ar.alloc("Xa%d_%d" % (d_, i), [4, 128], F32) for i in range(2)] for d_ in range(2)]
            KKs, KKSB = ar.alloc("KKs", [4, 128], F32)
            KQs, KQSB = ar.alloc("KQs", [4, 128], F32)
            ident4, ID4B = ar.alloc("ident4", [4, 128], F32)
            for j in range(4):
                f.op(DVE, lambda e, j=j: e.tensor_copy(out=ident4[:, j, :], in_=ident), reads=[IDENT], writes=[ID4B])
            print("phase2 arena KiB", ar.off * 2 / 1024)
            Sst = [ar.alloc("Sst%d" % i, [128], F32) for i in range(2)]
            Sbf = [ar.alloc("Sbf%d" % i, [128], BF16) for i in range(2)]
            Rb = [ar.alloc("Rb%d" % i, [128], BF16) for i in range(2)]
            Vn = [ar.alloc("Vn%d" % i, [2, 128], BF16) for i in range(2)]
            bfv = lambda bk: bk.bitcast(BF16)
            for s_i in range(NS):
                c0 = s_i * S
                gs = gates_sb[:, s_i * 16:(s_i + 1) * 16, :]
                for d_ in range(2):
                    f.op(DVE, lambda e, gs=gs, d_=d_: e.tensor_tensor(out=T1[:, d_, :, :], in0=gs[:, :, d_ * 8:(d_ + 1) * 8], in1=DTB[:, d_, :, :], op=ALU.add), reads=[GSB, DTBB], writes=[T1B])
                    f.op(ACT, lambda e, gs=gs, d_=d_: e.activation(out=BETA[:, d_, :, :], in_=gs[:, :, 16 + d_ * 8:16 + (d_ + 1) * 8], func=AF.Sigmoid), reads=[GSB], writes=[BETAB])
                f.op(ACT, lambda e: e.activation(out=T1f, in_=T1f, func=AF.Exp), reads=[T1B], writes=[T1B])
                f.op(ACT, lambda e: e.activation(out=T1f, in_=T1f, func=AF.Ln, bias=1.0), reads=[T1B], writes=[T1B])
                f.op(DVE, lambda e: e.tensor_tensor(out=GT_f, in0=T1f, in1=NEGAf, op=ALU.mult), reads=[T1B, NEGAB], writes=[GTB])
                f.op(DVE, lambda e: e.tensor_scalar(out=NEGBf, in0=BETAf, scalar1=-1.0, scalar2=None, op0=ALU.mult), reads=[BETAB], writes=[NEGBB])
                bk, BK = next_bank()
                for d_ in range(2):
                    f.op(PE, lambda e, bk=bk, d_=d_: e.matmul(bk[:, d_ * 128:(d_ + 1) * 128], lhsT=Lmat[d_], rhs=GT_f[:, d_ * 128:(d_ + 1) * 128], start=True, stop=True), reads=[CST, GTB], writes=[BK])
                f.op(DVE, lambda e, bk=bk: e.tensor_copy(out=CGf, in_=bk[:, 0:256]), reads=[BK], writes=[CGB])
                bk, BK = next_bank()
                f.op(PE, lambda e, bk=bk: e.matmul(bk[:, 0:256], lhsT=onesf, rhs=GT_f, start=True, stop=True), reads=[ONESF, GTB], writes=[BK])
                f.op(DVE, lambda e, bk=bk: e.tensor_copy(out=TOTf, in_=bk[:, 0:256]), reads=[BK], writes=[TOTB])
                for d_ in range(2):
                    f.op(DVE, lambda e, d_=d_: e.tensor_copy(out=CGc[:, :, d_ * 8:(d_ + 1) * 8], in_=CG[:, d_, :, :]), reads=[CGB], writes=[CGcB])
                f.op(ACT, lambda e: e.activation(out=GEXf, in_=CGf, func=AF.Exp), reads=[CGB], writes=[GEXB])
                f.op(DVE, lambda e: e.tensor_scalar(out=NEGGf, in0=GEXf, scalar1=-1.0, scalar2=None, op0=ALU.mult), reads=[GEXB], writes=[NEGGB])
                f.op(DVE, lambda e: e.tensor_tensor(out=W2f, in0=TOTf, in1=CGf, op=ALU.subtract), reads=[TOTB, CGB], writes=[W2B])
                f.op(ACT, lambda e: e.activation(out=W2f, in_=W2f, func=AF.Exp), reads=[W2B], writes=[W2B])
                f.op(ACT, lambda e: e.activation(out=DECf, in_=TOTf, func=AF.Exp), reads=[TOTB], writes=[DECB])
                for h in range(8):
                    for i in range(3):
                        ch = i * 8 + h
                        f.dma(SP, xp[i][0][:, 2:S + 2], projT_d.ap()[ch * 128:(ch + 1) * 128, c0:c0 + S], reads=[PROJT], writes=[xp[i][1]])
                    f.dma(SP, zt, projT_d.ap()[(24 + h) * 128:(25 + h) * 128, c0:c0 + S], reads=[PROJT], writes=[ZT])
                    f.op(ACT, lambda e: e.activation(out=zs, in_=zt, func=AF.Silu), reads=[ZT], writes=[ZS])
                    for i in range(3):
                        ch = i * 8 + h
                        x_, X_ = xp[i]
                        a_, A_ = acc[i % 2]
                        f.op(DVE, lambda e, x_=x_, a_=a_, ch=ch: e.tensor_scalar(out=a_, in0=x_[:, 2:S + 2], scalar1=cq[2][0][:, ch:ch + 1], scalar2=None, op0=ALU.mult),
                             reads=[X_, cq[2][1]], writes=[A_])
                        for j in (0, 1, 3, 4):
                            f.op(DVE, lambda e, x_=x_, a_=a_, ch=ch, j=j: e.scalar_tensor_tensor(out=a_, in0=x_[:, j:j + S], scalar=cq[j][0][:, ch:ch + 1], in1=a_, op0=ALU.mult, op1=ALU.add),
                                 reads=[X_, cq[j][1], A_], writes=[A_])
                        if i == 2:
                            f.op(ACT, lambda e, a_=a_: e.activation(out=VT, in_=a_, func=AF.Silu), reads=[A_], writes=[VTB])
                            continue
                        f.op(ACT, lambda e, a_=a_: e.activation(out=a_, in_=a_, func=AF.Silu), reads=[A_], writes=[A_])
                        f.op(ACT, lambda e, a_=a_: e.activation(out=sq, in_=a_, func=AF.Square), reads=[A_], writes=[SQ])
                        for t in range(4):
                            bk, BK = next_bank()
                            f.op(PE, lambda e, bk=bk, t=t: e.matmul(bk, lhsT=onesb, rhs=sq[:, t * 512:(t + 1) * 512], start=True, stop=True), reads=[ONESB, SQ], writes=[BK])
                            f.op(ACT, lambda e, bk=bk, t=t: e.activation(out=rstd[:, t * 512:(t + 1) * 512], in_=bk, func=AF.Sqrt, bias=cst[:, 162:163]), reads=[BK, CST], writes=[RSTD])
                        f.op(DVE, lambda e: e.reciprocal(out=rstd, in_=rstd), reads=[RSTD], writes=[RSTD])
                        if i == 0:
                            f.op(DVE, lambda e, a_=a_: e.scalar_tensor_tensor(out=QT, in0=a_, scalar=128.0 ** -0.5, in1=rstd, op0=ALU.mult, op1=ALU.mult), reads=[A_, RSTD], writes=[QTB])
                        else:
                            f.op(DVE, lambda e, a_=a_: e.tensor_tensor(out=KT, in0=a_, in1=rstd, op=ALU.mult), reads=[A_, RSTD], writes=[KTB])
                            f.op(POOL, lambda e, a_=a_: e.tensor_tensor(out=KT32, in0=a_, in1=rstd, op=ALU.mult), reads=[A_, RSTD], writes=[KT32B])
                    for (src, SRCB, dst, DSTB) in ((KT, KTB, Ktok, KTOK), (VT, VTB, Vtok, VTOK_)):
                        for c4 in range(4):
                            bk, BK = next_bank()
                            for j in range(4):
                                c = c4 * 4 + j
                                f.op(PE, lambda e, bk=bk, src=src, c=c, j=j: e.transpose(bfv(bk)[:, j * 128:(j + 1) * 128], src[:, c * 128:(c + 1) * 128], identb),
                                     reads=[SRCB, IDENTB], writes=[BK])
                            f.op(DVE, lambda e, bk=bk, dst=dst, c4=c4: e.tensor_copy(out=dst[:, c4 * 4:(c4 + 1) * 4, :], in_=bfv(bk)[:, 0:512]), reads=[BK], writes=[DSTB])
                    for c4 in range(4):
                        bX, BX = banks[0]
                        bY, BY = banks[1]
                        for j in range(4):
                            c = c4 * 4 + j
                            cs = slice(c * 128, (c + 1) * 128)
                            f.op(PE, lambda e, cs=cs, j=j: e.matmul(bX[:, j * 128:(j + 1) * 128], lhsT=KT32[:, cs], rhs=KT32[:, cs], start=True, stop=True), reads=[KT32B], writes=[BX])
                            f.op(PE, lambda e, cs=cs, j=j: e.matmul(bY[:, j * 128:(j + 1) * 128], lhsT=KT[:, cs], rhs=QT[:, cs], start=True, stop=True), reads=[KTB, QTB], writes=[BY])
                        f.op(DVE, lambda e: e.tensor_copy(out=KKs, in_=bX.rearrange("p (a b) -> p a b", b=128)), reads=[BX], writes=[KKSB])
                        f.op(ACT, lambda e: e.activation(out=KQs, in_=bY.rearrange("p (a b) -> p a b", b=128), func=AF.Copy), reads=[BY], writes=[KQSB])
                        bset = [(banks[2], banks[3], banks[4]), (banks[5], banks[6], banks[7])]
                        DD = (0, 1)
                        for d_ in DD:
                            (bA, BA) = bset[d_][0]
                            dgm, DGM = dgm2[d_]
                            for j in range(4):
                                c = c4 * 4 + j
                                f.op(POOL, lambda e, h=h, c=c, j=j, d_=d_, dgm=dgm: e.tensor_scalar(out=dgm[:, j, :], in0=ident, scalar1=CG[:, d_, c, h:h + 1], scalar2=None, op0=ALU.mult),
                                     reads=[IDENT, CGB], writes=[DGM])
                                f.op(PE, lambda e, j=j, bA=bA, dgm=dgm: e.matmul(bA[:, j * 128:(j + 1) * 128], lhsT=onesf, rhs=dgm[:, j, :], start=True, stop=True),
                                     reads=[ONESF, DGM], writes=[BA])
                        for d_ in DD:
                            (bA, BA) = bset[d_][0]
                            t1, T1b = t12[d_]
                            t3, T3b = t32[d_]
                            gam1, GAM1 = gam12[d_]
                            for j in range(4):
                                c = c4 * 4 + j
                                f.op(DVE, lambda e, h=h, c=c, j=j, d_=d_, bA=bA, t1=t1: e.scalar_tensor_tensor(out=t1[:, j, :], in0=bA[:, j * 128:(j + 1) * 128], scalar=CG[:, d_, c, h:h + 1], in1=maskN[d_], op0=ALU.subtract, op1=ALU.max),
                                     reads=[BA, CGB, CST], writes=[T1b])
                                f.op(DVE, lambda e, h=h, c=c, j=j, d_=d_, bA=bA, t3=t3: e.scalar_tensor_tensor(out=t3[:, j, :], in0=bA[:, j * 128:(j + 1) * 128], scalar=CG[:, d_, c, h:h + 1], in1=maskQ[d_], op0=ALU.subtract, op1=ALU.min),
                                     reads=[BA, CGB, CST], writes=[T3b])
                            f.op(ACT, lambda e, t1=t1, gam1=gam1: e.activation(out=gam1, in_=t1, func=AF.Exp, scale=-1.0), reads=[T1b], writes=[GAM1])
                            f.op(ACT, lambda e, t3=t3: e.activation(out=t3, in_=t3, func=AF.Exp), reads=[T3b], writes=[T3b])
                        for d_ in DD:
                            t3, T3b = t32[d_]
                            gam1, GAM1 = gam12[d_]
                            Nm, NM = Nm2[d_]
                            for j in range(4):
                                c = c4 * 4 + j
                                f.op(DVE, lambda e, h=h, c=c, j=j, d_=d_, Nm=Nm, gam1=gam1: e.scalar_tensor_tensor(out=Nm[:, j, :], in0=KKs[:, j, :], scalar=NEGB[:, d_, c, h:h + 1], in1=gam1[:, j, :], op0=ALU.mult, op1=ALU.mult),
                                     reads=[KKSB, NEGBB, GAM1], writes=[NM])
                            f.op(POOL, lambda e, d_=d_, c4=c4, t3=t3: e.tensor_tensor(out=QKm[:, d_, c4 * 4:(c4 + 1) * 4, :], in0=KQs, in1=t3, op=ALU.mult),
                                 reads=[KQSB, T3b], writes=[QKB2[d_][c4]])
                        st = {}
                        for d_ in DD:
                            (bB, BB) = bset[d_][1]
                            Nm, NM = Nm2[d_]
                            for j in range(4):
                                f.op(PE, lambda e, j=j, bB=bB, Nm=Nm: e.transpose(bB[:, j * 128:(j + 1) * 128], Nm[:, j, :], ident), reads=[NM, IDENT], writes=[BB])
                            PT_, PTB_ = PaT2[d_][0]
                            X_, XB_ = Xa2[d_][0]
                            f.op(ACT, lambda e, PT_=PT_, bB=bB: e.activation(out=PT_, in_=bB.rearrange("p (a b) -> p a b", b=128), func=AF.Copy), reads=[BB], writes=[PTB_])
                            f.op(POOL, lambda e, X_=X_, PT_=PT_: e.tensor_tensor(out=X_, in0=PT_, in1=ident4, op=ALU.add), reads=[PTB_, ID4B], writes=[XB_])
                            st[d_] = [Nm, NM, PT_, PTB_, X_, XB_]
                        for lv in range(6):
                            last = (lv == 5)
                            nxt = {}
                            for d_ in DD:
                                (bA, BA), (bB, BB), (bC, BC) = bset[d_]
                                P_, PB_, PT_, PTB_, X_, XB_ = st[d_]
                                for j in range(4):
                                    f.op(PE, lambda e, j=j, P_=P_, PT_=PT_, bA=bA: e.matmul(bA[:, j * 128:(j + 1) * 128], lhsT=PT_[:, j, :], rhs=P_[:, j, :], start=True, stop=True), reads=[PB_, PTB_], writes=[BA])
                                if not last:
                                    for j in range(4):
                                        f.op(PE, lambda e, j=j, P_=P_, PT_=PT_, bB=bB: e.matmul(bB[:, j * 128:(j + 1) * 128], lhsT=P_[:, j, :], rhs=PT_[:, j, :], start=True, stop=True), reads=[PB_, PTB_], writes=[BB])
                            for d_ in DD:
                                (bA, BA), (bB, BB), (bC, BC) = bset[d_]
                                P2, P2B = Pa2[d_][lv % 2]
                                P2T, P2TB = PaT2[d_][(lv + 1) % 2]
                                f.op(ACT, lambda e, P2=P2, bA=bA: e.activation(out=P2, in_=bA.rearrange("p (a b) -> p a b", b=128), func=AF.Copy), reads=[BA], writes=[P2B])
                                if not last:
                                    f.op(DVE, lambda e, P2T=P2T, bB=bB: e.tensor_copy(out=P2T, in_=bB.rearrange("p (a b) -> p a b", b=128)), reads=[BB], writes=[P2TB])
                                nxt[d_] = [P2, P2B, P2T, P2TB]
                            for d_ in DD:
                                (bA, BA), (bB, BB), (bC, BC) = bset[d_]
                                P2, P2B, P2T, P2TB = nxt[d_]
                                X_, XB_ = st[d_][4], st[d_][5]
                                for j in range(4):
                                    f.op(PE, lambda e, j=j, P2=P2, X_=X_, bC=bC: e.matmul(bC[:, j * 128:(j + 1) * 128], lhsT=P2[:, j, :], rhs=X_[:, j, :], start=True, stop=True), reads=[P2B, XB_], writes=[BC])
                            for d_ in DD:
                                (bA, BA), (bB, BB), (bC, BC) = bset[d_]
                                P2, P2B, P2T, P2TB = nxt[d_]
                                X_, XB_ = st[d_][4], st[d_][5]
                                X2, X2B = Xa2[d_][(lv + 1) % 2]
                                f.op(DVE, lambda e, X2=X2, X_=X_, bC=bC: e.tensor_tensor(out=X2, in0=bC.rearrange("p (a b) -> p a b", b=128), in1=X_, op=ALU.add), reads=[BC, XB_], writes=[X2B])
                                st[d_] = [P2, P2B, P2T, P2TB, X2, X2B]
                        for d_ in DD:
                            X_, XB_ = st[d_][4], st[d_][5]
                            for j in range(4):
                                c = c4 * 4 + j
                                f.op(ACT, lambda e, h=h, c=c, j=j, d_=d_, X_=X_: e.activation(out=TiTb[:, d_, c, :], in_=X_[:, j, :], func=AF.Copy, scale=BETA[:, d_, c, h:h + 1]),
                                     reads=[XB_, BETAB], writes=[TIB[d_][c4]])
                    f.op(DVE, lambda e: e.memset(Oacc, 0.0), writes=[OACC] + OACCB)
                    for d_ in range(2):
                        f.op(DVE, lambda e, d_=d_: e.memset(Sst[d_][0], 0.0), writes=[Sst[d_][1]])
                        f.op(POOL, lambda e, d_=d_: e.memset(Sbf[d_][0], 0.0), writes=[Sbf[d_][1]])
                    for step in range(16):
                        stages = [[] for _ in range(7)]
                        for d_ in range(2):
                            c = step if d_ == 0 else 15 - step
                            cs = slice(c * 128, (c + 1) * 128)
                            c4 = c // 4
                            S_, SB_ = Sst[d_]
                            sb_, SBB_ = Sbf[d_]
                            r_, RB_ = Rb[d_]
                            v_, VB_ = Vn[d_]
                            bP, BP = banks[d_ * 3 + 0]
                            bQ, BQ = banks[d_ * 3 + 1]
                            bR, BR = banks[d_ * 3 + 2]

                            def mk(d_=d_, c=c, cs=cs, c4=c4, S_=S_, SB_=SB_, sb_=sb_, SBB_=SBB_, r_=r_, RB_=RB_, v_=v_, VB_=VB_, bP=bP, BP=BP, bQ=bQ, BQ=BQ, bR=bR, BR=BR, h=h):
                                st0 = lambda: f.op(PE, lambda e: e.matmul(bP[:, 0:128], lhsT=KT[:, cs], rhs=sb_, start=True, stop=True), reads=[KTB, SBB_], writes=[BP])
                                st1 = lambda: f.op(DVE, lambda e: e.scalar_tensor_tensor(out=r_, in0=bP[:, 0:128], scalar=NEGG[:, d_, c, h:h + 1], in1=Vtok[:, c, :], op0=ALU.mult, op1=ALU.add),
                                                   reads=[BP, NEGGB, VTOK_], writes=[RB_])
                                st2 = lambda: f.op(PE, lambda e: e.matmul(bP[:, 128:256], lhsT=TiTb[:, d_, c, :], rhs=r_, start=True, stop=True), reads=[TIB[d_][c4], RB_], writes=[BP])

                                def st3():
                                    f.op(ACT, lambda e: e.activation(out=v_[:, 0, :], in_=bP[:, 128:256], func=AF.Copy), reads=[BP], writes=[VB_])
                                    f.op(ACT, lambda e: e.activation(out=v_[:, 1, :], in_=bP[:, 128:256], func=AF.Copy, scale=W2[:, d_, c, h:h + 1]), reads=[BP, W2B], writes=[VB_])

                                def st4():
                                    f.op(PE, lambda e: e.matmul(bQ[:, 0:128], lhsT=QT[:, cs], rhs=sb_, start=True, stop=True), reads=[QTB, SBB_], writes=[BQ])
                                    f.op(PE, lambda e: e.matmul(bQ[:, 128:256], lhsT=QKm[:, d_, c, :], rhs=v_[:, 0, :], start=True, stop=True), reads=[QKB2[d_][c4], VB_], writes=[BQ])
                                    f.op(PE, lambda e: e.matmul(bR[:, 0:128], lhsT=Ktok[:, c, :], rhs=v_[:, 1, :], start=True, stop=True), reads=[KTOK, VB_], writes=[BR])

                                def st5():
                                    f.op(DVE, lambda e: e.scalar_tensor_tensor(out=S_, in0=S_, scalar=DEC[:, d_, c, h:h + 1], in1=bR[:, 0:128], op0=ALU.mult, op1=ALU.add),
                                         reads=[BR, DECB, SB_], writes=[SB_])
                                    f.op(DVE, lambda e: e.scalar_tensor_tensor(out=Oacc[:, c, :], in0=bQ[:, 0:128], scalar=GEX[:, d_, c, h:h + 1], in1=Oacc[:, c, :], op0=ALU.mult, op1=ALU.add),
                                         reads=[BQ, GEXB, OACCB[c]], writes=[OACCB[c]])
                                    f.op(DVE, lambda e: e.tensor_tensor(out=Oacc[:, c, :], in0=bQ[:, 128:256], in1=Oacc[:, c, :], op=ALU.add), reads=[BQ, OACCB[c]], writes=[OACCB[c]])
                                st6 = lambda: f.op(ACT, lambda e: e.activation(out=sb_, in_=S_, func=AF.Copy), reads=[SB_], writes=[SBB_])
                                return [st0, st1, st2, st3, st4, st5, st6]
                            for i_, fn_ in enumerate(mk()):
                                stages[i_].append(fn_)
                        for stg_ in stages:
                            for fn_ in stg_:
                                fn_()
                    if False:
                        dbg("gam1", gam1, GAM1, [4, 128], BF16); dbg("gam3", t3, T3b, [4, 128], F32); dbg("Nm", Nm, NM, [4, 128], BF16)
                        dbg("QT", QT, QTB, [S], BF16); dbg("KT", KT, KTB, [S], BF16); dbg("VT", VT, VTB, [S], BF16)
                        dbg("Oacc", Oacc, OACC, [16, 128], F32)
                        dbg("CG", CGf, CGB, [256], F32); dbg("W2", W2f, W2B, [256], F32); dbg("GEX", GEXf, GEXB, [256], F32)
                        dbg("DEC", DECf, DECB, [256], F32); dbg("BETA", BETAf, BETAB, [256], F32); dbg("TOT", TOTf, TOTB, [256], F32)
                        dbg("G", GT_f, GTB, [256], F32)
                        dbg("TiTb0", TiTb[:, 0, 0:4, :], TIB[0][0], [4, 128], BF16); dbg("TiTb1", TiTb[:, 1, 12:16, :], TIB[1][3], [4, 128], BF16)
                        dbg("QKm0", QKm[:, 0, 0:4, :], QKB2[0][0], [4, 128], BF16); dbg("QKm1", QKm[:, 1, 12:16, :], QKB2[1][3], [4, 128], BF16)
                        dbg("Ktok", Ktok, KTOK, [16, 128], BF16); dbg("Vtok", Vtok, VTOK_, [16, 128], BF16)
                    for c in range(16):
                        f.op(ACT, lambda e, c=c: e.activation(out=onb[:, c, :], in_=Oacc[:, c, :], func=AF.Square, accum_out=osq[:, c, 0:1]), reads=[OACC, OACCB[c]], writes=[ONB, OSQ])
                    f.op(DVE, lambda e: e.tensor_scalar(out=osq[:, :, 1:2], in0=osq[:, :, 0:1], scalar1=1.0 / 128, scalar2=EPS, op0=ALU.mult, op1=ALU.add), reads=[OSQ], writes=[OSQ])
                    f.op(ACT, lambda e: e.activation(out=osq[:, :, 2:3], in_=osq[:, :, 1:2], func=AF.Sqrt), reads=[OSQ], writes=[OSQ])
                    f.op(DVE, lambda e: e.reciprocal(out=osq[:, :, 3:4], in_=osq[:, :, 2:3]), reads=[OSQ], writes=[OSQ])
                    for c in range(16):
                        f.op(DVE, lambda e, c=c: e.scalar_tensor_tensor(out=onb[:, c, :], in0=Oacc[:, c, :], scalar=osq[:, c, 3:4], in1=dnw, op0=ALU.mult, op1=ALU.mult),
                             reads=[OACC, OACCB[c], OSQ, DNW], writes=[ONB])
                    for c4 in range(4):
                        bk, BK = banks[7]
                        for j in range(4):
                            c = c4 * 4 + j
                            f.op(PE, lambda e, bk=bk, c=c, j=j: e.transpose(bfv(bk)[:, j * 128:(j + 1) * 128], onb[:, c, :], identb), reads=[ONB, IDENTB], writes=[BK])
                        f.op(DVE, lambda e, bk=bk, c4=c4: e.tensor_tensor(out=mo2[:, c4 * 512:(c4 + 1) * 512], in0=bfv(bk)[:, 0:512], in1=zs[:, c4 * 512:(c4 + 1) * 512], op=ALU.mult),
                             reads=[BK, ZS], writes=[MO2])
                    f.dma(POOL, mixT_d.ap()[h * 128:(h + 1) * 128, c0:c0 + S], mo2, reads=[MO2], writes=[MIXT])
        else:
            z_, Z_ = ar.alloc("zeros", [NT], BF16)
            f.op(DVE, lambda e: e.memset(z_, 0.0), writes=[Z_])
            for c in range(8):
                f.dma(SP, mixT_d.ap()[c * 128:(c + 1) * 128, :], z_, reads=[Z_], writes=[MIXT])
        f.barrier()
        ar.reset(m1)
        cstb, CSTB = ar.alloc("cstb", [NCST], BF16)
        f.op(DVE, lambda e: e.tensor_copy(out=cstb, in_=cst), reads=[CST], writes=[CSTB])
        Rm = cstb[0:32, 128:160]
        maskA, maskB, maskAc = cstb[:, 192:320], cstb[:, 320:448], cstb[0:64, 448:576]
        onesb, ONESB = ar.alloc("onesb", [128], BF16)
        f.op(DVE, lambda e: e.memset(onesb, 1.0), writes=[ONESB])
        posi, POSI = ar.alloc("posi", [S], I32)
        ang, ANG = ar.alloc("ang", [S], F32)
        tmpa, TMPA = ar.alloc("tmpa", [S], F32)
        tmpi, TMPI = posi, POSI
        cosT, COS = ar.alloc("cosT", [S], F32)
        sinT, SIN = ar.alloc("sinT", [S], F32)
        qk = [[[ar.alloc("qk%d_%d_%d" % (sl, g, i), [S], BF16, buf=False) for i in range(2)] for g in range(3)] for sl in range(2)]
        QKB = [[[[f.buf(qk[sl][g][i][:, t * 512:(t + 1) * 512], "qkb") for t in range(4)] for i in range(2)] for g in range(3)] for sl in range(2)]
        DIL = [1, 4, 16]
        vh = [[ar.alloc("vh%d_%d" % (sl, g), [16, 128], BF16) for g in range(3)] for sl in range(2)]
        v0 = [[ar.alloc("v0%d_%d" % (sl, g), [16, 128], BF16) for g in range(3)] for sl in range(1)] * 2
        accn, ACCN = ar.alloc("accn", [S], F32)
        accd, ACCD = ar.alloc("accd", [S], F32)
        mo = [ar.alloc("mo%d" % i, [S], BF16) for i in range(2)]
        rt = [ar.alloc("rt%d" % i, [2, 512], F32) for i in range(2)]
        pb = [ar.alloc("pb%d" % i, [2, 128], BF16) for i in range(4)]
        PI = float(np.pi)
        pit, PIT = ar.alloc("pit", [S], F32)
        f.op(DVE, lambda e: e.memset(pit[0:32, :], PI), writes=[PIT])
        it = 0
        for s_i in range(NS):
            f.dma(SP, posi[0:32, :], pos_d.ap()[s_i:s_i + 1, :].to_broadcast([32, S]), reads=[SMALL], writes=[POSI])
            f.op(DVE, lambda e: e.tensor_copy(out=ang[0:32, :], in_=posi[0:32, :]), reads=[POSI], writes=[ANG])
            f.op(DVE, lambda e: e.tensor_scalar(out=ang[0:32, :], in0=ang[0:32, :], scalar1=cst[0:32, 160:161], scalar2=None, op0=ALU.mult), reads=[ANG, CST], writes=[ANG])
            f.op(DVE, lambda e: e.tensor_scalar(out=tmpi[0:32, :], in0=ang[0:32, :], scalar1=1.0 / (2 * PI), scalar2=None, op0=ALU.mult), reads=[ANG], writes=[TMPI])
            f.op(DVE, lambda e: e.tensor_copy(out=tmpa[0:32, :], in_=tmpi[0:32, :]), reads=[TMPI], writes=[TMPA])
            f.op(DVE, lambda e: e.scalar_tensor_tensor(out=ang[0:32, :], in0=tmpa[0:32, :], scalar=-2 * PI, in1=ang[0:32, :], op0=ALU.mult, op1=ALU.add), reads=[TMPA, ANG], writes=[ANG])
            f.op(DVE, lambda e: e.tensor_tensor(out=tmpa[0:32, :], in0=ang[0:32, :], in1=pit[0:32, :], op=ALU.is_gt), reads=[ANG, PIT], writes=[TMPA])
            f.op(DVE, lambda e: e.scalar_tensor_tensor(out=ang[0:32, :], in0=tmpa[0:32, :], scalar=-2 * PI, in1=ang[0:32, :], op0=ALU.mult, op1=ALU.add), reads=[ANG, TMPA], writes=[ANG])
            f.op(ACT, lambda e: e.activation(out=sinT[0:32, :], in_=ang[0:32, :], func=AF.Sin), reads=[ANG], writes=[SIN])
            f.op(DVE, lambda e: e.scalar_tensor_tensor(out=tmpa[0:32, :], in0=ang[0:32, :], scalar=-1.0, in1=ang[0:32, :], op0=ALU.mult, op1=ALU.max), reads=[ANG], writes=[TMPA])
            f.op(ACT, lambda e: e.activation(out=cosT[0:32, :], in_=tmpa[0:32, :], func=AF.Sin, scale=-1.0, bias=cst[0:32, 161:162]), reads=[TMPA, CST], writes=[COS])
            for h in range(8):
                sl = it % 2
                it += 1
                c0 = s_i * S
                for g in range(3):
                    for i in range(2):
                        ch = 32 + i * 24 + g * 8 + h
                        for t in range(4):
                            f.dma(SP, qk[sl][g][i][:, t * 512:(t + 1) * 512], projT_d.ap()[ch * 128:(ch + 1) * 128, c0 + t * 512:c0 + (t + 1) * 512],
                                  reads=[PROJT], writes=[QKB[sl][g][i][t]])
                    d = DIL[g]
                    L = S // d
                    M = L // 128
                    vcol = g * 1024 + h * 128
                    view = vtok_d.ap()[c0:c0 + S, vcol:vcol + 128].rearrange("(i d) c -> d i c", d=d)
                    vh_, VH_ = vh[sl][g]
                    v0_, V0_ = v0[sl][g]
                    vhv = vh_.rearrange("p (r m) c -> p r m c", m=M)
                    for r in range(d):
                        if M > 1:
                            f.dma(SP, vhv[:, r, 0:M - 1, :], view[r, 64:L - 64, :].rearrange("(m j) c -> j m c", j=128), reads=[VTOK], writes=[VH_])
                    f.dma(SP, vhv[0:64, :, M - 1, :], view[:, L - 64:L, :].rearrange("r j c -> j r c"), reads=[VTOK], writes=[VH_])
                    f.dma(SP, v0_[0:64, 0:d, :], view[:, 0:64, :].rearrange("r j c -> j r c"), reads=[VTOK], writes=[V0_])
                for g in range(3):
                    for i in range(2):
                        X = qk[sl][g][i]
                        for t in range(4):
                            XB = QKB[sl][g][i][t]
                            cs = slice(t * 512, (t + 1) * 512)
                            bk, BK = next_bank(0, 4)
                            r_, R_ = rt[(g * 8 + i * 4 + t) % 2]
                            f.op(PE, lambda e, bk=bk, X=X, cs=cs: e.matmul(bk[0:32, :], lhsT=Rm, rhs=X[0:32, cs], start=True, stop=True), reads=[CSTB, XB], writes=[BK])
                            f.op(DVE, lambda e, r_=r_, X=X, cs=cs: e.tensor_tensor(out=r_[0:32, 0, :], in0=X[0:32, cs], in1=cosT[0:32, cs], op=ALU.mult), reads=[XB, COS], writes=[R_])
                            f.op(DVE, lambda e, r_=r_, bk=bk, cs=cs: e.tensor_tensor(out=r_[0:32, 1, :], in0=bk[0:32, :], in1=sinT[0:32, cs], op=ALU.mult), reads=[BK, SIN, R_], writes=[R_])
                            f.op(DVE, lambda e, r_=r_, X=X, cs=cs: e.tensor_tensor(out=X[0:32, cs], in0=r_[0:32, 0, :], in1=r_[0:32, 1, :], op=ALU.add), reads=[R_], writes=[XB])
                pendB = []
                un = 0
                for g in range(3):
                    d = DIL[g]
                    L = S // d
                    M = L // 128
                    Q, K = qk[sl][g][0], qk[sl][g][1]
                    QB_, KB_ = QKB[sl][g][0], QKB[sl][g][1]
                    vh_, VH_ = vh[sl][g]
                    v0_, V0_ = v0[sl][g]
                    vhv = vh_.rearrange("p (r m) c -> p r m c", m=M)
                    units = [(r, qb) for r in range(d) for qb in range(M)]
                    for u0 in range(0, len(units), 4):
                        nb_, NB_ = banks[4 + (un // 4) % 2]
                        db_, DB_ = banks[6 + (un // 4) % 2]
                        for j in range(4):
                            r, qb = units[u0 + j]
                            un += 1
                            sb_, SB_ = banks[un % 4]
                            p_, P_ = pb[un % 4]
                            qsl = slice(r + d * 128 * qb, r + d * 128 * qb + d * 127 + 1, d)
                            blocks = []
                            if qb >= 1:
                                blocks.append((128 * qb - 64, 128, maskA, vhv[:, r, qb - 1, :], VH_))
                            else:
                                blocks.append((0, 64, maskAc, v0_[0:64, r, :], V0_))
                            if qb < M - 1:
                                blocks.append((128 * qb + 64, 128, maskB, vhv[:, r, qb, :], VH_))
                            else:
                                blocks.append((128 * qb + 64, 64, maskB[0:64, :], vhv[0:64, r, qb, :], VH_))
                            for bi, (k0, nk, mk, vap, VB_) in enumerate(blocks):
                                ksl = slice(r + d * k0, r + d * k0 + d * (nk - 1) + 1, d)
                                f.op(PE, lambda e, sb_=sb_, K=K, Q=Q, ksl=ksl, qsl=qsl, nk=nk, bi=bi: e.matmul(
                                    sb_[0:nk, bi * 128:(bi + 1) * 128], lhsT=K[:, ksl], rhs=Q[:, qsl], start=True, stop=True),
                                    reads=KB_ + QB_, writes=[SB_])
                                f.op(ACT, lambda e, sb_=sb_, p_=p_, nk=nk, bi=bi: e.activation(
                                    out=p_[0:nk, bi, :], in_=sb_[0:nk, bi * 128:(bi + 1) * 128], func=AF.Exp, scale=128.0 ** -0.5),
                                    reads=[SB_], writes=[P_])
                                f.op(POOL, lambda e, p_=p_, nk=nk, bi=bi, mk=mk: e.tensor_tensor(out=p_[0:nk, bi, :], in0=p_[0:nk, bi, :], in1=mk, op=ALU.mult),
                                     reads=[P_, CSTB], writes=[P_])
                            def stB(blocks=blocks, nb_=nb_, NB_=NB_, db_=db_, DB_=DB_, p_=p_, P_=P_, j=j, g=g, u0=u0, units=units):
                                for bi, (k0, nk, mk, vap, VB_) in enumerate(blocks):
                                    f.op(PE, lambda e, vap=vap, nk=nk, bi=bi: e.matmul(
                                        nb_[:, j * 128:(j + 1) * 128], lhsT=vap, rhs=p_[0:nk, bi, :], start=(bi == 0), stop=(bi == 1)),
                                        reads=[P_, VB_], writes=[NB_])
                                for bi, (k0, nk, mk, vap, VB_) in enumerate(blocks):
                                    f.op(PE, lambda e, nk=nk, bi=bi: e.matmul(
                                        db_[:, j * 128:(j + 1) * 128], lhsT=onesb[0:nk, :], rhs=p_[0:nk, bi, :], start=(bi == 0), stop=(bi == 1)),
                                        reads=[P_, ONESB], writes=[DB_])
                                if j != 3:
                                    return
                                r, qb = units[u0]
                                if g == 0:
                                    sl_ = slice(qb * 128, qb * 128 + 512)
                                    f.op(DVE, lambda e: e.tensor_copy(out=accn[:, sl_], in_=nb_), reads=[NB_], writes=[ACCN])
                                    f.op(DVE, lambda e: e.tensor_copy(out=accd[:, sl_], in_=db_), reads=[DB_], writes=[ACCD])
                                else:
                                    if g == 1:
                                        av = lambda a: a[:, r:r + 4 * 511 + 1:4]
                                        iv = lambda b: b
                                    else:
                                        av = lambda a: a.rearrange("p (i r) -> p r i", r=16)[:, r:r + 4, :]
                                        iv = lambda b: b.rearrange("p (r i) -> p r i", r=4)
                                    f.op(DVE, lambda e: e.tensor_tensor(out=av(accn), in0=iv(nb_), in1=av(accn), op=ALU.add), reads=[NB_, ACCN], writes=[ACCN])
                                    f.op(DVE, lambda e: e.tensor_tensor(out=av(accd), in0=iv(db_), in1=av(accd), op=ALU.add), reads=[DB_, ACCD], writes=[ACCD])
                            if pendB:
                                pendB.pop()()
                            pendB.append(stB)
                if pendB:
                    pendB.pop()()
                m_, M_ = mo[sl]
                f.op(DVE, lambda e: e.reciprocal(out=accd, in_=accd), reads=[ACCD], writes=[ACCD])
                f.op(DVE, lambda e, m_=m_: e.tensor_tensor(out=m_, in0=accn, in1=accd, op=ALU.mult), reads=[ACCN, ACCD], writes=[M_])
                f.dma(POOL, mixT_d.ap()[(8 + h) * 128:(9 + h) * 128, c0:c0 + S], m_, reads=[M_], writes=[MIXT])

    f.barrier()
    ar.reset(m1)
    wo, WO = ar.alloc("wo", [16, D], BF16)
    for q in range(4):
        f.dma(SP, wo[:, q * 4:(q + 1) * 4, :], woutb_d.ap()[q * 512:(q + 1) * 512, :].rearrange("(k p) n -> p k n", p=128), reads=[WOUTB], writes=[WO])
    mx = [ar.alloc("mx%d" % i, [16, 512], BF16) for i in range(2)]
    xr = [ar.alloc("xr%d" % i, [D], F32) for i in range(2)]
    hb = [ar.alloc("hb%d" % i, [D], F32) for i in range(2)]
    for tile in range(NT // 512):
        t0 = tile * 512
        m_, M_ = mx[tile % 2]
        f.dma(SP, m_, mixT_d.ap()[:, t0:t0 + 512].rearrange("(k p) t -> p k t", p=128), reads=[MIXT], writes=[M_])
        for b in range(4):
            r0 = t0 + b * 128
            x_, X_ = xr[b % 2]
            h_, H_ = hb[b % 2]
            f.dma(SP, x_, x_d.ap()[r0:r0 + 128, :], reads=[Xd], writes=[X_])
            for ds in range(4):
                bk, BK = next_bank(2, 8)
                for kc in range(16):
                    f.op(PE, lambda e, bk=bk, m_=m_, kc=kc, b=b, ds=ds: e.matmul(
                        bk, lhsT=m_[:, kc, b * 128:(b + 1) * 128], rhs=wo[:, kc, ds * 512:(ds + 1) * 512], start=(kc == 0), stop=(kc == 15)),
                        reads=[M_, WO], writes=[BK])
                f.op(DVE, lambda e, bk=bk, h_=h_, x_=x_, ds=ds: e.tensor_tensor(
                    out=h_[:, ds * 512:(ds + 1) * 512], in0=bk, in1=x_[:, ds * 512:(ds + 1) * 512], op=ALU.add),
                    reads=[BK, X_], writes=[H_])
            f.dma(POOL, h1_d.ap()[r0:r0 + 128, :], h_, reads=[H_], writes=[H1])

    if upto <= 3:
        f.barrier(); f.emit(); f.close()
        return nc
    f.barrier()
    ar.reset(m1)
    TT4 = 512
    norm4 = make_norm("n4", 2)
    n2T = ar.alloc("n2T", [16, TT4 + 2], BF16, buf=False)
    N2B = [f.buf(n2T[:, kc, :], "n2T_%d" % kc) for kc in range(16)]
    actT = ar.alloc("actT", [NFC, TT4], BF16, buf=False)
    ACTB = [f.buf(actT[:, c, :], "actT_%d" % c) for c in range(NFC)]
    wu = [ar.alloc("wu%d" % i, [16, 2, 256], BF16) for i in range(2)]
    wd = [ar.alloc("wd%d" % i, [11, 512], BF16) for i in range(2)]
    ug = [ar.alloc("ug%d" % i, [TT4 + 2], F32) for i in range(4)]
    ag = [ar.alloc("ag%d" % i, [TT4], F32) for i in range(4)]
    hq = [ar.alloc("hq%d" % i, [512], F32) for i in range(2)]
    oq = [ar.alloc("oq%d" % i, [512], F32) for i in range(2)]
    halo_rr = [0]
    wu_i = 0
    wd_i = 0
    oq_i = 0
    for tile in range(NT // TT4):
        t0 = tile * TT4
        tl = t0 % S
        blocks = [(h1_d.ap()[t0 + b * 128:t0 + (b + 1) * 128, :], 128, 1 + b * 128, 0, H1) for b in range(TT4 // 128)]
        left = h1_d.ap()[t0 - 1:t0, :] if tl > 0 else None
        right = h1_d.ap()[t0 + TT4:t0 + TT4 + 1, :] if tl + TT4 < S else None
        blocks.append(([left, right], 2, 0, TT4 + 1, H1))
        norm4(blocks, nfw, NFW, n2T, N2B)
        for slab in range(NFC // 2):
            w_, W_ = wu[wu_i % 2]
            wu_i += 1
            f.dma(SP, w_.rearrange("p k g n -> p (k g n)"), wfit_d.ap()[slab], reads=[WFIT], writes=[W_])
            for pj in range(2):
                fc = slab * 2 + pj
                accs = []
                for gv in range(2):
                    bk, BK = next_bank(2, 6)
                    hk, HK = banks[6 + halo_rr[0] % 2]
                    halo_rr[0] += 1
                    for kc in range(16):
                        f.op(PE, lambda e, bk=bk, w_=w_, kc=kc, gv=gv, pj=pj: e.matmul(
                            bk, lhsT=w_[:, kc, gv, pj * 128:(pj + 1) * 128], rhs=n2T[:, kc, 1:TT4 + 1], start=(kc == 0), stop=(kc == 15)),
                            reads=[W_, N2B[kc]], writes=[BK])
                    for kc in range(16):
                        f.op(PE, lambda e, hk=hk, w_=w_, kc=kc, gv=gv, pj=pj: e.matmul(
                            hk[:, 0:2], lhsT=w_[:, kc, gv, pj * 128:(pj + 1) * 128], rhs=n2T[:, kc, 0:TT4 + 2:TT4 + 1], start=(kc == 0), stop=(kc == 15)),
                            reads=[W_, N2B[kc]], writes=[HK])
                    u_, U_ = ug[(fc * 2 + gv) % 4]
                    a_, A_ = ag[(fc * 2 + gv) % 4]
                    f.op(ACT, lambda e, u_=u_, bk=bk: e.activation(out=u_[:, 1:TT4 + 1], in_=bk, func=AF.Copy), reads=[BK], writes=[U_])
                    f.op(ACT, lambda e, u_=u_, hk=hk: e.activation(out=u_[:, 0:TT4 + 2:TT4 + 1], in_=hk[:, 0:2], func=AF.Copy), reads=[HK], writes=[U_])
                    ch = gv * NFC + fc
                    f.op(DVE, lambda e, u_=u_, a_=a_, ch=ch: e.tensor_scalar(out=a_, in0=u_[:, 1:TT4 + 1], scalar1=cffn[1][0][:, ch:ch + 1], scalar2=None, op0=ALU.mult),
                         reads=[U_, cffn[1][1]], writes=[A_])
                    f.op(DVE, lambda e, u_=u_, a_=a_, ch=ch: e.scalar_tensor_tensor(out=a_, in0=u_[:, 0:TT4], scalar=cffn[0][0][:, ch:ch + 1], in1=a_, op0=ALU.mult, op1=ALU.add),
                         reads=[U_, cffn[0][1], A_], writes=[A_])
                    f.op(DVE, lambda e, u_=u_, a_=a_, ch=ch: e.scalar_tensor_tensor(out=a_, in0=u_[:, 2:TT4 + 2], scalar=cffn[2][0][:, ch:ch + 1], in1=a_, op0=ALU.mult, op1=ALU.add),
                         reads=[U_, cffn[2][1], A_], writes=[A_])
                    accs.append((a_, A_))
                (a_g, A_G), (a_v, A_V) = accs
                f.op(ACT, lambda e, a_g=a_g: e.activation(out=a_g, in_=a_g, func=AF.Silu), reads=[A_G], writes=[A_G])
                f.op(DVE, lambda e, a_g=a_g, a_v=a_v, fc=fc: e.tensor_tensor(out=actT[:, fc, :], in0=a_g, in1=a_v, op=ALU.mult),
                     reads=[A_G, A_V], writes=[ACTB[fc]])
        for ds in range(4):
            obanks = [banks[2 + b] for b in range(4)]
            for q in range(4):
                w_, W_ = wd[wd_i % 2]
                wd_i += 1
                f.dma(SP, w_.rearrange("p k n -> p (k n)"), wfot_d.ap()[ds * 4 + q], reads=[WFOT], writes=[W_])
                for b in range(4):
                    bk, BK = obanks[b]
                    for i in range(11):
                        fc = q * 11 + i
                        f.op(PE, lambda e, bk=bk, w_=w_, i=i, fc=fc, b=b, q=q: e.matmul(
                            bk, lhsT=actT[:, fc, b * 128:(b + 1) * 128], rhs=w_[:, i, :], start=(q == 0 and i == 0), stop=(q == 3 and i == 10)),
                            reads=[W_, ACTB[fc]], writes=[BK])
            for b in range(4):
                bk, BK = obanks[b]
                r0 = t0 + b * 128
                h_, H_ = hq[oq_i % 2]
                o_, O_ = oq[oq_i % 2]
                oq_i += 1
                f.dma(SP, h_, h1_d.ap()[r0:r0 + 128, ds * 512:(ds + 1) * 512], reads=[H1], writes=[H_])
                f.op(DVE, lambda e, bk=bk, h_=h_, o_=o_: e.tensor_tensor(out=o_, in0=bk, in1=h_, op=ALU.add), reads=[BK, H_], writes=[O_])
                f.dma(POOL, out_d.ap()[r0:r0 + 128, ds * 512:(ds + 1) * 512], o_, reads=[O_], writes=[OUT])

    if upto <= 4:
        f.barrier(); f.emit(); f.close()
        return nc
    f.barrier()
    ar.reset(m1)
    fx = [ar.alloc("fx%d" % i, [D], F32) for i in range(3)]
    fy = [ar.alloc("fy%d" % i, [D], F32) for i in range(2)]
    fj, FJ = ar.alloc("fj", [D], BF16)
    fs = [ar.alloc("fs%d" % i, [4], F32) for i in range(2)]
    for b in range(NT // 128):
        r0 = b * 128
        x_, X_ = fx[b % 3]
        y_, Y_ = fy[b % 2]
        s_, S_ = fs[b % 2]
        f.dma(SP, x_, out_d.ap()[r0:r0 + 128, :], reads=[OUT], writes=[X_])
        f.op(ACT, lambda e, x_=x_, s_=s_: e.activation(out=fj, in_=x_, func=AF.Square, accum_out=s_[:, 0:1]), reads=[X_], writes=[FJ, S_])
        f.op(DVE, lambda e, s_=s_: e.tensor_scalar(out=s_[:, 1:2], in0=s_[:, 0:1], scalar1=1.0 / D, scalar2=EPS, op0=ALU.mult, op1=ALU.add), reads=[S_], writes=[S_])
        f.op(ACT, lambda e, s_=s_: e.activation(out=s_[:, 2:3], in_=s_[:, 1:2], func=AF.Sqrt), reads=[S_], writes=[S_])
        f.op(DVE, lambda e, s_=s_: e.reciprocal(out=s_[:, 3:4], in_=s_[:, 2:3]), reads=[S_], writes=[S_])
        f.op(DVE, lambda e, s_=s_, x_=x_, y_=y_: e.scalar_tensor_tensor(out=y_, in0=x_, scalar=s_[:, 3:4], in1=nfin, op0=ALU.mult, op1=ALU.mult),
             reads=[S_, X_, NFIN], writes=[Y_])
        f.dma(SP, out_d.ap()[r0:r0 + 128, :], y_, reads=[Y_], writes=[OUT])
    f.barrier()
    f.emit()
    f.close()
    return nc


_NC_CACHE = {}


def _get_nc(debug=False, stub_mixers=False):
    key = (debug, stub_mixers)
    if key not in _NC_CACHE:
        _NC_CACHE[key] = build(debug=debug, stub_mixers=stub_mixers)
    return _NC_CACHE[key]


def make_in_maps(inputs, ncores=8):
    g = lambda k: np.ascontiguousarray(np.asarray(inputs[k]))
    x = g("x").astype(np.float32, copy=False)
    pos = g("positions").astype(np.int32, copy=False)
    shared = {
        "norm_mix_w": g("norm_mix_w").reshape(16, 128),
        "w_in": g("w_in").reshape(D, PW),
        "conv_qkv_w": g("conv_qkv_w").reshape(5, 24, 128),
        "a_log_fwd": g("a_log_fwd").reshape(1, 8), "a_log_bwd": g("a_log_bwd").reshape(1, 8),
        "dt_bias_fwd": g("dt_bias_fwd").reshape(1, 8), "dt_bias_bwd": g("dt_bias_bwd").reshape(1, 8),
        "delta_norm_w": g("delta_norm_w").reshape(1, 128),
        "w_out": g("w_out").reshape(D, D),
        "norm_ffn_w": g("norm_ffn_w").reshape(16, 128),
        "w_ffn_in": g("w_ffn_in").reshape(D, 2 * DFF),
        "conv_ffn_w": g("conv_ffn_w").reshape(3, 88, 128),
        "w_ffn_out": g("w_ffn_out").reshape(DFF, D),
        "norm_final_w": g("norm_final_w").reshape(1, D),
        "cst": make_cst(),
    }
    shared = {k: np.ascontiguousarray(v, dtype=np.float32) for k, v in shared.items()}
    maps = []
    for c in range(ncores):
        m = dict(shared)
        m["x"] = np.ascontiguousarray(x[c * NS:(c + 1) * NS].reshape(NT, D))
        m["pos"] = np.ascontiguousarray(pos[c * NS:(c + 1) * NS])
        maps.append(m)
    return maps


def kernel(**inputs):
    nc = _get_nc()
    maps = make_in_maps(inputs)
    res = run_bass_kernel_spmd(nc, maps, core_ids=list(range(8)))
    out = np.concatenate([np.asarray(r["out"]).reshape(NS, S, D) for r in res.results], axis=0)
    return out.astype(np.float32, copy=False)
```
